# Optimizing a Trainium2 kernel written in Bass

```python
import jax, jax.numpy as jnp
from jax import lax
import numpy as np

D_MODEL = 1024
BATCH = 16
SEQ = 2048
DEPTH = 4
DEC_BATCH = 8
DEC_SEQ = 64
PAST_LEN = 2048

CHUNK = 64
QBLOCK = 128
D_PLE = 256
A_HEADS = 8
A_HEAD_DIM = 64
A_WIDTH = A_HEADS * A_HEAD_DIM
A_LORA_W = 64
A_LORA_A = 64
A_SHIFT = 3 * A_WIDTH + A_LORA_W + A_LORA_A
B_HEADS = 4
B_HEAD_DIM = 128
B_WIDTH = B_HEADS * B_HEAD_DIM
B_CONV = 4
C_HEADS = 8
C_HEAD_DIM = 64
C_WIDTH = C_HEADS * C_HEAD_DIM
N_BRANCH = 3
BRANCH_WIDTH = 512
IN_SPLITS = (A_SHIFT, A_WIDTH,
             2 * B_WIDTH, B_WIDTH, B_HEADS, B_HEADS, B_WIDTH, B_WIDTH,
             C_WIDTH, C_WIDTH, C_WIDTH, C_WIDTH,
             N_BRANCH * D_MODEL)
N_IN = A_SHIFT + A_WIDTH + 5 * B_WIDTH + 2 * B_HEADS + 4 * C_WIDTH + N_BRANCH * D_MODEL
F_COL = A_SHIFT + A_WIDTH + 3 * B_WIDTH + B_HEADS
DEEPNORM_ALPHA = (2 * DEPTH) ** 0.25
DEEPNORM_BETA = (8 * DEPTH) ** -0.25
LN_EPS = 1e-5
GN_EPS_A = 64e-5
HN_EPS = 1e-6

kernel_name = 'rwkv7_mlstm_stickbreak_gated_stream_step'


def layer_norm(x, g, b):
    xf = x.astype(jnp.float32)
    mu = jnp.mean(xf, -1, keepdims=True)
    var = jnp.mean(jnp.square(xf - mu), -1, keepdims=True)
    return ((xf - mu) * lax.rsqrt(var + LN_EPS) * g + b).astype(x.dtype)


def head_norm(x, g, b, eps):
    xf = x.astype(jnp.float32)
    mu = jnp.mean(xf, -1, keepdims=True)
    var = jnp.mean(jnp.square(xf - mu), -1, keepdims=True)
    y = ((xf - mu) * lax.rsqrt(var + eps)).reshape(x.shape[0], x.shape[1], -1) * g
    return y if b is None else y + b


def token_shift(u, last, mu):
    prev = jnp.concatenate([last[:, None].astype(u.dtype), u[:, :-1]], axis=1)
    return u + (prev - u) * mu, u[:, -1]


def causal_conv(u, buf, w, b):
    T = u.shape[1]
    ext = jnp.concatenate([buf.astype(u.dtype), u], axis=1)
    out = b + sum(ext[:, j:j + T] * w[j] for j in range(B_CONV))
    return out, ext[:, T:]


def rwkv7_branch(cols, last, S0, mu, w0, w_up, a0, a_up, k_k, k_a, r_k, gn_g, gn_b):
    Bn, T, _ = cols.shape
    f32 = jnp.float32
    xs, new_last = token_shift(cols, last, mu)
    r, k, v, wd, ad = jnp.split(xs, [A_WIDTH, 2 * A_WIDTH, 3 * A_WIDTH, 3 * A_WIDTH + A_LORA_W], axis=-1)
    w_log = -jax.nn.softplus(-(w0 + jnp.tanh(wd) @ w_up).astype(f32)) - 0.5
    decay = jnp.exp(-jnp.exp(w_log))
    a = jax.nn.sigmoid((a0 + ad @ a_up).astype(f32))
    heads = lambda t: t.astype(f32).reshape(Bn, T, A_HEADS, A_HEAD_DIM)
    kk = heads(k * k_k)
    kk = kk / jnp.maximum(jnp.sqrt(jnp.sum(kk * kk, -1, keepdims=True)), 1e-12)
    kh = heads(k * (1.0 + (a - 1.0) * k_a))
    rh, vh, wh, ah = heads(r), heads(v), heads(decay), heads(a)

    def step(S, inp):
        r_t, w_t, k_t, v_t, kk_t, a_t = inp
        sa = jnp.einsum('bhvk,bhk->bhv', S, -kk_t)
        S = (S * w_t[:, :, None, :] + sa[..., None] * (kk_t * a_t)[:, :, None, :]
             + v_t[..., None] * k_t[:, :, None, :])
        return S, jnp.einsum('bhvk,bhk->bhv', S, r_t)

    tmaj = lambda t: jnp.moveaxis(t, 1, 0)
    S_T, y = lax.scan(step, S0.astype(f32), tuple(tmaj(t) for t in (rh, wh, kh, vh, kk, ah)))
    y = jnp.moveaxis(y, 0, 1)
    bonus = jnp.sum(rh * kh * r_k, -1, keepdims=True) * vh
    out = head_norm(y, gn_g, gn_b, GN_EPS_A) + bonus.reshape(Bn, T, A_WIDTH)
    return out.astype(cols.dtype), new_last, S_T.astype(S0.dtype)


def mlstm_branch(qk_raw, v, i_pre, f_pre, o_pre, buf, C0, n0, m0, conv_w, conv_b, hn_g):
    Bn, T, _ = v.shape
    f32 = jnp.float32
    qk, new_buf = causal_conv(qk_raw, buf, conv_w, conv_b)
    q, k = jnp.split(jax.nn.silu(qk), 2, axis=-1)
    L = min(CHUNK, T)
    NC = T // L

    def chunks(t):
        return t.astype(f32).reshape(Bn, NC, L, B_HEADS, B_HEAD_DIM).transpose(1, 0, 3, 2, 4)

    def gchunks(t):
        return t.astype(f32).reshape(Bn, NC, L, B_HEADS).transpose(1, 0, 3, 2)

    causal = jnp.tril(jnp.ones((L, L), dtype=bool))

    def step(carry, inp):
        C, n, m = carry
        q_c, k_c, v_c, i_c, lf_c = inp
        b = jnp.cumsum(lf_c, axis=-1)
        D = jnp.where(causal, b[..., :, None] - b[..., None, :] + i_c[..., None, :], -jnp.inf)
        inter = b + m[..., None]
        m_t = jnp.maximum(inter, jnp.max(D, axis=-1))
        W = jnp.exp(D - m_t[..., None]) * jnp.einsum('bhtd,bhsd->bhts', q_c, k_c)
        scale = jnp.exp(inter - m_t)
        num = jnp.einsum('bhts,bhsd->bhtd', W, v_c) + scale[..., None] * jnp.einsum('bhvk,bhtk->bhtv', C, q_c)
        den = jnp.sum(W, -1) + scale * jnp.einsum('bhk,bhtk->bht', n, q_c)
        h = num / jnp.maximum(jnp.abs(den), jnp.exp(-m_t))[..., None]
        m_new = m_t[..., -1]
        g = jnp.exp(D[..., -1, :] - m_new[..., None])
        c_scale = jnp.exp(inter[..., -1] - m_new)
        C = c_scale[..., None, None] * C + jnp.einsum('bhs,bhsv,bhsk->bhvk', g, v_c, k_c)
        n = c_scale[..., None] * n + jnp.einsum('bhs,bhsk->bhk', g, k_c)
        return (C, n, m_new), h

    xs = (chunks(q), chunks(k) * B_HEAD_DIM ** -0.5, chunks(v), gchunks(i_pre),
          jax.nn.log_sigmoid(gchunks(f_pre)))
    (C_T, n_T, m_T), h = lax.scan(step, (C0.astype(f32), n0.astype(f32), m0.astype(f32)), xs)
    h = h.transpose(1, 0, 3, 2, 4).reshape(Bn, T, B_HEADS, B_HEAD_DIM)
    out = jax.nn.sigmoid(o_pre.astype(f32)) * head_norm(h, hn_g, None, HN_EPS)
    return (out.astype(v.dtype), new_buf, C_T.astype(C0.dtype), n_T.astype(n0.dtype), m_T.astype(m0.dtype))


def stick_breaking_block(q_blk, q_pos, k, v, k_pos):
    z = jnp.einsum('bqhd,bshd->bhqs', q_blk, k).astype(jnp.float32) * C_HEAD_DIM ** -0.5
    mask = k_pos[None, :] < q_pos[:, None]
    log_keep = jnp.where(mask, jax.nn.log_sigmoid(-z), 0.0)
    between = lax.cumsum(log_keep, axis=3, reverse=True) - log_keep
    A = jnp.where(mask, jnp.exp(jax.nn.log_sigmoid(z) + between), 0.0)
    return jnp.einsum('bhqs,bshd->bqhd', A.astype(v.dtype), v)


def stick_breaking(q, k_new, v_new, k_past, v_past):
    Bn, T = q.shape[:2]
    P = k_past.shape[1]
    k = jnp.concatenate([k_past.astype(k_new.dtype), k_new], axis=1)
    v = jnp.concatenate([v_past.astype(v_new.dtype), v_new], axis=1)
    k_pos = jnp.arange(P + T)
    q_pos = P + jnp.arange(T)
    if T > QBLOCK:
        nb = T // QBLOCK
        qb = q.reshape(Bn, nb, QBLOCK, C_HEADS, C_HEAD_DIM).transpose(1, 0, 2, 3, 4)
        out = lax.map(lambda blk: stick_breaking_block(blk[0], blk[1], k, v, k_pos),
                      (qb, q_pos.reshape(nb, QBLOCK)))
        return out.transpose(1, 0, 2, 3, 4).reshape(Bn, T, C_WIDTH)
    return stick_breaking_block(q, q_pos, k, v, k_pos).reshape(Bn, T, C_WIDTH)


def trunk_layer(x, p, st, lw):
    shift0, wkv0, conv0, c0, n0, m0, k_past, v_past = st
    Bn, T, _ = x.shape
    cols = x @ lw['w_in'] + lw['b_in']
    (a_cols, a_z, b_qk, b_v, b_i, b_f, b_o, b_z,
     c_q, c_k, c_v, c_z, gates) = jnp.split(cols, np.cumsum(IN_SPLITS)[:-1], axis=-1)
    ya, shift1, wkv1 = rwkv7_branch(a_cols, shift0, wkv0, lw['mu_a'], lw['w0_a'], lw['w_decay_up'],
                                    lw['a0_a'], lw['w_iclr_up'], lw['k_k'], lw['k_a'], lw['r_k'],
                                    lw['gn_a_g'], lw['gn_a_b'])
    yb, conv1, c1, n1, m1 = mlstm_branch(b_qk, b_v, b_i, b_f, b_o, conv0, c0, n0, m0,
                                         lw['conv_b_w'], lw['conv_b_b'], lw['hn_b_g'])
    heads_c = lambda t: t.reshape(Bn, T, C_HEADS, C_HEAD_DIM)
    kc, vc = heads_c(c_k), heads_c(c_v)
    yc = stick_breaking(heads_c(c_q), kc, vc, k_past, v_past)
    ys = jnp.stack([ya * jax.nn.silu(a_z), yb * jax.nn.silu(b_z), yc * jax.nn.silu(c_z)], axis=2)
    br = jnp.einsum('btnc,ncd->btnd', ys, lw['w_branch'])
    g = jax.nn.sigmoid(gates.reshape(Bn, T, N_BRANCH, D_MODEL))
    mix = jnp.sum(g * br, axis=2) @ lw['w_out']
    x = layer_norm(DEEPNORM_ALPHA * x + mix, lw['ln_g'], lw['ln_b'])
    x = x + (p @ lw['w_ple']) * jax.nn.sigmoid(x @ lw['w_ple_gate'])
    return x, (shift1, wkv1, conv1, c1, n1, m1, kc, vc)


def run_group(x, p, init, ln_in_g, ln_in_b, weights):
    x = layer_norm(x, ln_in_g, ln_in_b)
    new = [[] for _ in init]
    for i in range(DEPTH):
        lw = {name: w[i] for name, w in weights.items()}
        x, st = trunk_layer(x, p[i], tuple(s[i] for s in init), lw)
        for lst, s in zip(new, st):
            lst.append(s)
    return x, [jnp.stack(lst) for lst in new]


def setup_inputs(seed: int = 0) -> dict:
    key = jax.random.key(seed)
    ks = iter(jax.random.split(key, 40))

    def nrm(shape, scale=1.0):
        return jax.random.normal(next(ks), shape, jnp.float32) * scale

    def unif(shape, lo, hi):
        return jax.random.uniform(next(ks), shape, jnp.float32, lo, hi)

    def gain(shape):
        return 1.0 + nrm(shape, 0.01)

    b_in = nrm((DEPTH, N_IN), 0.01).at[:, F_COL:F_COL + B_HEADS].add(jnp.linspace(3.0, 6.0, B_HEADS))
    return {
        'x_prompt': nrm((BATCH, SEQ, D_MODEL)),
        'x_sample': nrm((DEC_BATCH, DEC_SEQ, D_MODEL)),
        'state_shift_a': nrm((DEPTH, DEC_BATCH, A_SHIFT)),
        'state_wkv': nrm((DEPTH, DEC_BATCH, A_HEADS, A_HEAD_DIM, A_HEAD_DIM), 0.5),
        'state_conv_b': nrm((DEPTH, DEC_BATCH, B_CONV - 1, 2 * B_WIDTH)),
        'state_mlstm_c': nrm((DEPTH, DEC_BATCH, B_HEADS, B_HEAD_DIM, B_HEAD_DIM), 0.1),
        'state_mlstm_n': nrm((DEPTH, DEC_BATCH, B_HEADS, B_HEAD_DIM), 0.1),
        'state_mlstm_m': nrm((DEPTH, DEC_BATCH, B_HEADS)),
        'cache_sb_k': nrm((DEPTH, DEC_BATCH, PAST_LEN, C_HEADS, C_HEAD_DIM)),
        'cache_sb_v': nrm((DEPTH, DEC_BATCH, PAST_LEN, C_HEADS, C_HEAD_DIM)),
        'p_prompt': nrm((DEPTH, BATCH, SEQ, D_PLE)),
        'p_sample': nrm((DEPTH, DEC_BATCH, DEC_SEQ, D_PLE)),
        'ln_in_g': gain((D_MODEL,)),
        'ln_in_b': nrm((D_MODEL,), 0.01),
        'w_in': nrm((DEPTH, D_MODEL, N_IN), D_MODEL ** -0.5),
        'b_in': b_in,
        'mu_a': unif((DEPTH, A_SHIFT), 0.0, 1.0),
        'w0_a': unif((DEPTH, A_WIDTH), -6.0, 1.0),
        'w_decay_up': nrm((DEPTH, A_LORA_W, A_WIDTH), 0.5 * A_LORA_W ** -0.5),
        'a0_a': nrm((DEPTH, A_WIDTH), 0.1),
        'w_iclr_up': nrm((DEPTH, A_LORA_A, A_WIDTH), 0.5 * A_LORA_A ** -0.5),
        'k_k': 0.85 + nrm((DEPTH, A_WIDTH), 0.02),
        'k_a': gain((DEPTH, A_WIDTH)),
        'r_k': nrm((DEPTH, A_HEADS, A_HEAD_DIM), 0.1),
        'gn_a_g': gain((DEPTH, A_WIDTH)),
        'gn_a_b': nrm((DEPTH, A_WIDTH), 0.01),
        'conv_b_w': nrm((DEPTH, B_CONV, 2 * B_WIDTH), 0.5),
        'conv_b_b': nrm((DEPTH, 2 * B_WIDTH), 0.01),
        'hn_b_g': gain((DEPTH, B_WIDTH)),
        'w_branch': nrm((DEPTH, N_BRANCH, BRANCH_WIDTH, D_MODEL), DEEPNORM_BETA * BRANCH_WIDTH ** -0.5),
        'w_out': nrm((DEPTH, D_MODEL, D_MODEL), DEEPNORM_BETA * D_MODEL ** -0.5),
        'ln_g': gain((DEPTH, D_MODEL)),
        'ln_b': nrm((DEPTH, D_MODEL), 0.01),
        'w_ple': nrm((DEPTH, D_PLE, D_MODEL), D_PLE ** -0.5),
        'w_ple_gate': nrm((DEPTH, D_MODEL, D_MODEL), D_MODEL ** -0.5),
    }


def reference(x_prompt, x_sample, state_shift_a, state_wkv, state_conv_b, state_mlstm_c, state_mlstm_n,
              state_mlstm_m, cache_sb_k, cache_sb_v, p_prompt, p_sample, ln_in_g, ln_in_b, w_in, b_in,
              mu_a, w0_a, w_decay_up, a0_a, w_iclr_up, k_k, k_a, r_k, gn_a_g, gn_a_b, conv_b_w, conv_b_b,
              hn_b_g, w_branch, w_out, ln_g, ln_b, w_ple, w_ple_gate):
    weights = dict(w_in=w_in, b_in=b_in, mu_a=mu_a, w0_a=w0_a, w_decay_up=w_decay_up, a0_a=a0_a,
                   w_iclr_up=w_iclr_up, k_k=k_k, k_a=k_a, r_k=r_k, gn_a_g=gn_a_g, gn_a_b=gn_a_b,
                   conv_b_w=conv_b_w, conv_b_b=conv_b_b, hn_b_g=hn_b_g, w_branch=w_branch, w_out=w_out,
                   ln_g=ln_g, ln_b=ln_b, w_ple=w_ple, w_ple_gate=w_ple_gate)
    Bp = x_prompt.shape[0]
    dt = x_prompt.dtype
    init_prompt = (jnp.zeros((DEPTH, Bp, A_SHIFT), dt),
                   jnp.zeros((DEPTH, Bp, A_HEADS, A_HEAD_DIM, A_HEAD_DIM), dt),
                   jnp.zeros((DEPTH, Bp, B_CONV - 1, 2 * B_WIDTH), dt),
                   jnp.zeros((DEPTH, Bp, B_HEADS, B_HEAD_DIM, B_HEAD_DIM), dt),
                   jnp.zeros((DEPTH, Bp, B_HEADS, B_HEAD_DIM), dt),
                   jnp.zeros((DEPTH, Bp, B_HEADS), dt),
                   jnp.zeros((DEPTH, Bp, 0, C_HEADS, C_HEAD_DIM), dt),
                   jnp.zeros((DEPTH, Bp, 0, C_HEADS, C_HEAD_DIM), dt))
    init_sample = (state_shift_a, state_wkv, state_conv_b, state_mlstm_c, state_mlstm_n, state_mlstm_m,
                   cache_sb_k, cache_sb_v)
    y_prompt, sp = run_group(x_prompt, p_prompt, init_prompt, ln_in_g, ln_in_b, weights)
    y_sample, ss = run_group(x_sample, p_sample, init_sample, ln_in_g, ln_in_b, weights)
    return (y_prompt, y_sample, sp[0], ss[0], sp[1], ss[1], sp[2], ss[2], sp[3], ss[3],
            sp[4], ss[4], sp[5], ss[5], sp[6], ss[6], sp[7], ss[7])
```

```python
import contextlib
import math
import numpy as np
import concourse.bass as bass
import concourse.mybir as mybir
from concourse.bass_utils import run_bass_kernel_spmd

F32 = mybir.dt.float32
BF16 = mybir.dt.bfloat16
AF = mybir.ActivationFunctionType
ALU = mybir.AluOpType

ENG = ('pe', 'act', 'dve', 'pool', 'sp')
ROT = 16000
NDSEM = 12


class Sched:
    def __init__(self):
        self.q = {e: [] for e in ENG}
        self.cnt = {e: 0 for e in ENG}
        self.clock = {e: {} for e in ENG}
        self.lastw = {}
        self.readers = {}
        self.dcount = {'sp': 0, 'pool': 0, 'act': 0}
        self.dlast = {}
        self.out_dmas = []
        self.enabled = True
        self.budget = None

    def _collect(self, eng, reads, writes, is_dma):
        deps = []
        for k in reads:
            w = self.lastw.get(k)
            if w is not None:
                deps.append(w)
        for k in writes:
            w = self.lastw.get(k)
            if w is not None:
                deps.append(w)
            for r in self.readers.get(k, ()):
                deps.append(r)
        ck = self.clock[eng]
        best = {}
        for d in deps:
            src = (d[0], d[1])
            if ck.get(src, 0) >= d[2]:
                continue
            if src not in best or best[src][2] < d[2]:
                best[src] = d
        for src, d in best.items():
            ck[src] = d[2]
        return list(best.values())

    def op(self, eng, fn, reads=(), writes=()):
        if not self.enabled:
            return None
        if self.budget is not None:
            if self.budget <= 0:
                return None
            self.budget -= 1
        waits = self._collect(eng, reads, writes, False)
        self.cnt[eng] += 1
        n = self.cnt[eng]
        me = ('eng', eng, n)
        self.q[eng].append(('op', fn, waits, n))
        for k in reads:
            self.readers.setdefault(k, []).append(me)
        for k in writes:
            self.lastw[k] = me
            self.readers[k] = []
        return me

    def dma(self, eng, fn, reads=(), writes=(), is_out=False):
        if not self.enabled:
            return None
        if self.budget is not None:
            if self.budget <= 0:
                return None
            self.budget -= 1
        waits = self._collect(eng, reads, writes, True)
        i = self.dcount[eng]
        self.dcount[eng] += 1
        slot = (eng, i % NDSEM)
        prev = self.dlast.get(slot, 0)
        val = prev + 16
        ck = self.clock[eng]
        src = ('dma', slot)
        if prev > 0 and ck.get(src, 0) < prev:
            ck[src] = prev
            waits = [w for w in waits if (w[0], w[1]) != src] + [('dma', slot, prev)]
        self.dlast[slot] = val
        me = ('dma', slot, val)
        self.q[eng].append(('dma', fn, waits, slot))
        for k in reads:
            self.readers.setdefault(k, []).append(me)
        for k in writes:
            self.lastw[k] = me
            self.readers[k] = []
        if is_out:
            self.out_dmas.append(me)
        return me

    def emit(self, nc):
        with contextlib.ExitStack() as st:
            esem = {}
            for e in ENG:
                for r in range(self.cnt[e] // ROT + 1):
                    esem[(e, r)] = st.enter_context(nc.semaphore(f"s_{e}_{r}"))
            dsem = {}
            for slot in self.dlast:
                dsem[slot] = st.enter_context(nc.semaphore(f"d_{slot[0]}_{slot[1]}"))
            block = st.enter_context(nc.Block())

            def do_wait(engine, d):
                if d[0] == 'eng':
                    n = d[2]
                    engine.wait_ge(esem[(d[1], (n - 1) // ROT)], (n - 1) % ROT + 1)
                else:
                    engine.wait_ge(dsem[d[1]], d[2])

            def run(e, engine):
                for item in self.q[e]:
                    kind, fn, waits = item[0], item[1], item[2]
                    for d in waits:
                        do_wait(engine, d)
                    inst = fn(engine)
                    if kind == 'op':
                        n = item[3]
                        inst.then_inc(esem[(e, (n - 1) // ROT)], 1)
                    else:
                        inst.then_inc(dsem[item[3]], 16)
                if e == 'sp':
                    for slot, v in self.dlast.items():
                        engine.wait_ge(dsem[slot], v)

            @block.tensor
            def _(eng):
                run('pe', eng)

            @block.scalar
            def _(eng):
                run('act', eng)

            @block.vector
            def _(eng):
                run('dve', eng)

            @block.gpsimd
            def _(eng):
                run('pool', eng)

            @block.sync
            def _(eng):
                run('sp', eng)


D = 1024
KC = 8
NIN = 9864
DPLE = 256
C_A0, C_AZ, C_BQ, C_BK, C_BV, C_BI, C_BF, C_BO, C_BZ = 0, 1664, 2176, 2688, 3200, 3712, 3716, 3720, 4232
C_CQ, C_CK, C_CV, C_CZ, C_G0 = 4744, 5256, 5768, 6280, 6792
DECAY_C = math.exp(-0.5)
LN_C = -0.5 * math.log(128.0)

CS_ID = 0
CS_SU = 128
CS_SL = 256
CS_IU = 384
CS_BO = 512
CS_TRI = 640
CS_ONE = 768
CS_M01 = 896
CS_SEL = 1408
NCST = 1920


def make_consts():
    c = np.zeros((128, NCST), np.float32)
    i = np.arange(128)
    r, cc = i[:, None], i[None, :]
    same = (r // 64) == (cc // 64)
    c[:, CS_ID:CS_ID + 128] = (r == cc)
    c[:, CS_SU:CS_SU + 128] = same & (r < cc)
    c[:, CS_SL:CS_SL + 128] = same & (r > cc)
    c[:, CS_IU:CS_IU + 128] = same & (r <= cc)
    c[:, CS_BO:CS_BO + 128] = same
    c[:, CS_TRI:CS_TRI + 128] = (r >= cc)
    c[:, CS_ONE:CS_ONE + 128] = 1.0
    t = np.arange(512)
    c[:, CS_M01:CS_M01 + 512] = (t % 64 != 0)[None, :]
    for h in range(4):
        c[32 * h, CS_SEL + 128 * h:CS_SEL + 128 * (h + 1)] = 1.0
    return c


def make_att():
    a = np.zeros((128, 2048), np.float32)
    r = np.arange(128)[:, None]
    t = np.arange(512)
    for j in range(4):
        a[:, 512 * j:512 * (j + 1)] = (t[None, :] - r) > 128 * j
    return a


PV_BA, PV_BQK, PV_BCQ, PV_NBCQ, PV_BCK, PV_BCZ, PV_BG = 0, 17, 25, 29, 33, 37, 41
PV_BI, PV_NBF, PV_MU, PV_OMU, PV_W0, PV_A0, PV_KK, PV_KA, PV_RK = 65, 66, 67, 80, 93, 97, 101, 105, 109
PV_CW, PV_CB, PV_LNG, PV_LNB = 113, 145, 153, 161
NPV = 170
X_B, X_MU, X_OMU, X_BZ, X_W0, X_A0, X_KK, X_KA, X_RK = 0, 26, 52, 78, 86, 94, 102, 110, 118
NX = 126


def build(cfg):
    L = cfg['L']
    seqs = cfg['seqs']
    nc = bass.Bass("TRN2", target_bir_lowering=False)

    def din(name, shape, dt=F32):
        return nc.dram_tensor(name, list(shape), dt, kind="ExternalInput").ap()

    def dout(name, shape):
        return nc.dram_tensor(name, list(shape), F32, kind="ExternalOutput").ap()

    def dint(name, shape, dt):
        return nc.dram_tensor(name, list(shape), dt, kind="Internal").ap()

    W = {}
    for nm, shp in [('ln_in_g', [D]), ('ln_in_b', [D]), ('w_in', [L, D, NIN]), ('b_in', [L, NIN]), ('mu_a', [L, 1664]),
                    ('w0_a', [L, 512]), ('w_decay_up', [L, 64, 512]), ('a0_a', [L, 512]), ('w_iclr_up', [L, 64, 512]),
                    ('k_k', [L, 512]), ('k_a', [L, 512]), ('r_k', [L, 512]), ('gn_a_g', [L, 512]), ('gn_a_b', [L, 512]),
                    ('conv_b_w', [L, 4, D]), ('conv_b_b', [L, D]), ('hn_b_g', [L, 512]), ('w_branch', [L, 3, 512, D]),
                    ('w_out', [L, D, D]), ('ln_g', [L, D]), ('ln_b', [L, D]), ('w_ple', [L, DPLE, D]),
                    ('w_ple_gate', [L, D, D]), ('cst', [128, NCST]), ('catt', [128, 2048]), ('pvh', [128, L, NPV]), ('pvg', [128, 16]),
                    ('gnh', [L, 64, 16, 64]), ('hnh', [L, 128, 512]), ('cst3', [64, 512]), ('pvx', [64, L, NX])]:
        W[nm] = din(nm, shp)
    SI, SO, SX = [], [], []
    for i, sq in enumerate(seqs):
        T, P = sq['T'], sq['P']
        d = {'xT': din(f"xT{i}", [D, T]), 'pT': din(f"pT{i}", [L, DPLE, T])}
        if P > 0:
            d['shift'] = din(f"shift{i}", [L, 64, 26])
            d['wkv'] = din(f"wkv{i}", [L, 4, 64, 2, 64])
            d['conv'] = din(f"conv{i}", [L, 128, 8, 3])
            d['ct'] = din(f"ct{i}", [L, 4, 128, 129])
            d['m'] = din(f"m{i}", [L, 128, 1])
            d['ck'] = din(f"ck{i}", [L, 4, 128, P])
            d['cv'] = din(f"cv{i}", [L, P, 512])
        SI.append(d)
        SO.append({'yT': dout(f"yT{i}", [D, T]), 'shift': dout(f"shift_o{i}", [L, 64, 26]),
                   'wkv': dout(f"wkv_o{i}", [L, 4, 64, 2, 64]), 'conv': dout(f"conv_o{i}", [L, 128, 8, 3]),
                   'ct': dout(f"ct_o{i}", [L, 4, 128, 129]), 'm': dout(f"m_o{i}", [L, 128, 1]),
                   'sbk': dout(f"sbk{i}", [L, T, 512]), 'sbv': dout(f"sbv{i}", [L, T, 512])})
        SX.append({'xres': dint(f"xres{i}", [KC, 128, T], F32), 'kTd': dint(f"kTd{i}", [4, 128, P + T], BF16), 'xTd': dint(f"xTd{i}", [KC, 128, T], BF16),
                   'vd': dint(f"vd{i}", [P + T, 512], BF16)})
    wb_in = dint("wb_in", [L, D, NIN], BF16)
    wb_br = dint("wb_br", [L, 3, 512, D], BF16)
    wb_out = dint("wb_out", [L, D, D], BF16)
    wb_pg = dint("wb_pg", [L, D, D], BF16)
    wb_ple = dint("wb_ple", [L, DPLE, D], BF16)

    S = Sched()
    st = contextlib.ExitStack()
    with st:
        def sb(name, shape, dt=F32):
            return st.enter_context(nc.sbuf_tensor(name, list(shape), dt))

        cstF = sb("cstF", [128, 1280])
        cstB = sb("cstB", [128, CS_M01], BF16)
        attB = sb("attB", [128, 4, 512], BF16)
        PV = sb("PV", [128, L, NPV])
        PVG = sb("PVG", [128, 16])
        slabs = [sb(f"slab{i}", [128, 4096], BF16) for i in range(4)]
        xTb = [sb(f"xTb{i}", [128, KC, 512], BF16) for i in range(2)]
        slabZ = sb("slabZ", [128, KC, 128], BF16)
        tmpF = [sb(f"tmpF{i}", [128, 512]) for i in range(9)]
        lnm = sb("lnm", [128, 512])
        lnr = sb("lnr", [128, 512])
        tmpH = [sb(f"tmpH{i}", [128, 512], BF16) for i in range(28)]
        ysT = sb("ysT", [128, 12, 512], BF16)
        p16 = sb("p16", [128, 2, 512], BF16)
        pstg = sb("pstg", [128, 2, 512])
        kTblk = [sb(f"kTblk{i}", [128, 4, 128], BF16) for i in range(2)]
        vblk = [sb(f"vblk{i}", [128, 512], BF16) for i in range(2)]
        brow = sb("brow", [1, 2560], BF16)
        hnG = sb("hnG", [128, 512])
        gnx = sb("gnx", [64, 16, 64])
        upw = sb("upw", [64, 2, 512], BF16)
        wrep = sb("wrep", [128, 2, KC, 128], BF16)
        wcol = sb("wcol", [128, KC, 8], BF16)
        PVX = sb("PVX", [64, L, NX])
        cm2 = sb("cm2", [64, 4, 128], BF16)
        rawA = [sb(f"rawA{i}", [64, 513]) for i in range(2)]
        carX = sb("carX", [64, 26])
        lora = sb("lora", [64, 2, 512], BF16)
        sqh = sb("sqh", [64, 512], BF16)
        egP = [sb(f"egP{i}", [64, 2, 512]) for i in range(2)]
        Hf = [sb(f"Hf{i}", [64, 2, 64]) for i in range(4)]
        Hb = [sb(f"Hb{i}", [64, 2, 64], BF16) for i in range(4)]
        RS = []
        for p in range(2):
            d = {}
            for nm in ('Qa', 'Qb', 'Pa', 'Pb', 'Za', 'Zb', 'Abk', 'Ara', 'Ark', 'ast', 'kst', 'Vst', 'Xb', 'Ub', 'ysa'):
                d[nm] = sb(f"r{nm}{p}", [64, 2, 64], BF16)
            for nm in ('azst', 'e1', 'e2', 'e3', 'htmp'):
                d[nm] = sb(f"r{nm}{p}", [64, 2, 64])
            d['sc'] = sb(f"rsc{p}", [64, 8, 2])
            RS.append(d)
        rawB = [sb(f"rawB{i}", [128, 515]) for i in range(2)]
        carB = sb("carB", [128, 8, 3])
        mbuf = sb("mbuf", [128, 576])
        carM = sb("carM", [128, 2])
        vaug = sb("vaug", [128, 4, 4, 129], BF16)
        tokS = sb("tokS", [128, 4, 5, 4])
        tokS2 = sb("tokS2", [128, 4, 2, 4])
        rmask = sb("rmask", [128, 2])
        csB = sb("csB", [128, 4, 8])
        CTf = [sb(f"CTf{i}", [128, 129]) for i in range(4)]
        CTb = [sb(f"CTb{i}", [128, 129], BF16) for i in range(4)]
        mlt = [sb(f"mlt{i}", [128, 132]) for i in range(4)]
        mlh = [sb(f"mlh{i}", [128, 128], BF16) for i in range(2)]
        mlh2 = [sb(f"mlh2{i}", [128, 512], BF16) for i in range(2)]
        mls = sb("mls", [128, 8, 8])
        psb = [st.enter_context(nc.psum_tensor(f"psb{i}", [128, 512], F32)) for i in range(8)]
        psbH = [psb[i][:, :].bitcast(BF16) for i in range(8)]

        idF = cstF[:, 0:128]
        onesF = cstF[:, 128:256]
        m01 = cstF[:, 256:768]
        idB = cstB[:, CS_ID:CS_ID + 128]
        mIU = cstB[:, CS_IU:CS_IU + 128]
        blkones = cstB[:, CS_BO:CS_BO + 128]
        triB = cstB[:, CS_TRI:CS_TRI + 128]
        onesB = cstB[:, CS_ONE:CS_ONE + 128]

        def MM(out, lhsT, rhs, r, w, start=True, stop=True):
            S.op('pe', lambda e: e.matmul(out, lhsT=lhsT, rhs=rhs, start=start, stop=stop), r, w)

        def TR(out, in_, ident, r, w):
            S.op('pe', lambda e: e.transpose(out, in_, ident), r, w)

        def ACT(out, in_, func, r, w, bias=None, scale=None, accum=None):
            kw = {}
            if bias is not None:
                kw['bias'] = bias
            if scale is not None:
                kw['scale'] = scale
            if accum is not None:
                kw['accum_out'] = accum
            S.op('act', lambda e: e.activation(out=out, in_=in_, func=func, **kw), r, w)

        def TT(eng, out, in0, in1, op, r, w):
            S.op(eng, lambda e: e.tensor_tensor(out=out, in0=in0, in1=in1, op=op), r, w)

        def TS(eng, out, in0, s1, op0, r, w, s2=None, op1=None, accum=None):
            kw = {}
            if op1 is not None:
                kw['op1'] = op1
            if accum is not None:
                kw['accum_out'] = accum
            S.op(eng, lambda e: e.tensor_scalar(out=out, in0=in0, scalar1=s1, scalar2=s2, op0=op0, **kw), r, w)

        def STT(out, in0, scalar, in1, op0, op1, r, w):
            S.op('dve', lambda e: e.scalar_tensor_tensor(out=out, in0=in0, scalar=scalar, in1=in1, op0=op0, op1=op1), r, w)

        def CP(eng, out, in_, r, w):
            if eng == 'act':
                S.op('act', lambda e: e.activation(out=out, in_=in_, func=AF.Copy), r, w)
            else:
                S.op(eng, lambda e: e.tensor_copy(out=out, in_=in_), r, w)

        def MSET(eng, ap, val, w):
            S.op(eng, lambda e: e.memset(ap, val), (), w)

        def SCAN(out, d0, d1, init, op0, op1, r, w):
            S.op('dve', lambda e: e.tensor_tensor_scan(out=out, data0=d0, data1=d1, initial=init, op0=op0, op1=op1), r, w)

        def RECIP(out, in_, r, w):
            S.op('dve', lambda e: e.reciprocal(out=out, in_=in_), r, w)

        def DMA(eng, out, in_, r, w, is_out=False, slow=False):
            if slow:
                S.dma(eng, lambda e: e.dma_start(out=out, in_=in_, allow_slow_non_contiguous=True), r, w, is_out)
            else:
                S.dma(eng, lambda e: e.dma_start(out=out, in_=in_), r, w, is_out)

        RB = [0, 1, 2, 3, 7]
        ring = {1: 0, 2: 0, 4: 0, 't': 0}

        def psq(n=1):
            c = ring[n]
            ring[n] = c + 1
            b = RB[c % 5]
            if n == 1:
                o = (c // 5) % 4
            elif n == 2:
                o = 2 * ((c // 5) % 2)
            else:
                o = 0
            return psb[b][:, o * 128:(o + n) * 128], [('ps', b)]

        def psbank(b):
            return psb[b][:, :], [('ps', b)]

        def pst():
            c = ring['t']
            ring['t'] = c + 1
            b = RB[(c + 2) % 5]
            o = (c // 5) % 4
            return psbH[b][:, o * 256:o * 256 + 128], [('ps', b)]

        sring = {'i': 0}

        def load_slab(src3, kdim, ncols, wkey):
            i = sring['i']
            sring['i'] = (i + 1) % 4
            view = slabs[i][:, 0:kdim * ncols].rearrange("p (k c) -> p k c", k=kdim)
            DMA('sp', view, src3, wkeys.get(wkey, []), [('slab', i)])
            return view, ('slab', i)

        def win_slab(l, c0, n):
            return load_slab(wb_in[l].rearrange("(k p) c -> p k c", p=128)[:, :, c0:c0 + n], KC, n, ('wb_in', l))

        stg = [slabs[i][:, :].bitcast(F32) for i in range(2)]
        kst_ = [('slab', 0), ('slab', 1)]
        DMA('sp', stg[0][:, 0:NCST], W['cst'], [], [kst_[0]])
        DMA('sp', stg[1][:, 0:2048], W['catt'], [], [kst_[1]])
        CP('dve', cstB[:], stg[0][:, 0:CS_M01], [kst_[0]], ['cstB'])
        CP('dve', cstF[:, 0:128], stg[0][:, CS_ID:CS_ID + 128], [kst_[0]], ['cstF'])
        CP('dve', cstF[:, 128:256], stg[0][:, CS_ONE:CS_ONE + 128], [kst_[0]], ['cstF'])
        CP('dve', cstF[:, 256:768], stg[0][:, CS_M01:CS_M01 + 512], [kst_[0]], ['cstF'])
        CP('dve', cstF[:, 768:1280], stg[0][:, CS_SEL:CS_SEL + 512], [kst_[0]], ['cstF'])
        CP('dve', attB[:], stg[1][:, 0:2048].rearrange("p (j c) -> p j c", j=4), [kst_[1]], ['attB'])
        DMA('sp', lnm[0:64, :], W['cst3'], [], ['lnm'])
        CP('dve', cm2[:], lnm[0:64, :].rearrange("p (j c) -> p j c", j=4), ['lnm'], ['cm2'])
        DMA('sp', PVX[:], W['pvx'], [], ['PVX'])
        for b in range(8):
            MSET('dve', psb[b][:], 0.0, [('ps', b)])
        pc = {'i': 0}
        ceng = ('act', 'pool', 'dve')

        def precast(dst2d, src2d, key):
            rows, cols = src2d.shape
            for r0 in range(0, rows, 128):
                for c0 in range(0, cols, 2048):
                    n = min(2048, cols - c0)
                    i = pc['i']
                    pc['i'] += 1
                    sf, kf_ = stg[i % 2], kst_[i % 2]
                    db, kd_ = slabs[2 + i % 2], ('slab', 2 + i % 2)
                    DMA('sp', sf[:, 0:n], src2d[r0:r0 + 128, c0:c0 + n], [], [kf_])
                    CP(ceng[i % 3], db[:, 0:n], sf[:, 0:n], [kf_], [kd_])
                    DMA('sp', dst2d[r0:r0 + 128, c0:c0 + n], db[:, 0:n], [kd_], [(key, pc['i'])])
                    wkeys.setdefault(key, []).append((key, pc['i']))

        wkeys = {}
        for l in range(L):
            precast(wb_in[l], W['w_in'][l], ('wb_in', l))
            for n in range(3):
                precast(wb_br[l, n], W['w_branch'][l, n], ('wb_br', l))
            precast(wb_out[l], W['w_out'][l], ('wb_out', l))
            precast(wb_pg[l], W['w_ple_gate'][l], ('wb_pg', l))
            precast(wb_ple[l], W['w_ple'][l], ('wb_ple', l))
        DMA('sp', PV[:], W['pvh'], [], ['PV'])
        DMA('sp', PVG[:], W['pvg'], [], ['PV'])
        for l in range(L):
            TS('dve', PV[:, l, PV_OMU:PV_OMU + 13], PV[:, l, PV_MU:PV_MU + 13], -1.0, ALU.mult, ['PV'], ['PV'], s2=1.0, op1=ALU.add)
            TS('dve', PV[:, l, PV_NBF:PV_NBF + 1], PV[:, l, PV_NBF:PV_NBF + 1], -1.0, ALU.mult, ['PV'], ['PV'])
            TS('dve', PV[:, l, PV_NBCQ:PV_NBCQ + 4], PV[:, l, PV_BCQ:PV_BCQ + 4], -0.125, ALU.mult, ['PV'], ['PV'])
            TS('dve', PV[:, l, PV_BCQ:PV_BCQ + 4], PV[:, l, PV_BCQ:PV_BCQ + 4], 0.125, ALU.mult, ['PV'], ['PV'])
        MSET('dve', vaug[:], 1.0, ['vaug'])
        MSET('dve', rmask[:], 0.0, ['rmask'])
        MSET('dve', rmask[0:64, 0:1], 1.0, ['rmask'])
        MSET('dve', rmask[64:128, 1:2], 1.0, ['rmask'])
        for l in range(L):
            TS('dve', PVX[:, l, X_OMU:X_OMU + 26], PVX[:, l, X_MU:X_MU + 26], -1.0, ALU.mult, ['PVX'], ['PVX'], s2=1.0, op1=ALU.add)

        def layernorm(TB, gcol, bcol, eps):
            s5, k5 = psbank(4)
            s6, k6 = psbank(5)
            for fc in range(KC):
                ACT(lnr[:, :TB], tmpF[fc][:, :TB], AF.Square, [('tF', fc)], ['lnr'])
                MM(s5[:, :TB], onesF, tmpF[fc][:, :TB], ['cstF', ('tF', fc)], k5, start=(fc == 0), stop=(fc == KC - 1))
                MM(s6[:, :TB], onesF, lnr[:, :TB], ['cstF', 'lnr'], k6, start=(fc == 0), stop=(fc == KC - 1))
            ACT(lnm[:, :TB], s5[:, :TB], AF.Copy, k5, ['lnm'], scale=1.0 / D)
            TT('dve', tmpF[8][:, :TB], lnm[:, :TB], lnm[:, :TB], ALU.mult, ['lnm'], [('tF', 8)])
            STT(tmpF[8][:, :TB], s6[:, :TB], 1.0 / D, tmpF[8][:, :TB], ALU.mult, ALU.subtract, k6 + [('tF', 8)], [('tF', 8)])
            TS('dve', tmpF[8][:, :TB], tmpF[8][:, :TB], 0.0, ALU.max, [('tF', 8)], [('tF', 8)], s2=eps, op1=ALU.add)
            ACT(tmpF[8][:, :TB], tmpF[8][:, :TB], AF.Sqrt, [('tF', 8)], [('tF', 8)])
            RECIP(lnr[:, :TB], tmpF[8][:, :TB], [('tF', 8)], ['lnr'])
            for fc in range(KC):
                TT('dve', tmpF[fc][:, :TB], tmpF[fc][:, :TB], lnm[:, :TB], ALU.subtract, [('tF', fc), 'lnm'], [('tF', fc)])
                TT('dve', tmpF[fc][:, :TB], tmpF[fc][:, :TB], lnr[:, :TB], ALU.mult, [('tF', fc), 'lnr'], [('tF', fc)])
                ACT(tmpF[fc][:, :TB], tmpF[fc][:, :TB], AF.Identity, [('tF', fc), 'PV'], [('tF', fc)],
                    bias=bcol(fc), scale=gcol(fc))

        for si, sq in enumerate(seqs):
            T, P = sq['T'], sq['P']
            TB = min(512, T)
            NBLK = T // TB
            TP = min(128, TB)
            NTL = TB // TP
            NCH = TB // 64
            CPT = TP // 64
            I, O, X = SI[si], SO[si], SX[si]
            KX = ('xres', si)
            KXD = ('xTd', si)

            for j in range(NBLK):
                t0 = j * TB
                for fc in range(KC):
                    DMA('sp', tmpF[fc][:, :TB], I['xT'][fc * 128:(fc + 1) * 128, t0:t0 + TB], [], [('tF', fc)])
                layernorm(TB, lambda fc: PVG[:, fc:fc + 1], lambda fc: PVG[:, 8 + fc:9 + fc], 1e-5)
                for fc in range(KC):
                    DMA('sp', X['xres'][fc, :, t0:t0 + TB], tmpF[fc][:, :TB], [('tF', fc)], [KX])
                    CP('pool', tmpH[fc][:, :TB], tmpF[fc][:, :TB], [('tF', fc)], [('tH', fc)])
                    DMA('sp', X['xTd'][fc, :, t0:t0 + TB], tmpH[fc][:, :TB], [('tH', fc)], [KXD])

            for l in range(L):
                pv = lambda c, n=1: PV[:, l, c:c + n]
                DMA('sp', lnm[0:64, :], W['w_decay_up'][l], [], ['lnm'])
                DMA('sp', lnr[0:64, :], W['w_iclr_up'][l], [], ['lnr'])
                CP('pool', upw[:, 0, :], lnm[0:64, :], ['lnm'], ['upw'])
                CP('pool', upw[:, 1, :], lnr[0:64, :], ['lnr'], ['upw'])
                bl = W['b_in'][l]
                for gi, c0 in enumerate((C_BV, C_BO, C_BZ, C_CK, C_CV)):
                    DMA('sp', lnr[0:1, :], bl[c0:c0 + 512].rearrange("(o c) -> o c", o=1), [], ['lnr'])
                    CP('pool', brow[0:1, gi * 512:(gi + 1) * 512], lnr[0:1, :], ['lnr'], ['brow'])
                DMA('sp', hnG[:], W['hnh'][l], [], ['hnG'])
                DMA('sp', gnx[:], W['gnh'][l], [], ['gnx'])
                DMA('sp', wcol[:], wb_in[l].rearrange("(k p) c -> p k c", p=128)[:, :, C_BI:C_BI + 8], ['wb_in'], ['wcol'], slow=True)
                for g in range(2):
                    for h in range(4):
                        CP('pool', wrep[:, g, :, 32 * h:32 * h + 32], wcol[:, :, 4 * g + h:4 * g + h + 1].broadcast_to([128, KC, 32]),
                           ['wcol'], ['wrep'])
                if P > 0:
                    DMA('sp', carX[:], I['shift'][l], [], ['carX'])
                    for p in range(4):
                        DMA('sp', Hf[p][:, :, :], I['wkv'][l, p], [], [('Hf', p)])
                        DMA('sp', CTf[p][:], I['ct'][l, p], [], [('CTf', p)])
                    DMA('sp', carB[:], I['conv'][l], [], ['carB'])
                    DMA('sp', mbuf[:, 63:64], I['m'][l], [], ['mbuf'])
                    CP('dve', carM[:, 1:2], mbuf[:, 63:64], ['mbuf'], ['carM'])
                    MSET('dve', carM[:, 0:1], 0.0, ['carM'])
                    ci = 0
                    for p in range(4):
                        for c0 in range(0, P, 512):
                            n = min(512, P - c0)
                            DMA('sp', tmpF[ci % 8][:, 0:n], I['ck'][l, p][:, c0:c0 + n], [], [('tF', ci % 8)])
                            CP(('act', 'pool', 'dve')[ci % 3], tmpH[ci % 8][:, 0:n], tmpF[ci % 8][:, 0:n], [('tF', ci % 8)], [('tH', ci % 8)])
                            DMA('sp', X['kTd'][p, :, c0:c0 + n], tmpH[ci % 8][:, 0:n], [('tH', ci % 8)], [('kTd', si)])
                            ci += 1
                    for r0 in range(0, P, 128):
                        n = min(128, P - r0)
                        DMA('sp', tmpF[ci % 8][0:n, :], I['cv'][l, r0:r0 + n, :], [], [('tF', ci % 8)])
                        CP(('act', 'pool', 'dve')[ci % 3], tmpH[ci % 8][0:n, :], tmpF[ci % 8][0:n, :], [('tF', ci % 8)], [('tH', ci % 8)])
                        DMA('sp', X['vd'][r0:r0 + n, :], tmpH[ci % 8][0:n, :], [('tH', ci % 8)], [('vd', si)])
                        ci += 1
                else:
                    MSET('pool', carX[:], 0.0, ['carX'])
                    MSET('pool', carB[:], 0.0, ['carB'])
                    MSET('pool', mbuf[:, 0:64], 0.0, ['mbuf'])
                    MSET('pool', carM[:], 0.0, ['carM'])
                    for p in range(4):
                        MSET('pool', Hf[p][:, :, :], 0.0, [('Hf', p)])
                        MSET('pool', CTf[p][:], 0.0, [('CTf', p)])
                for p in range(4):
                    CP('pool', Hb[p][:, :, :], Hf[p][:, :, :], [('Hf', p)], [('Hb', p)])
                    CP('pool', CTb[p][:], CTf[p][:], [('CTf', p)], [('CTb', p)])

                for j in range(NBLK):
                    t0 = j * TB
                    S.enabled = True
                    S.budget = cfg.get('cut')
                    xb = xTb[(j + l) % 2]
                    KXB = ('xTb', (j + l) % 2)
                    DMA('sp', xb[:, :, :TB], X['xTd'][:, :, t0:t0 + TB].rearrange("k p t -> p k t"), [KXD], [KXB])
                    DMA('sp', pstg[:, :, :TB], I['pT'][l, :, t0:t0 + TB].rearrange("(k p) t -> p k t", p=128), [], ['pstg'])
                    CP('pool', p16[:, :, :TB], pstg[:, :, :TB], ['pstg'], ['p16'])

                    def proj_fm(slab, skey, c0, M, out_ap, okeys):
                        for kc in range(KC):
                            MM(out_ap, slab[:, kc, c0:c0 + M], xb[:, kc, :TB], [skey, KXB], okeys, start=(kc == 0), stop=(kc == KC - 1))

                    def proj_tm(slab, skey, tt, out_ap, okeys, boff):
                        for kc in range(KC):
                            MM(out_ap, xb[:, kc, tt * TP:(tt + 1) * TP], slab[:, kc, :], [skey, KXB], okeys, start=(kc == 0), stop=False)
                        MM(out_ap, onesB[0:1, 0:TP], brow[0:1, boff:boff + 512], ['cstB', 'brow'], okeys, start=False, stop=True)

                    S.enabled = 'A' in cfg.get('ph', 'ABCG')
                    pvx = lambda c, n=1: PVX[:, l, c:c + n]
                    H64 = slice(0, 64)
                    sR, kR = win_slab(l, 0, 512)
                    sK, kK = win_slab(l, 512, 512)
                    sV, kV = win_slab(l, 1024, 512)
                    sW, kW = win_slab(l, 1536, 512)
                    sZ, kZ = None, None

                    def fmx(slab, skey, crel, xi, dst, dkey):
                        rb = rawA[xi % 2]
                        rk_ = ('rawA', xi % 2)
                        ps, pk = psq(4)
                        proj_fm(slab, skey, crel, 64, ps[H64, :TB], pk)
                        ACT(rb[:, 1:1 + TB], ps[H64, :TB], AF.Identity, pk + ['PVX'], [rk_], bias=pvx(X_B + xi))
                        CP('pool', rb[:, 0:1], carX[:, xi:xi + 1], ['carX'], [rk_])
                        TS('dve', dst[H64, :TB], rb[:, 0:TB], pvx(X_MU + xi), ALU.mult, [rk_, 'PVX'], [dkey])
                        STT(dst[H64, :TB], rb[:, 1:1 + TB], pvx(X_OMU + xi), dst[H64, :TB], ALU.mult, ALU.add, [rk_, 'PVX', dkey], [dkey])
                        CP('pool', carX[:, xi:xi + 1], rb[:, TB:TB + 1], [rk_], ['carX'])

                    fmx(sW, kW, 0, 24, tmpF[0], ('tF', 0))
                    ACT(lora[:, 0, :TB], tmpF[0][H64, :TB], AF.Tanh, [('tF', 0)], ['lora'])
                    fmx(sW, kW, 64, 25, tmpF[1], ('tF', 1))
                    CP('dve', lora[:, 1, :TB], tmpF[1][H64, :TB], [('tF', 1)], ['lora'])
                    for hg in range(2):
                        for hi in range(4):
                            h = hg * 4 + hi
                            pp, hh = hi // 2, hi % 2
                            xr, xk, xv = tmpF[0], tmpF[1], tmpF[2]
                            kr_, kk_, kv_ = ('tF', 0), ('tF', 1), ('tF', 2)
                            t0_, t1_, t2_, t3_, t4_, t5_ = tmpF[3], tmpF[4], tmpF[5], tmpF[6], tmpF[7], tmpF[8]
                            k0_, k1_, k2_, k3_, k4_, k5_ = [('tF', i) for i in range(3, 9)]
                            a16, b16, k16, r16, v16, rk16, az16 = [tmpH[q * 4 + hi] for q in range(7)]
                            ka16, kb16, kk16, kr16, kv16, krk16, kaz16 = [('tH', q * 4 + hi) for q in range(7)]
                            fmx(sR, kR, h * 64, h, xr, kr_)
                            fmx(sK, kK, h * 64, 8 + h, xk, kk_)
                            fmx(sV, kV, h * 64, 16 + h, xv, kv_)
                            ps, pk = psq(4)
                            if h < 6:
                                proj_fm(sW, kW, 128 + h * 64, 64, ps[H64, :TB], pk)
                            else:
                                if sZ is None:
                                    sZ, kZ = slabZ, 'slabZ'
                                    DMA('sp', slabZ[:, :, :], wb_in[l].rearrange("(k p) c -> p k c", p=128)[:, :, 2048:2176], wkeys.get(('wb_in', l), []), ['slabZ'])
                                proj_fm(sZ, kZ, (h - 6) * 64, 64, ps[H64, :TB], pk)
                            ACT(az16[H64, :TB], ps[H64, :TB], AF.Silu, pk + ['PVX'], [kaz16], bias=pvx(X_BZ + h))
                            ps, pk = psq(4)
                            MM(ps[H64, :TB], upw[:, 0, h * 64:(h + 1) * 64], lora[:, 0, :TB], ['upw', 'lora'], pk)
                            ACT(t0_[H64, :TB], ps[H64, :TB], AF.Sigmoid, pk + ['PVX'], [k0_], bias=pvx(X_W0 + h))
                            ps, pk = psq(4)
                            MM(ps[H64, :TB], upw[:, 1, h * 64:(h + 1) * 64], lora[:, 1, :TB], ['upw', 'lora'], pk)
                            ACT(t1_[H64, :TB], ps[H64, :TB], AF.Sigmoid, pk + ['PVX'], [k1_], bias=pvx(X_A0 + h))
                            SCAN(t2_[H64, :TB], m01[H64, :TB], t0_[H64, :TB], 0.0, ALU.mult, ALU.add, ['cstF', k0_], [k2_])
                            TT('dve', t3_[H64, :TB], t2_[H64, :TB], t0_[H64, :TB], ALU.subtract, [k2_, k0_], [k3_])
                            eg = egP[pp][:, hh, :]
                            keg = ('eg', pp)
                            ACT(eg[:, :TB], t2_[H64, :TB], AF.Exp, [k2_], [keg], scale=-DECAY_C)
                            ACT(t4_[H64, :TB], t2_[H64, :TB], AF.Exp, [k2_], [k4_], scale=DECAY_C)
                            ACT(t3_[H64, :TB], t3_[H64, :TB], AF.Exp, [k3_], [k3_], scale=-DECAY_C)
                            TS('dve', t5_[H64, :TB], xk[H64, :TB], pvx(X_KK + h), ALU.mult, [kk_, 'PVX'], [k5_])
                            ACT(sqh[:, :TB], t5_[H64, :TB], AF.Square, [k5_], ['sqh'])
                            ps, pk = psq(4)
                            MM(ps[H64, :TB], onesB[H64, 0:64], sqh[:, :TB], ['cstB', 'sqh'], pk)
                            ACT(t0_[H64, :TB], ps[H64, :TB], AF.Sqrt, pk, [k0_])
                            TS('dve', t0_[H64, :TB], t0_[H64, :TB], 1e-12, ALU.max, [k0_], [k0_])
                            RECIP(t0_[H64, :TB], t0_[H64, :TB], [k0_], [k0_])
                            TT('dve', t5_[H64, :TB], t5_[H64, :TB], t0_[H64, :TB], ALU.mult, [k5_, k0_], [k5_])
                            TS('dve', t2_[H64, :TB], t1_[H64, :TB], -1.0, ALU.add, [k1_, 'PVX'], [k2_], s2=pvx(X_KA + h), op1=ALU.mult)
                            STT(t2_[H64, :TB], t2_[H64, :TB], 1.0, xk[H64, :TB], ALU.add, ALU.mult, [k2_, kk_], [k2_])
                            TT('dve', b16[H64, :TB], t5_[H64, :TB], t3_[H64, :TB], ALU.mult, [k5_, k3_], [kb16])
                            TT('dve', t0_[H64, :TB], t1_[H64, :TB], t5_[H64, :TB], ALU.mult, [k1_, k5_], [k0_])
                            STT(a16[H64, :TB], t0_[H64, :TB], -1.0, t4_[H64, :TB], ALU.mult, ALU.mult, [k0_, k4_], [ka16])
                            TT('pool', k16[H64, :TB], t2_[H64, :TB], t4_[H64, :TB], ALU.mult, [k2_, k4_], [kk16])
                            TT('pool', r16[H64, :TB], xr[H64, :TB], eg[:, :TB], ALU.mult, [kr_, keg], [kr16])
                            CP('pool', v16[H64, :TB], xv[H64, :TB], [kv_], [kv16])
                            TT('dve', t0_[H64, :TB], xr[H64, :TB], t2_[H64, :TB], ALU.mult, [kr_, k2_], [k0_])
                            TS('dve', rk16[H64, :TB], t0_[H64, :TB], pvx(X_RK + h), ALU.mult, [k0_, 'PVX'], [krk16])

                        TH = lambda q, hi: tmpH[q * 4 + hi]
                        KH = lambda q, hi: ('tH', q * 4 + hi)
                        mSU2, mSL2, mIU2, mID2 = [cm2[:, i, :] for i in range(4)]
                        f2 = lambda t: t[:, :, :].rearrange("p a b -> p (a b)")
                        for c in range(NCH):
                            cs = slice(c * 64, (c + 1) * 64)
                            for pp in range(2):
                                R_ = RS[pp]
                                rk = lambda nm, pp=pp: ('rs', pp, nm)

                                def score(ql, qr, mask, dst):
                                    ps, pk = psq(1)
                                    for hh in range(2):
                                        hi = pp * 2 + hh
                                        MM(ps[H64, hh * 64:(hh + 1) * 64], TH(ql, hi)[H64, cs], TH(qr, hi)[H64, cs], [KH(ql, hi), KH(qr, hi)], pk)
                                    TT('dve', f2(R_[dst]), ps[H64, :], mask, ALU.mult, pk + ['cm2'], [rk(dst)])

                                score(0, 1, mSU2, 'Qa')
                                score(1, 0, mSL2, 'Pa')
                                score(2, 1, mSU2, 'Abk')
                                score(0, 3, mIU2, 'Ara')
                                score(2, 3, mIU2, 'Ark')
                                TT('pool', f2(R_['Za']), f2(R_['Qa']), mID2, ALU.add, [rk('Qa'), 'cm2'], [rk('Za')])
                                for q, dst in ((4, 'Vst'), (0, 'ast'), (2, 'kst'), (6, 'azst')):
                                    pt, ptk = pst()
                                    for hh in range(2):
                                        hi = pp * 2 + hh
                                        TR(pt[H64, hh * 64:(hh + 1) * 64], TH(q, hi)[H64, cs], idB[H64, 0:64], [KH(q, hi), 'cstB'], ptk)
                                    CP('act', f2(R_[dst]), pt[H64, :], ptk, [rk(dst)])
                                ps, pk = psq(1)
                                for hh in range(2):
                                    hi = pp * 2 + hh
                                    MM(ps[H64, hh:hh + 1], TH(5, hi)[H64, cs], onesB[H64, 0:1], [KH(5, hi), 'cstB'], pk)
                                CP('act', R_['sc'][:, 0, :], ps[H64, 0:2], pk, [rk('sc')])
                            qc, pc, zc = 'Qa', 'Pa', 'Za'
                            flip = {'Qa': 'Qb', 'Qb': 'Qa', 'Pa': 'Pb', 'Pb': 'Pa', 'Za': 'Zb', 'Zb': 'Za'}
                            for lev in range(1, 7):
                                qn, pn, zn = flip[qc], flip[pc], flip[zc]
                                for pp in range(2):
                                    R_ = RS[pp]
                                    rk = lambda nm, pp=pp: ('rs', pp, nm)

                                    def mm2(lk, rk2):
                                        ps, pk = psq(1)
                                        for hh in range(2):
                                            MM(ps[H64, hh * 64:(hh + 1) * 64], R_[lk][:, hh, :], R_[rk2][:, hh, :], [rk(lk), rk(rk2)], pk)
                                        return ps, pk

                                    if lev >= 2:
                                        ps, pk = mm2(pc, zc)
                                        TT('dve', f2(R_[zn]), ps[H64, :], f2(R_[zc]), ALU.add, pk + [rk(zc)], [rk(zn)])
                                    if lev <= 5:
                                        ps, pk = mm2(qc, pc)
                                        CP('act', f2(R_[pn]), ps[H64, :], pk, [rk(pn)])
                                    if lev <= 4:
                                        ps, pk = mm2(pc, qc)
                                        CP('act', f2(R_[qn]), ps[H64, :], pk, [rk(qn)])
                                if lev >= 2:
                                    zc = zn
                                if lev <= 5:
                                    pc = pn
                                if lev <= 4:
                                    qc = qn
                            Zf = zc
                            for pp in range(2):
                                R_ = RS[pp]
                                rk = lambda nm, pp=pp: ('rs', pp, nm)
                                gp = hg * 2 + pp
                                ps, pk = psq(1)
                                for hh in range(2):
                                    hi = pp * 2 + hh
                                    o_ = ps[H64, hh * 64:(hh + 1) * 64]
                                    MM(o_, TH(1, hi)[H64, cs], Hb[gp][:, hh, :], [KH(1, hi), ('Hb', gp)], pk, start=True, stop=False)
                                    MM(o_, R_['Abk'][:, hh, :], R_['Vst'][:, hh, :], [rk('Abk'), rk('Vst')], pk, start=False, stop=True)
                                CP('act', f2(R_['Xb']), ps[H64, :], pk, [rk('Xb')])
                            for pp in range(2):
                                R_ = RS[pp]
                                rk = lambda nm, pp=pp: ('rs', pp, nm)
                                ps, pk = psq(1)
                                for hh in range(2):
                                    MM(ps[H64, hh * 64:(hh + 1) * 64], R_[Zf][:, hh, :], R_['Xb'][:, hh, :], [rk(Zf), rk('Xb')], pk)
                                CP('act', f2(R_['Ub']), ps[H64, :], pk, [rk('Ub')])
                            for pp in range(2):
                                R_ = RS[pp]
                                rk = lambda nm, pp=pp: ('rs', pp, nm)
                                gp = hg * 2 + pp
                                keg = ('eg', pp)
                                psy, pky = psq(1)
                                psh, pkh = psq(1)
                                for hh in range(2):
                                    hi = pp * 2 + hh
                                    o_ = psy[H64, hh * 64:(hh + 1) * 64]
                                    MM(o_, TH(3, hi)[H64, cs], Hb[gp][:, hh, :], [KH(3, hi), ('Hb', gp)], pky, start=True, stop=False)
                                    MM(o_, R_['Ara'][:, hh, :], R_['Ub'][:, hh, :], [rk('Ara'), rk('Ub')], pky, start=False, stop=False)
                                    MM(o_, R_['Ark'][:, hh, :], R_['Vst'][:, hh, :], [rk('Ark'), rk('Vst')], pky, start=False, stop=True)
                                for hh in range(2):
                                    o_ = psh[H64, hh * 64:(hh + 1) * 64]
                                    MM(o_, R_['ast'][:, hh, :], R_['Ub'][:, hh, :], [rk('ast'), rk('Ub')], pkh, start=True, stop=False)
                                    MM(o_, R_['kst'][:, hh, :], R_['Vst'][:, hh, :], [rk('kst'), rk('Vst')], pkh, start=False, stop=True)
                                TT('dve', f2(R_['htmp']), psh[H64, :], f2(Hf[gp]), ALU.add, pkh + [('Hf', gp)], [rk('htmp')])
                                gam = egP[pp][:, :, c * 64 + 63:c * 64 + 64].broadcast_to([64, 2, 64])
                                TT('pool', Hf[gp][:, :, :], R_['htmp'][:, :, :], gam, ALU.mult, [rk('htmp'), keg], [('Hf', gp)])
                                TT('dve', Hb[gp][:, :, :], R_['htmp'][:, :, :], gam, ALU.mult, [rk('htmp'), keg], [('Hb', gp)])
                                e1, e2, e3, scr = R_['e1'], R_['e2'], R_['e3'], R_['sc']
                                y3 = psy[H64, :].rearrange("p (a b) -> p a b", a=2)
                                bc2 = lambda i: scr[:, i, :].unsqueeze(2).broadcast_to([64, 2, 64])
                                S.op('dve', lambda e, o=scr[:, 1, :], i=y3: e.reduce_sum(out=o, in_=i, axis=mybir.AxisListType.X), pky, [rk('sc')])
                                STT(e2[:, :, :], bc2(1), -1.0 / 64, y3, ALU.mult, ALU.add, pky + [rk('sc')], [rk('e2')])
                                ACT(f2(e1), f2(e2), AF.Square, [rk('e2')], [rk('e1')])
                                S.op('dve', lambda e, o=scr[:, 2, :], i=e1[:, :, :]: e.reduce_sum(out=o, in_=i, axis=mybir.AxisListType.X), [rk('e1')], [rk('sc')])
                                TS('dve', scr[:, 3, :], scr[:, 2, :], 1.0 / 64, ALU.mult, [rk('sc')], [rk('sc')], s2=64e-5, op1=ALU.add)
                                ACT(scr[:, 3, :], scr[:, 3, :], AF.Sqrt, [rk('sc')], [rk('sc')])
                                RECIP(scr[:, 4, :], scr[:, 3, :], [rk('sc')], [rk('sc')])
                                TT('dve', e3[:, :, :], e2[:, :, :], bc2(4), ALU.mult, [rk('e2'), rk('sc')], [rk('e3')])
                                TT('dve', e3[:, :, :], e3[:, :, :], gnx[:, 2 * gp:2 * gp + 2, :], ALU.mult, [rk('e3'), 'gnx'], [rk('e3')])
                                TT('dve', e3[:, :, :], e3[:, :, :], gnx[:, 8 + 2 * gp:8 + 2 * gp + 2, :], ALU.add, [rk('e3'), 'gnx'], [rk('e3')])
                                TT('pool', e1[:, :, :], R_['Vst'][:, :, :], bc2(0), ALU.mult, [rk('Vst'), rk('sc')], [rk('e1')])
                                TT('dve', e3[:, :, :], e3[:, :, :], e1[:, :, :], ALU.add, [rk('e3'), rk('e1')], [rk('e3')])
                                TT('dve', R_['ysa'][:, :, :], e3[:, :, :], R_['azst'][:, :, :], ALU.mult, [rk('e3'), rk('azst')], [rk('ysa')])
                                pt, ptk = pst()
                                for hh in range(2):
                                    TR(pt[hh * 64:(hh + 1) * 64, 0:64], R_['ysa'][:, hh, :], idB[H64, 0:64], [rk('ysa'), 'cstB'], ptk)
                                CP('act', ysT[:, gp, c * 64:c * 64 + 64], pt[:, 0:64], ptk, [('ysT', gp)])

                    S.enabled = 'B' in cfg.get('ph', 'ABCG')
                    sBQ, kBQ = win_slab(l, C_BQ, 512)
                    sBK, kBK = win_slab(l, C_BK, 512)
                    qT = [tmpH[i] for i in range(4)]
                    kTm = [tmpH[4 + i] for i in range(4)]
                    oz = [tmpH[8 + i] for i in range(4)]
                    ktg = [tmpH[12 + i] for i in range(8)]
                    ysb = [tmpH[20 + i] for i in range(4)]
                    for fc in range(8):
                        slab, skey = (sBQ, kBQ) if fc < 4 else (sBK, kBK)
                        rb = rawB[fc % 2]
                        rk_ = ('rawB', fc % 2)
                        ps, pk = psq(4)
                        proj_fm(slab, skey, (fc % 4) * 128, 128, ps[:, :TB], pk)
                        ACT(rb[:, 3:3 + TB], ps[:, :TB], AF.Identity, pk + ['PV'], [rk_], bias=pv(PV_BQK + fc))
                        CP('pool', rb[:, 0:3], carB[:, fc, :], ['carB'], [rk_])
                        tq = tmpF[fc % 2]
                        tk_ = ('tF', fc % 2)
                        ACT(tq[:, :TB], rb[:, 3:3 + TB], AF.Identity, [rk_, 'PV'], [tk_], scale=pv(PV_CW + 24 + fc), bias=pv(PV_CB + fc))
                        for jj in range(3):
                            STT(tq[:, :TB], rb[:, jj:jj + TB], pv(PV_CW + 8 * jj + fc), tq[:, :TB], ALU.mult, ALU.add, [rk_, 'PV', tk_], [tk_])
                        CP('pool', carB[:, fc, :], rb[:, TB:TB + 3], [rk_], ['carB'])
                        dst = qT[fc] if fc < 4 else kTm[fc - 4]
                        ACT(dst[:, :TB], tq[:, :TB], AF.Silu, [tk_], [('tH', fc)])
                    iv, sp_, fn_, bn_, x_, g_, t1_, t3_ = [tmpF[i] for i in range(8)]
                    K = [('tF', i) for i in range(8)]
                    ps, pk = psq(4)
                    for kc in range(KC):
                        MM(ps[:, :TB], wrep[:, 0, kc, :], xb[:, kc, :TB], ['wrep', KXB], pk, start=(kc == 0), stop=(kc == KC - 1))
                    ACT(iv[:, :TB], ps[:, :TB], AF.Identity, pk + ['PV'], [K[0]], bias=pv(PV_BI))
                    ps, pk = psq(4)
                    for kc in range(KC):
                        MM(ps[:, :TB], wrep[:, 1, kc, :], xb[:, kc, :TB], ['wrep', KXB], pk, start=(kc == 0), stop=(kc == KC - 1))
                    ACT(sp_[:, :TB], ps[:, :TB], AF.Exp, pk + ['PV'], [K[1]], bias=pv(PV_NBF), scale=-1.0)
                    ACT(sp_[:, :TB], sp_[:, :TB], AF.Ln, [K[1]], [K[1]], bias=1.0)
                    SCAN(fn_[:, :TB], sp_[:, :TB], sp_[:, :TB], carM[:, 0:1], ALU.add, ALU.bypass, [K[1], 'carM'], [K[2]])
                    SCAN(bn_[:, :TB], m01[:, :TB], sp_[:, :TB], 0.0, ALU.mult, ALU.add, ['cstF', K[1]], [K[3]])
                    TT('dve', x_[:, :TB], iv[:, :TB], fn_[:, :TB], ALU.add, [K[0], K[2]], [K[4]])
                    SCAN(g_[:, :TB], x_[:, :TB], x_[:, :TB], carM[:, 1:2], ALU.max, ALU.bypass, [K[4], 'carM'], [K[5]])
                    TT('dve', mbuf[:, 64:64 + TB], g_[:, :TB], fn_[:, :TB], ALU.subtract, [K[5], K[2]], ['mbuf'])
                    CP('pool', carM[:, 0:1], fn_[:, TB - 1:TB], [K[2]], ['carM'])
                    CP('pool', carM[:, 1:2], g_[:, TB - 1:TB], [K[5]], ['carM'])
                    mcur = mbuf[:, 64:64 + TB]
                    v3 = lambda ap: ap.rearrange("p (c t) -> p c t", t=64)
                    bc = lambda ap: v3(ap)[:, :, 63:64].broadcast_to([128, NCH, 64])
                    Ra, Rsc, Rcl, Rg, RgL = x_, g_, fn_, iv, t3_
                    STT(t1_[:, :TB], bn_[:, :TB], -1.0, mcur, ALU.mult, ALU.subtract, [K[3], 'mbuf'], [K[6]])
                    ACT(Ra[:, :TB], t1_[:, :TB], AF.Exp, [K[6]], [K[4]])
                    TT('dve', v3(t1_[:, :TB]), v3(t1_[:, :TB]), bc(mbuf[:, 0:TB]), ALU.add, [K[6], 'mbuf'], [K[6]])
                    ACT(Rsc[:, :TB], t1_[:, :TB], AF.Exp, [K[6]], [K[5]])
                    ACT(Rcl[:, :TB], mcur, AF.Exp, ['mbuf'], [K[2]], scale=-1.0)
                    TT('dve', t3_[:, :TB], iv[:, :TB], bn_[:, :TB], ALU.add, [K[0], K[3]], [K[7]])
                    ACT(Rg[:, :TB], t3_[:, :TB], AF.Exp, [K[7]], [K[0]], bias=LN_C)
                    TT('dve', v3(t3_[:, :TB]), v3(t3_[:, :TB]), bc(bn_[:, :TB]), ALU.subtract, [K[7], K[3]], [K[7]])
                    TT('dve', v3(t3_[:, :TB]), v3(t3_[:, :TB]), bc(mcur), ALU.subtract, [K[7], 'mbuf'], [K[7]])
                    ACT(RgL[:, :TB], t3_[:, :TB], AF.Exp, [K[7]], [K[7]], bias=LN_C)
                    RQ = [(Ra, K[4]), (Rsc, K[5]), (Rcl, K[2]), (Rg, K[0]), (RgL, K[7])]
                    for tt in range(NTL):
                        ps, pk = psq(4)
                        for qi in range(4):
                            MM(ps[0:TP, qi * 128:(qi + 1) * 128], RQ[qi][0][:, tt * TP:(tt + 1) * TP], idF, [RQ[qi][1], 'cstF'], pk)
                        CP('dve', tokS[0:TP, tt, 0:4, :], ps[0:TP, :].rearrange("p (q h r) -> p q h r", q=4, h=4)[:, :, :, 0], pk, [('tokS', tt)])
                        ps, pk = psq(1)
                        MM(ps[0:TP, :], RQ[4][0][:, tt * TP:(tt + 1) * TP], idF, [RQ[4][1], 'cstF'], pk)
                        CP('dve', tokS[0:TP, tt, 4, :], ps[0:TP, :].rearrange("p (h r) -> p h r", h=4)[:, :, 0], pk, [('tokS', tt)])
                        for jj in range(CPT):
                            TS('dve', tokS2[0:TP, tt, jj, :], tokS[0:TP, tt, 4, :], rmask[0:TP, jj:jj + 1], ALU.mult, [('tokS', tt), 'rmask'], [('tokS2', tt)])
                    ps, pk = psq(1)
                    for h in range(4):
                        MM(ps[:, h * NCH:(h + 1) * NCH], cstF[:, 768 + 128 * h:768 + 128 * (h + 1)],
                           v3(Rsc[:, :TB])[:, :, 63], ['cstF', K[5]], pk)
                    CP('dve', csB[:, :, 0:NCH], ps[:, 0:4 * NCH].rearrange("p (h c) -> p h c", h=4), pk, ['csB'])
                    CP('pool', mbuf[:, 63:64], mbuf[:, 63 + TB:64 + TB], ['mbuf'], ['mbuf'])
                    sV2, kV2 = win_slab(l, C_BV, 512)
                    sO, kO = win_slab(l, C_BO, 512)
                    for tt in range(NTL):
                        ps, pk = psq(4)
                        proj_tm(sV2, kV2, tt, ps[0:TP, :], pk, 0)
                        CP('act', vaug[0:TP, tt, :, 0:128], ps[0:TP, :].rearrange("p (h d) -> p h d", h=4), pk, [('vaug', tt)])
                    for tt in range(NTL):
                        ps, pk = psq(4)
                        proj_tm(sO, kO, tt, ps[0:TP, :], pk, 512)
                        ACT(oz[tt][0:TP, :], ps[0:TP, :], AF.Sigmoid, pk, [('tH', 8 + tt)])
                    sZ2, kZ2 = win_slab(l, C_BZ, 512)
                    for tt in range(NTL):
                        ps, pk = psq(4)
                        proj_tm(sZ2, kZ2, tt, ps[0:TP, :], pk, 1024)
                        ACT(tmpF[8][0:TP, :], ps[0:TP, :], AF.Silu, pk, [('tF', 8)])
                        TT('dve', oz[tt][0:TP, :], oz[tt][0:TP, :], tmpF[8][0:TP, :], ALU.mult, [('tH', 8 + tt), ('tF', 8)], [('tH', 8 + tt)])
                    for tt in range(NTL):
                        for h in range(4):
                            pt, ptk = pst()
                            TR(pt[0:TP, :], kTm[h][:, tt * TP:(tt + 1) * TP], idB, [('tH', 4 + h), 'cstB'], ptk)
                            for jj in range(CPT):
                                TS('dve', ktg[2 * tt + jj][0:TP, h * 128:(h + 1) * 128], pt[0:TP, :], tokS2[0:TP, tt, jj, h:h + 1], ALU.mult,
                                   ptk + [('tokS2', tt)], [('tH', 12 + 2 * tt + jj)])
                    for tt in range(NTL):
                        tsl = slice(tt * TP, (tt + 1) * TP)
                        for h in range(4):
                            hsl = slice(h * 128, (h + 1) * 128)
                            wt_ = mlh[h % 2]
                            kwt = ('mlh', h % 2)
                            ps, pk = psq(1)
                            MM(ps[0:TP, 0:TP], kTm[h][:, tsl], qT[h][:, tsl], [('tH', 4 + h), ('tH', h)], pk)
                            STT(wt_[0:TP, 0:TP], ps[0:TP, 0:TP], tokS[0:TP, tt, 3, h:h + 1], mIU[0:TP, 0:TP], ALU.mult, ALU.mult,
                                pk + [('tokS', tt), 'cstB'], [kwt])
                            psn, pkn = psq(2)
                            MM(psn[0:TP, 0:129], wt_[0:TP, 0:TP], vaug[0:TP, tt, h, :], [kwt, ('vaug', tt), 'vaug'], pkn)
                            psi, pki = psq(2)
                            for jj in range(CPT):
                                c = tt * CPT + jj
                                js = slice(jj * 64, jj * 64 + 64)
                                MM(psi[js, 0:129], qT[h][:, tt * TP + jj * 64:tt * TP + jj * 64 + 64], CTb[h][:], [('tH', h), ('CTb', h)], pki)
                                pss, pks = psq(2)
                                MM(pss[:, 0:129], ktg[2 * tt + jj][0:TP, hsl], vaug[0:TP, tt, h, :], [('tH', 12 + 2 * tt + jj), ('vaug', tt), 'vaug'], pks)
                                STT(CTf[h][:], CTf[h][:], csB[:, h, c:c + 1], pss[:, 0:129], ALU.mult, ALU.add, [('CTf', h), 'csB'] + pks, [('CTf', h)])
                                CP('pool', CTb[h][:], CTf[h][:], [('CTf', h)], [('CTb', h)])
                            m1, m2 = mlt[(2 * h) % 4], mlt[(2 * h + 1) % 4]
                            km1, km2 = ('mlt', (2 * h) % 4), ('mlt', (2 * h + 1) % 4)
                            sc_ = mls[:, h % 8, :] if False else mls[:, (tt * 4 + h) % 8, :]
                            ksc = ('mls', (tt * 4 + h) % 8)
                            TS('dve', m1[0:TP, 0:129], psi[0:TP, 0:129], tokS[0:TP, tt, 1, h:h + 1], ALU.mult, pki + [('tokS', tt)], [km1])
                            STT(m2[0:TP, 0:129], psn[0:TP, 0:129], tokS[0:TP, tt, 0, h:h + 1], m1[0:TP, 0:129], ALU.mult, ALU.add,
                                pkn + [('tokS', tt), km1], [km2])
                            ACT(sc_[0:TP, 0:1], m2[0:TP, 128:129], AF.Abs, [km2], [ksc])
                            TS('dve', sc_[0:TP, 0:1], sc_[0:TP, 0:1], tokS[0:TP, tt, 2, h:h + 1], ALU.max, [ksc, ('tokS', tt)], [ksc])
                            RECIP(sc_[0:TP, 1:2], sc_[0:TP, 0:1], [ksc], [ksc])
                            TS('dve', m1[0:TP, 0:128], m2[0:TP, 0:128], sc_[0:TP, 1:2], ALU.mult, [km2, ksc], [km1, ksc],
                               s2=0.0, op1=ALU.add, accum=sc_[0:TP, 2:3])
                            TS('dve', sc_[0:TP, 2:3], sc_[0:TP, 2:3], -1.0 / 128, ALU.mult, [ksc], [ksc])
                            TS('dve', m1[0:TP, 0:128], m1[0:TP, 0:128], sc_[0:TP, 2:3], ALU.add, [km1, ksc], [km1])
                            ACT(m2[0:TP, 0:128], m1[0:TP, 0:128], AF.Square, [km1], [km2, ksc], accum=sc_[0:TP, 3:4])
                            TS('dve', sc_[0:TP, 4:5], sc_[0:TP, 3:4], 1.0 / 128, ALU.mult, [ksc], [ksc], s2=1e-6, op1=ALU.add)
                            ACT(sc_[0:TP, 4:5], sc_[0:TP, 4:5], AF.Sqrt, [ksc], [ksc])
                            RECIP(sc_[0:TP, 5:6], sc_[0:TP, 4:5], [ksc], [ksc])
                            STT(m2[0:TP, 0:128], m1[0:TP, 0:128], sc_[0:TP, 5:6], hnG[0:TP, hsl], ALU.mult, ALU.mult, [km1, ksc, 'hnG'], [km2])
                            TT('dve', ysb[tt][0:TP, hsl], m2[0:TP, 0:128], oz[tt][0:TP, hsl], ALU.mult, [km2, ('tH', 8 + tt)], [('tH', 20 + tt)])
                        for h in range(4):
                            pt, ptk = pst()
                            TR(pt[:, 0:TP], ysb[tt][0:TP, h * 128:(h + 1) * 128], idB[0:TP, 0:TP], [('tH', 20 + tt), 'cstB'], ptk)
                            CP('act', ysT[:, 4 + h, tsl], pt[:, 0:TP], ptk, [('ysT', 4 + h)])

                    S.enabled = ('C' in cfg.get('ph', 'ABCG')) or ('c' in cfg.get('ph', 'ABCG'))
                    sQ, kQ = win_slab(l, C_CQ, 512)
                    sKc, kKc = win_slab(l, C_CK, 512)
                    qs16 = [tmpH[i] for i in range(8)]
                    nq16 = [tmpH[8 + i] for i in range(8)]
                    kf16 = [tmpH[16 + i] for i in range(4)]
                    for h in range(8):
                        oth = slice(64 - (h % 2) * 64, 128 - (h % 2) * 64)
                        MSET('pool', qs16[h][oth, :TB], 0.0, [('tH', h)])
                        MSET('pool', nq16[h][oth, :TB], 0.0, [('tH', 8 + h)])
                    for p in range(4):
                        ps, pk = psq(4)
                        proj_fm(sQ, kQ, p * 128, 128, ps[:, :TB], pk)
                        for hh in range(2):
                            h = 2 * p + hh
                            hs = slice(hh * 64, hh * 64 + 64)
                            ACT(qs16[h][hs, :TB], ps[hs, :TB], AF.Identity, pk + ['PV'], [('tH', h)], bias=PV[hs, l, PV_BCQ + p:PV_BCQ + p + 1], scale=0.125)
                            ACT(nq16[h][hs, :TB], ps[hs, :TB], AF.Identity, pk + ['PV'], [('tH', 8 + h)], bias=PV[hs, l, PV_NBCQ + p:PV_NBCQ + p + 1], scale=-0.125)
                    for p in range(4):
                        ps, pk = psq(4)
                        proj_fm(sKc, kKc, p * 128, 128, ps[:, :TB], pk)
                        ACT(kf16[p][:, :TB], ps[:, :TB], AF.Identity, pk + ['PV'], [('tH', 16 + p)], bias=pv(PV_BCK + p))
                        DMA('sp', X['kTd'][p, :, P + t0:P + t0 + TB], kf16[p][:, :TB], [('tH', 16 + p)], [('kTd', si)])
                    for tt in range(NTL):
                        ps, pk = psq(4)
                        proj_tm(sKc, kKc, tt, ps[0:TP, :], pk, 1536)
                        CP('act', tmpF[tt % 2][0:TP, :], ps[0:TP, :], pk, [('tF', tt % 2)])
                        DMA('sp', O['sbk'][l, t0 + tt * TP:t0 + (tt + 1) * TP, :], tmpF[tt % 2][0:TP, :], [('tF', tt % 2)], [], is_out=True)
                    sVc, kVc = win_slab(l, C_CV, 512)
                    for tt in range(NTL):
                        ps, pk = psq(4)
                        proj_tm(sVc, kVc, tt, ps[0:TP, :], pk, 2048)
                        CP('act', tmpF[2 + tt % 2][0:TP, :], ps[0:TP, :], pk, [('tF', 2 + tt % 2)])
                        CP('act', tmpH[20 + tt % 2][0:TP, :], ps[0:TP, :], pk, [('tH', 20 + tt % 2)])
                        DMA('sp', O['sbv'][l, t0 + tt * TP:t0 + (tt + 1) * TP, :], tmpF[2 + tt % 2][0:TP, :], [('tF', 2 + tt % 2)], [], is_out=True)
                        DMA('sp', X['vd'][P + t0 + tt * TP:P + t0 + (tt + 1) * TP, :], tmpH[20 + tt % 2][0:TP, :], [('tH', 20 + tt % 2)], [('vd', si)])
                    sZc, kZc = win_slab(l, C_CZ, 512)
                    S.enabled = 'C' in cfg.get('ph', 'ABCG')
                    q0 = P + t0
                    kend = q0 + TB
                    nkb = (kend + 127) // 128
                    for half in range(2):
                        accs = [psbank(4 + i) for i in range(2)]
                        SP = [tmpH[22 + i] for i in range(4)]
                        for ki, kb in enumerate(reversed(range(nkb))):
                            k0 = kb * 128
                            ks = min(128, kend - k0)
                            kt_, vt_ = kTblk[ki % 2], vblk[ki % 2]
                            kkt, kvt = ('kTblk', ki % 2), ('vblk', ki % 2)
                            DMA('sp', kt_[:, :, 0:ks], X['kTd'][:, :, k0:k0 + ks].rearrange("c p s -> p c s"), [('kTd', si)], [kkt])
                            DMA('sp', vt_[0:ks, :], X['vd'][k0:k0 + ks, :], [('vd', si)], [kvt])
                            masked = (k0 + ks > q0)
                            mk = attB[0:ks, (k0 - q0) // 128, 0:TB] if masked else None
                            for hi in range(4):
                                h = half * 4 + hi
                                p = h // 2
                                hs = slice((h % 2) * 64, (h % 2) * 64 + 64)
                                psz, pkz = psq(4)
                                MM(psz[0:ks, :TB], kt_[:, p, 0:ks], qs16[h][:, :TB], [kkt, ('tH', h)], pkz)
                                ef = tmpF[4 + hi % 4]
                                kef = ('tF', 4 + hi % 4)
                                sp16 = tmpH[26 + hi % 2]
                                ksp = ('tH', 26 + hi % 2)
                                A16 = mlh2[hi % 2]
                                kA = ('mlh2', hi % 2)
                                ACT(ef[0:ks, :TB], psz[0:ks, :TB], AF.Exp, pkz, [kef])
                                ACT(sp16[0:ks, :TB], ef[0:ks, :TB], AF.Ln, [kef], [ksp], bias=1.0)
                                if masked:
                                    TT('pool', sp16[0:ks, :TB], sp16[0:ks, :TB], mk, ALU.mult, [ksp, 'attB'], [ksp])
                                psa, pka = psq(4)
                                MM(psa[0:ks, :TB], triB[0:ks, 0:ks], sp16[0:ks, :TB], ['cstB', ksp], pka, start=True, stop=False)
                                if ki > 0:
                                    MM(psa[0:ks, :TB], onesB[:, 0:ks], SP[hi][:, :TB], ['cstB', ('tH', 22 + hi)], pka, start=False, stop=False)
                                MM(psa[0:ks, :TB], kt_[:, p, 0:ks], nq16[h][:, :TB], [kkt, ('tH', 8 + h)], pka, start=False, stop=True)
                                ACT(A16[0:ks, :TB], psa[0:ks, :TB], AF.Exp, pka, [kA], scale=-1.0)
                                if masked:
                                    TT('dve', A16[0:ks, :TB], A16[0:ks, :TB], mk, ALU.mult, [kA, 'attB'], [kA])
                                acc, kacc = accs[hi // 2]
                                MM(acc[hs, :TB], vt_[0:ks, h * 64:(h + 1) * 64], A16[0:ks, :TB], [kvt, kA], kacc,
                                   start=(ki == 0), stop=(ki == nkb - 1))
                                if ki == 0:
                                    if ks < 128:
                                        MSET('pool', SP[hi][:, :TB], 0.0, [('tH', 22 + hi)])
                                    CP('pool', SP[hi][0:ks, :TB], sp16[0:ks, :TB], [ksp], [('tH', 22 + hi)])
                                elif ki < nkb - 1:
                                    TT('pool', SP[hi][0:ks, :TB], SP[hi][0:ks, :TB], sp16[0:ks, :TB], ALU.add, [ksp, ('tH', 22 + hi)], [('tH', 22 + hi)])
                        for pi in range(2):
                            p = half * 2 + pi
                            ps, pk = psq(4)
                            proj_fm(sZc, kZc, p * 128, 128, ps[:, :TB], pk)
                            ACT(tmpH[21][:, :TB], ps[:, :TB], AF.Silu, pk + ['PV'], [('tH', 21)], bias=pv(PV_BCZ + p))
                            acc, kacc = accs[pi]
                            TT('dve', ysT[:, 8 + p, :TB], acc[:, :TB], tmpH[21][:, :TB], ALU.mult, kacc + [('tH', 21)], [('ysT', 8 + p)])

                    S.enabled = 'G' in cfg.get('ph', 'ABCG')
                    for n in range(3):
                        sB, kB = load_slab(wb_br[l, n].rearrange("(k p) c -> p k c", p=128), 4, 1024, ('wb_br', l))
                        for half in range(2):
                            sG, kG = win_slab(l, C_G0 + n * 1024 + half * 512, 512)
                            for f4 in range(4):
                                fc = half * 4 + f4
                                psg, pkg = psq(4)
                                proj_fm(sG, kG, f4 * 128, 128, psg[:, :TB], pkg)
                                psr, pkr = psq(4)
                                for kc in range(4):
                                    MM(psr[:, :TB], sB[:, kc, fc * 128:(fc + 1) * 128], ysT[:, 4 * n + kc, :TB], [kB, ('ysT', 4 * n + kc)], pkr,
                                       start=(kc == 0), stop=(kc == 3))
                                ACT(tmpF[8][:, :TB], psg[:, :TB], AF.Sigmoid, pkg + ['PV'], [('tF', 8)], bias=pv(PV_BG + n * 8 + fc))
                                if n == 0:
                                    TT('dve', tmpF[fc][:, :TB], tmpF[8][:, :TB], psr[:, :TB], ALU.mult, [('tF', 8)] + pkr, [('tF', fc)])
                                else:
                                    TT('dve', tmpF[8][:, :TB], tmpF[8][:, :TB], psr[:, :TB], ALU.mult, [('tF', 8)] + pkr, [('tF', 8)])
                                    if n == 1:
                                        TT('dve', tmpF[fc][:, :TB], tmpF[fc][:, :TB], tmpF[8][:, :TB], ALU.add, [('tF', fc), ('tF', 8)], [('tF', fc)])
                                    else:
                                        TT('dve', tmpH[fc][:, :TB], tmpF[fc][:, :TB], tmpF[8][:, :TB], ALU.add, [('tF', fc), ('tF', 8)], [('tH', fc)])
                    s5, k5 = psbank(4)
                    s6, k6 = psbank(5)
                    alpha = (2.0 * cfg.get('DEPTH', 4)) ** 0.25
                    for half in range(2):
                        sWo, kWo = load_slab(wb_out[l].rearrange("(k p) c -> p k c", p=128)[:, :, half * 512:(half + 1) * 512], KC, 512, ('wb_out', l))
                        for f4 in range(4):
                            fc = half * 4 + f4
                            psm, pkm = psq(4)
                            for kc in range(KC):
                                MM(psm[:, :TB], sWo[:, kc, f4 * 128:(f4 + 1) * 128], tmpH[kc][:, :TB], [kWo, ('tH', kc)], pkm,
                                   start=(kc == 0), stop=(kc == KC - 1))
                            DMA('sp', tmpF[8][:, :TB], X['xres'][fc, :, t0:t0 + TB], [KX], [('tF', 8)])
                            STT(tmpF[fc][:, :TB], tmpF[8][:, :TB], alpha, psm[:, :TB], ALU.mult, ALU.add, [('tF', 8)] + pkm, [('tF', fc)])
                    layernorm(TB, lambda fc: pv(PV_LNG + fc), lambda fc: pv(PV_LNB + fc), 1e-5)
                    for fc in range(KC):
                        CP('pool', ysT[:, fc, :TB], tmpF[fc][:, :TB], [('tF', fc)], [('ysT', fc)])
                    sPl, kPl = load_slab(wb_ple[l].rearrange("(k p) c -> p k c", p=128), 2, 1024, ('wb_ple', l))
                    last = (l == L - 1)
                    for half in range(2):
                        sPg, kPg = load_slab(wb_pg[l].rearrange("(k p) c -> p k c", p=128)[:, :, half * 512:(half + 1) * 512], KC, 512, ('wb_pg', l))
                        for f4 in range(4):
                            fc = half * 4 + f4
                            psg, pkg = psq(4)
                            for kc in range(KC):
                                MM(psg[:, :TB], sPg[:, kc, f4 * 128:(f4 + 1) * 128], ysT[:, kc, :TB], [kPg, ('ysT', kc)], pkg,
                                   start=(kc == 0), stop=(kc == KC - 1))
                            psp, pkp = psq(4)
                            for kc in range(2):
                                MM(psp[:, :TB], sPl[:, kc, fc * 128:(fc + 1) * 128], p16[:, kc, :TB], [kPl, 'p16'], pkp, start=(kc == 0), stop=(kc == 1))
                            ACT(tmpF[8][:, :TB], psg[:, :TB], AF.Sigmoid, pkg, [('tF', 8)])
                            TT('dve', tmpF[8][:, :TB], tmpF[8][:, :TB], psp[:, :TB], ALU.mult, [('tF', 8)] + pkp, [('tF', 8)])
                            TT('dve', tmpF[fc][:, :TB], tmpF[fc][:, :TB], tmpF[8][:, :TB], ALU.add, [('tF', fc), ('tF', 8)], [('tF', fc)])
                            if last:
                                DMA('sp', O['yT'][fc * 128:(fc + 1) * 128, t0:t0 + TB], tmpF[fc][:, :TB], [('tF', fc)], [], is_out=True)
                            else:
                                DMA('sp', X['xres'][fc, :, t0:t0 + TB], tmpF[fc][:, :TB], [('tF', fc)], [KX])
                                CP('pool', tmpH[16 + fc][:, :TB], tmpF[fc][:, :TB], [('tF', fc)], [('tH', 16 + fc)])
                                DMA('sp', X['xTd'][fc, :, t0:t0 + TB], tmpH[16 + fc][:, :TB], [('tH', 16 + fc)], [KXD])

                S.enabled = True
                S.budget = None
                DMA('sp', O['shift'][l], carX[:], ['carX'], [], is_out=True)
                DMA('sp', O['conv'][l], carB[:], ['carB'], [], is_out=True)
                DMA('sp', O['m'][l], mbuf[:, 63:64], ['mbuf'], [], is_out=True)
                for p in range(4):
                    DMA('sp', O['wkv'][l, p], Hf[p][:, :, :], [('Hf', p)], [], is_out=True)
                    DMA('sp', O['ct'][l, p], CTf[p][:], [('CTf', p)], [], is_out=True)

        S.emit(nc)
    return nc


N_CORES = 8
_CACHE = {}


def _seq_inputs(i, x, p, st):
    d = {f"xT{i}": np.ascontiguousarray(x.T), f"pT{i}": np.ascontiguousarray(p.transpose(0, 2, 1))}
    if st is not None:
        shift, wkv, conv, c, n, m, ck, cv = st
        L = shift.shape[0]
        d[f"shift{i}"] = np.ascontiguousarray(shift.reshape(L, 26, 64).transpose(0, 2, 1))
        d[f"wkv{i}"] = np.ascontiguousarray(wkv.reshape(L, 4, 2, 64, 64).transpose(0, 1, 4, 2, 3))
        d[f"conv{i}"] = np.ascontiguousarray(conv.reshape(L, 3, 8, 128).transpose(0, 3, 2, 1))
        d[f"ct{i}"] = np.ascontiguousarray(np.concatenate([c.transpose(0, 1, 3, 2), n[..., None]], axis=-1))
        d[f"m{i}"] = np.ascontiguousarray(np.repeat(m, 32, axis=1)[..., None])
        Pn = ck.shape[1]
        d[f"ck{i}"] = np.ascontiguousarray(ck.reshape(L, Pn, 4, 128).transpose(0, 2, 3, 1))
        d[f"cv{i}"] = np.ascontiguousarray(cv.reshape(L, Pn, 512))
    return d


def _seq_outputs(r, i, L, T):
    y = r[f"yT{i}"].T
    shift = r[f"shift_o{i}"].transpose(0, 2, 1).reshape(L, 1664)
    wkv = r[f"wkv_o{i}"].transpose(0, 1, 3, 4, 2).reshape(L, 8, 64, 64)
    conv = r[f"conv_o{i}"].transpose(0, 3, 2, 1).reshape(L, 3, 1024)
    ct = r[f"ct_o{i}"]
    c = ct[..., 0:128].transpose(0, 1, 3, 2)
    n = ct[..., 128]
    m = r[f"m_o{i}"][:, ::32, 0]
    sbk = r[f"sbk{i}"].reshape(L, T, 8, 64)
    sbv = r[f"sbv{i}"].reshape(L, T, 8, 64)
    return [y, shift, wkv, conv, c, n, m, sbk, sbv]


WNAMES = ['ln_in_g', 'ln_in_b', 'w_in', 'b_in', 'mu_a', 'w0_a', 'w_decay_up', 'a0_a', 'w_iclr_up', 'k_k', 'k_a', 'r_k',
          'gn_a_g', 'gn_a_b', 'conv_b_w', 'conv_b_b', 'hn_b_g', 'w_branch', 'w_out', 'ln_g', 'ln_b', 'w_ple', 'w_ple_gate']


def weight_inputs(w, L):
    f = lambda a: np.ascontiguousarray(np.asarray(a, dtype=np.float32))
    wd = {k: f(w[k]) for k in WNAMES}
    wd['r_k'] = wd['r_k'].reshape(L, 512)
    wd['cst'] = make_consts()
    wd['catt'] = make_att()
    col = lambda v: v.reshape(-1, 128).T
    pv = np.zeros((128, L, NPV), np.float32)
    for l in range(L):
        b = wd['b_in'][l]
        pv[:, l, PV_BA:PV_BA + 17] = col(b[0:2176])
        pv[:, l, PV_BQK:PV_BQK + 8] = col(b[C_BQ:C_BQ + 1024])
        pv[:, l, PV_BCQ:PV_BCQ + 4] = col(b[C_CQ:C_CQ + 512])
        pv[:, l, PV_BCK:PV_BCK + 4] = col(b[C_CK:C_CK + 512])
        pv[:, l, PV_BCZ:PV_BCZ + 4] = col(b[C_CZ:C_CZ + 512])
        pv[:, l, PV_BG:PV_BG + 24] = col(b[C_G0:C_G0 + 3072])
        pv[:, l, PV_BI] = np.repeat(b[C_BI:C_BI + 4], 32)
        pv[:, l, PV_NBF] = np.repeat(b[C_BF:C_BF + 4], 32)
        pv[:, l, PV_MU:PV_MU + 13] = col(wd['mu_a'][l])
        for nm, c in (('w0_a', PV_W0), ('a0_a', PV_A0), ('k_k', PV_KK), ('k_a', PV_KA), ('r_k', PV_RK)):
            pv[:, l, c:c + 4] = col(wd[nm][l])
        for j in range(4):
            pv[:, l, PV_CW + 8 * j:PV_CW + 8 * j + 8] = col(wd['conv_b_w'][l, j])
        pv[:, l, PV_CB:PV_CB + 8] = col(wd['conv_b_b'][l])
        pv[:, l, PV_LNG:PV_LNG + 8] = col(wd['ln_g'][l])
        pv[:, l, PV_LNB:PV_LNB + 8] = col(wd['ln_b'][l])
    wd['pvh'] = pv
    wd['pvg'] = np.ascontiguousarray(np.concatenate([col(wd['ln_in_g']), col(wd['ln_in_b'])], axis=1))
    gn = np.zeros((L, 64, 16, 64), np.float32)
    gn[:, :, 0:8, :] = wd['gn_a_g'].reshape(L, 1, 8, 64)
    gn[:, :, 8:16, :] = wd['gn_a_b'].reshape(L, 1, 8, 64)
    wd['gnh'] = gn
    c64 = lambda v: v.reshape(-1, 64).T
    pvx = np.zeros((64, L, NX), np.float32)
    for l in range(L):
        pvx[:, l, X_B:X_B + 26] = c64(wd['b_in'][l, 0:1664])
        pvx[:, l, X_MU:X_MU + 26] = c64(wd['mu_a'][l])
        pvx[:, l, X_BZ:X_BZ + 8] = c64(wd['b_in'][l, C_AZ:C_AZ + 512])
        for nm, c in (('w0_a', X_W0), ('a0_a', X_A0), ('k_k', X_KK), ('k_a', X_KA), ('r_k', X_RK)):
            pvx[:, l, c:c + 8] = c64(wd[nm][l])
    wd['pvx'] = pvx
    i64 = np.arange(64)
    r_, c_ = i64[:, None], i64[None, :]
    wd['cst3'] = np.ascontiguousarray(np.concatenate([np.tile(m, (1, 2)) for m in (r_ < c_, r_ > c_, r_ <= c_, r_ == c_)], axis=1).astype(np.float32))
    wd['hnh'] = np.ascontiguousarray(np.broadcast_to(wd['hn_b_g'][:, None, :], (L, 128, 512)))
    return wd


def kernel(x_prompt, x_sample, state_shift_a, state_wkv, state_conv_b, state_mlstm_c, state_mlstm_n,
           state_mlstm_m, cache_sb_k, cache_sb_v, p_prompt, p_sample, **weights):
    f = lambda a: np.ascontiguousarray(np.asarray(a, dtype=np.float32))
    x_prompt, x_sample, p_prompt, p_sample = f(x_prompt), f(x_sample), f(p_prompt), f(p_sample)
    L = p_prompt.shape[0]
    Bp, Tp = x_prompt.shape[:2]
    Bs, Ts = x_sample.shape[:2]
    Pn = cache_sb_k.shape[2]
    npc = Bp // N_CORES
    cfg = {'L': L, 'seqs': [{'T': Tp, 'P': 0}] * npc + [{'T': Ts, 'P': Pn}]}
    key = (L, Tp, Ts, Pn, npc)
    if key not in _CACHE:
        _CACHE[key] = build(cfg)
    nc = _CACHE[key]
    wd = weight_inputs(weights, L)
    sts = [f(a) for a in (state_shift_a, state_wkv, state_conv_b, state_mlstm_c, state_mlstm_n, state_mlstm_m, cache_sb_k, cache_sb_v)]
    in_maps = []
    for c in range(N_CORES):
        d = dict(wd)
        for i in range(npc):
            b = c * npc + i
            d.update(_seq_inputs(i, x_prompt[b], p_prompt[:, b], None))
        d.update(_seq_inputs(npc, x_sample[c], p_sample[:, c], [a[:, c] for a in sts]))
        in_maps.append(d)
    res = run_bass_kernel_spmd(nc, in_maps, core_ids=list(range(N_CORES)))
    pr = [[] for _ in range(9)]
    sr = [[] for _ in range(9)]
    for c in range(N_CORES):
        r = res.results[c]
        for i in range(npc):
            for k, v in enumerate(_seq_outputs(r, i, L, Tp)):
                pr[k].append(v)
        for k, v in enumerate(_seq_outputs(r, npc, L, Ts)):
            sr[k].append(v)
    outs = []
    for k in range(9):
        a = np.stack(pr[k], 0)
        b = np.stack(sr[k], 0)
        if k >= 1:
            a = np.moveaxis(a, 1, 0)
            b = np.moveaxis(b, 1, 0)
        outs.append((np.ascontiguousarray(a, dtype=np.float32), np.ascontiguousarray(b, dtype=np.float32)))
    y, sh, wkv, conv, cc, nn, mm, sbk, sbv = outs
    return (y[0], y[1], sh[0], sh[1], wkv[0], wkv[1], conv[0], conv[1], cc[0], cc[1], nn[0], nn[1], mm[0], mm[1],
            sbk[0], sbk[1], sbv[0], sbv[1])
```

```python
import contextlib
import math
import numpy as np
import concourse.bass as bass
import concourse.mybir as mybir
from concourse.bass_utils import run_bass_kernel_spmd

F32 = mybir.dt.float32
BF16 = mybir.dt.bfloat16
AF = mybir.ActivationFunctionType
ALU = mybir.AluOpType

ENG = ('pe', 'act', 'dve', 'pool', 'sp')
ROT = 16000
NDSEM = 12


class Sched:
    def __init__(self):
        self.q = {e: [] for e in ENG}
        self.cnt = {e: 0 for e in ENG}
        self.clock = {e: {} for e in ENG}
        self.lastw = {}
        self.readers = {}
        self.dcount = {'sp': 0, 'pool': 0, 'act': 0}
        self.dlast = {}
        self.out_dmas = []
        self.enabled = True
        self.budget = None

    def _collect(self, eng, reads, writes, is_dma):
        deps = []
        for k in reads:
            w = self.lastw.get(k)
            if w is not None:
                deps.append(w)
        for k in writes:
            w = self.lastw.get(k)
            if w is not None and not (eng == 'pe' and w[0] == 'eng' and w[1] == 'pe'):
                deps.append(w)
            for r in self.readers.get(k, ()):
                if not (eng == 'pe' and r[0] == 'eng' and r[1] == 'pe'):
                    deps.append(r)
        ck = self.clock[eng]
        best = {}
        for d in deps:
            src = (d[0], d[1])
            if ck.get(src, 0) >= d[2]:
                continue
            if src not in best or best[src][2] < d[2]:
                best[src] = d
        for src, d in best.items():
            ck[src] = d[2]
        return list(best.values())

    def op(self, eng, fn, reads=(), writes=()):
        if not self.enabled:
            return None
        if self.budget is not None:
            if self.budget <= 0:
                return None
            self.budget -= 1
        waits = self._collect(eng, reads, writes, False)
        self.cnt[eng] += 1
        n = self.cnt[eng]
        me = ('eng', eng, n)
        self.q[eng].append(('op', fn, waits, n))
        for k in reads:
            self.readers.setdefault(k, []).append(me)
        for k in writes:
            self.lastw[k] = me
            self.readers[k] = []
        return me

    def dma(self, eng, fn, reads=(), writes=(), is_out=False):
        if not self.enabled:
            return None
        if self.budget is not None:
            if self.budget <= 0:
                return None
            self.budget -= 1
        waits = self._collect(eng, reads, writes, True)
        i = self.dcount[eng]
        self.dcount[eng] += 1
        slot = (eng, i % NDSEM)
        prev = self.dlast.get(slot, 0)
        val = prev + 16
        ck = self.clock[eng]
        src = ('dma', slot)
        if prev > 0 and ck.get(src, 0) < prev:
            ck[src] = prev
            waits = [w for w in waits if (w[0], w[1]) != src] + [('dma', slot, prev)]
        self.dlast[slot] = val
        me = ('dma', slot, val)
        self.q[eng].append(('dma', fn, waits, slot))
        for k in reads:
            self.readers.setdefault(k, []).append(me)
        for k in writes:
            self.lastw[k] = me
            self.readers[k] = []
        if is_out:
            self.out_dmas.append(me)
        return me

    def emit(self, nc):
        with contextlib.ExitStack() as st:
            esem = {}
            for e in ENG:
                for r in range(self.cnt[e] // ROT + 1):
                    esem[(e, r)] = st.enter_context(nc.semaphore(f"s_{e}_{r}"))
            dsem = {}
            for slot in self.dlast:
                dsem[slot] = st.enter_context(nc.semaphore(f"d_{slot[0]}_{slot[1]}"))
            block = st.enter_context(nc.Block())

            def do_wait(engine, d):
                if d[0] == 'eng':
                    n = d[2]
                    engine.wait_ge(esem[(d[1], (n - 1) // ROT)], (n - 1) % ROT + 1)
                else:
                    engine.wait_ge(dsem[d[1]], d[2])

            def run(e, engine):
                for item in self.q[e]:
                    kind, fn, waits = item[0], item[1], item[2]
                    for d in waits:
                        do_wait(engine, d)
                    inst = fn(engine)
                    if kind == 'op':
                        n = item[3]
                        inst.then_inc(esem[(e, (n - 1) // ROT)], 1)
                    else:
                        inst.then_inc(dsem[item[3]], 16)
                if e == 'sp':
                    for slot, v in self.dlast.items():
                        engine.wait_ge(dsem[slot], v)

            @block.tensor
            def _(eng):
                run('pe', eng)

            @block.scalar
            def _(eng):
                run('act', eng)

            @block.vector
            def _(eng):
                run('dve', eng)

            @block.gpsimd
            def _(eng):
                run('pool', eng)

            @block.sync
            def _(eng):
                run('sp', eng)


D = 1024
KC = 8
NIN = 9864
DPLE = 256
C_A0, C_AZ, C_BQ, C_BK, C_BV, C_BI, C_BF, C_BO, C_BZ = 0, 1664, 2176, 2688, 3200, 3712, 3716, 3720, 4232
C_CQ, C_CK, C_CV, C_CZ, C_G0 = 4744, 5256, 5768, 6280, 6792
DECAY_C = math.exp(-0.5)
LN_C = -0.5 * math.log(128.0)

CS_ID = 0
CS_SU = 128
CS_SL = 256
CS_IU = 384
CS_BO = 512
CS_TRI = 640
CS_ONE = 768
CS_M01 = 896
CS_SEL = 1408
NCST = 1920


def make_consts():
    c = np.zeros((128, NCST), np.float32)
    i = np.arange(128)
    r, cc = i[:, None], i[None, :]
    same = (r // 64) == (cc // 64)
    c[:, CS_ID:CS_ID + 128] = (r == cc)
    c[:, CS_SU:CS_SU + 128] = same & (r < cc)
    c[:, CS_SL:CS_SL + 128] = same & (r > cc)
    c[:, CS_IU:CS_IU + 128] = same & (r <= cc)
    c[:, CS_BO:CS_BO + 128] = same
    c[:, CS_TRI:CS_TRI + 128] = (r >= cc)
    c[:, CS_ONE:CS_ONE + 128] = 1.0
    t = np.arange(512)
    c[:, CS_M01:CS_M01 + 512] = (t % 64 != 0)[None, :]
    for h in range(4):
        c[32 * h, CS_SEL + 128 * h:CS_SEL + 128 * (h + 1)] = 1.0
    return c


def make_att():
    a = np.zeros((128, 2048), np.float32)
    r = np.arange(128)[:, None]
    t = np.arange(512)
    for j in range(4):
        a[:, 512 * j:512 * (j + 1)] = (t[None, :] - r) > 128 * j
    return a


PV_BA, PV_BQK, PV_BCQ, PV_NBCQ, PV_BCK, PV_BCZ, PV_BG = 0, 17, 25, 29, 33, 37, 41
PV_BI, PV_NBF, PV_MU, PV_OMU, PV_W0, PV_A0, PV_KK, PV_KA, PV_RK = 65, 66, 67, 80, 93, 97, 101, 105, 109
PV_CW, PV_CB, PV_LNG, PV_LNB = 113, 145, 153, 161
NPV = 170
X_B, X_MU, X_OMU, X_BZ, X_W0, X_A0, X_KK, X_KA, X_RK = 0, 26, 52, 78, 86, 94, 102, 110, 118
NX = 126


def build(cfg):
    L = cfg['L']
    seqs = cfg['seqs']
    nc = bass.Bass("TRN2", target_bir_lowering=False)

    def din(name, shape, dt=F32):
        return nc.dram_tensor(name, list(shape), dt, kind="ExternalInput").ap()

    def dout(name, shape):
        return nc.dram_tensor(name, list(shape), F32, kind="ExternalOutput").ap()

    def dint(name, shape, dt):
        return nc.dram_tensor(name, list(shape), dt, kind="Internal").ap()

    W = {}
    for nm, shp in [('ln_in_g', [D]), ('ln_in_b', [D]), ('w_in', [L, D, NIN]), ('b_in', [L, NIN]), ('mu_a', [L, 1664]),
                    ('w0_a', [L, 512]), ('w_decay_up', [L, 64, 512]), ('a0_a', [L, 512]), ('w_iclr_up', [L, 64, 512]),
                    ('k_k', [L, 512]), ('k_a', [L, 512]), ('r_k', [L, 512]), ('gn_a_g', [L, 512]), ('gn_a_b', [L, 512]),
                    ('conv_b_w', [L, 4, D]), ('conv_b_b', [L, D]), ('hn_b_g', [L, 512]), ('w_branch', [L, 3, 512, D]),
                    ('w_out', [L, D, D]), ('ln_g', [L, D]), ('ln_b', [L, D]), ('w_ple', [L, DPLE, D]),
                    ('w_ple_gate', [L, D, D]), ('cst', [128, NCST]), ('catt', [128, 2048]), ('pvh', [128, L, NPV]), ('pvg', [128, 16]),
                    ('gnh', [L, 64, 16, 64]), ('hnh', [L, 128, 512]), ('cst3', [64, 512]), ('pvx', [64, L, NX])]:
        W[nm] = din(nm, shp)
    SI, SO, SX = [], [], []
    for i, sq in enumerate(seqs):
        T, P = sq['T'], sq['P']
        d = {'xT': din(f"xT{i}", [D, T]), 'pT': din(f"pT{i}", [L, DPLE, T])}
        if P > 0:
            d['shift'] = din(f"shift{i}", [L, 64, 26])
            d['wkv'] = din(f"wkv{i}", [L, 4, 64, 2, 64])
            d['conv'] = din(f"conv{i}", [L, 128, 8, 3])
            d['ct'] = din(f"ct{i}", [L, 4, 128, 129])
            d['m'] = din(f"m{i}", [L, 128, 1])
            d['ck'] = din(f"ck{i}", [L, 4, 128, P])
            d['cv'] = din(f"cv{i}", [L, P, 512])
        SI.append(d)
        SO.append({'yT': dout(f"yT{i}", [D, T]), 'shift': dout(f"shift_o{i}", [L, 64, 26]),
                   'wkv': dout(f"wkv_o{i}", [L, 4, 64, 2, 64]), 'conv': dout(f"conv_o{i}", [L, 128, 8, 3]),
                   'ct': dout(f"ct_o{i}", [L, 4, 128, 129]), 'm': dout(f"m_o{i}", [L, 128, 1]),
                   'sbk': dout(f"sbk{i}", [L, T, 512]), 'sbv': dout(f"sbv{i}", [L, T, 512])})
        SX.append({'xres': dint(f"xres{i}", [KC, 128, T], F32), 'kTd': dint(f"kTd{i}", [4, 128, P + T], BF16), 'xTd': dint(f"xTd{i}", [KC, 128, T], BF16),
                   'vd': dint(f"vd{i}", [P + T, 512], BF16)})
    wb_in = dint("wb_in", [L, D, NIN], BF16)
    wb_br = dint("wb_br", [L, 3, 512, D], BF16)
    wb_out = dint("wb_out", [L, D, D], BF16)
    wb_pg = dint("wb_pg", [L, D, D], BF16)
    wb_ple = dint("wb_ple", [L, DPLE, D], BF16)

    S = Sched()
    st = contextlib.ExitStack()
    with st:
        def sb(name, shape, dt=F32):
            return st.enter_context(nc.sbuf_tensor(name, list(shape), dt))

        cstF = sb("cstF", [128, 1280])
        cstB = sb("cstB", [128, CS_M01], BF16)
        attB = sb("attB", [128, 4, 512], BF16)
        PV = sb("PV", [128, L, NPV])
        PVG = sb("PVG", [128, 16])
        slabs = [sb(f"slab{i}", [128, 4096], BF16) for i in range(4)]
        xTb = [sb(f"xTb{i}", [128, KC, 512], BF16) for i in range(2)]
        slabZ = sb("slabZ", [128, KC, 128], BF16)
        tmpF = [sb(f"tmpF{i}", [128, 512]) for i in range(9)]
        lnm = sb("lnm", [128, 512])
        lnr = sb("lnr", [128, 512])
        tmpH = [sb(f"tmpH{i}", [128, 512], BF16) for i in range(28)]
        ysT = sb("ysT", [128, 12, 512], BF16)
        p16 = sb("p16", [128, 2, 512], BF16)
        pstg = sb("pstg", [128, 2, 512])
        kTblk = [sb(f"kTblk{i}", [128, 4, 128], BF16) for i in range(2)]
        vblk = [sb(f"vblk{i}", [128, 512], BF16) for i in range(2)]
        brow = sb("brow", [1, 2560], BF16)
        hnG = sb("hnG", [128, 512])
        gnx = sb("gnx", [64, 16, 64])
        upw = sb("upw", [64, 2, 512], BF16)
        wrep = sb("wrep", [128, 2, KC, 128], BF16)
        wcol = sb("wcol", [128, KC, 8], BF16)
        PVX = sb("PVX", [64, L, NX])
        cm2 = sb("cm2", [64, 4, 128], BF16)
        rawA = [sb(f"rawA{i}", [64, 513]) for i in range(2)]
        carX = sb("carX", [64, 26])
        lora = sb("lora", [64, 2, 512], BF16)
        sqh = sb("sqh", [64, 512], BF16)
        egP = [sb(f"egP{i}", [64, 2, 512]) for i in range(2)]
        Hf = [sb(f"Hf{i}", [64, 2, 64]) for i in range(4)]
        Hb = [sb(f"Hb{i}", [64, 2, 64], BF16) for i in range(4)]
        RS = []
        for p in range(2):
            d = {}
            for nm in ('Qa', 'Qb', 'Pa', 'Pb', 'Za', 'Zb', 'Abk', 'Ara', 'Ark', 'ast', 'kst', 'Vst', 'Xb', 'Ub', 'ysa'):
                d[nm] = sb(f"r{nm}{p}", [64, 2, 64], BF16)
            for nm in ('azst', 'e1', 'e2', 'e3', 'htmp'):
                d[nm] = sb(f"r{nm}{p}", [64, 2, 64])
            d['sc'] = sb(f"rsc{p}", [64, 8, 2])
            RS.append(d)
        rawB = [sb(f"rawB{i}", [128, 515]) for i in range(2)]
        carB = sb("carB", [128, 8, 3])
        mbuf = sb("mbuf", [128, 576])
        carM = sb("carM", [128, 2])
        vaug = sb("vaug", [128, 4, 4, 129], BF16)
        tokS = sb("tokS", [128, 4, 5, 4])
        tokS2 = sb("tokS2", [128, 4, 2, 4])
        rmask = sb("rmask", [128, 2])
        csB = sb("csB", [128, 4, 8])
        CTf = [sb(f"CTf{i}", [128, 129]) for i in range(4)]
        CTb = [sb(f"CTb{i}", [128, 129], BF16) for i in range(4)]
        mlt = [sb(f"mlt{i}", [128, 132]) for i in range(4)]
        mlh = [sb(f"mlh{i}", [128, 128], BF16) for i in range(2)]
        mlh2 = [sb(f"mlh2{i}", [128, 512], BF16) for i in range(2)]
        mls = sb("mls", [128, 8, 8])
        psb = [st.enter_context(nc.psum_tensor(f"psb{i}", [128, 512], F32)) for i in range(8)]
        psbH = [psb[i][:, :].bitcast(BF16) for i in range(8)]

        idF = cstF[:, 0:128]
        onesF = cstF[:, 128:256]
        m01 = cstF[:, 256:768]
        idB = cstB[:, CS_ID:CS_ID + 128]
        mIU = cstB[:, CS_IU:CS_IU + 128]
        blkones = cstB[:, CS_BO:CS_BO + 128]
        triB = cstB[:, CS_TRI:CS_TRI + 128]
        onesB = cstB[:, CS_ONE:CS_ONE + 128]

        def MM(out, lhsT, rhs, r, w, start=True, stop=True):
            S.op('pe', lambda e: e.matmul(out, lhsT=lhsT, rhs=rhs, start=start, stop=stop), r, w)

        def TR(out, in_, ident, r, w):
            S.op('pe', lambda e: e.transpose(out, in_, ident), r, w)

        def ACT(out, in_, func, r, w, bias=None, scale=None, accum=None):
            kw = {}
            if bias is not None:
                kw['bias'] = bias
            if scale is not None:
                kw['scale'] = scale
            if accum is not None:
                kw['accum_out'] = accum
            S.op('act', lambda e: e.activation(out=out, in_=in_, func=func, **kw), r, w)

        def TT(eng, out, in0, in1, op, r, w):
            S.op(eng, lambda e: e.tensor_tensor(out=out, in0=in0, in1=in1, op=op), r, w)

        def TS(eng, out, in0, s1, op0, r, w, s2=None, op1=None, accum=None):
            kw = {}
            if op1 is not None:
                kw['op1'] = op1
            if accum is not None:
                kw['accum_out'] = accum
            S.op(eng, lambda e: e.tensor_scalar(out=out, in0=in0, scalar1=s1, scalar2=s2, op0=op0, **kw), r, w)

        def STT(out, in0, scalar, in1, op0, op1, r, w):
            S.op('dve', lambda e: e.scalar_tensor_tensor(out=out, in0=in0, scalar=scalar, in1=in1, op0=op0, op1=op1), r, w)

        def CP(eng, out, in_, r, w):
            if eng == 'act':
                S.op('act', lambda e: e.activation(out=out, in_=in_, func=AF.Copy), r, w)
            else:
                S.op(eng, lambda e: e.tensor_copy(out=out, in_=in_), r, w)

        def MSET(eng, ap, val, w):
            S.op(eng, lambda e: e.memset(ap, val), (), w)

        def SCAN(out, d0, d1, init, op0, op1, r, w):
            S.op('dve', lambda e: e.tensor_tensor_scan(out=out, data0=d0, data1=d1, initial=init, op0=op0, op1=op1), r, w)

        def RECIP(out, in_, r, w):
            S.op('dve', lambda e: e.reciprocal(out=out, in_=in_), r, w)

        def DMA(eng, out, in_, r, w, is_out=False, slow=False):
            if slow:
                S.dma(eng, lambda e: e.dma_start(out=out, in_=in_, allow_slow_non_contiguous=True), r, w, is_out)
            else:
                S.dma(eng, lambda e: e.dma_start(out=out, in_=in_), r, w, is_out)

        RB = [0, 1, 2, 3, 7]
        ring = {1: 0, 2: 0, 4: 0, 't': 0}

        def psq(n=1):
            c = ring[n]
            ring[n] = c + 1
            b = RB[c % 5]
            if n == 1:
                o = (c // 5) % 4
            elif n == 2:
                o = 2 * ((c // 5) % 2)
            else:
                o = 0
            return psb[b][:, o * 128:(o + n) * 128], [('ps', b)]

        def psbank(b):
            return psb[b][:, :], [('ps', b)]

        def pst():
            c = ring['t']
            ring['t'] = c + 1
            b = RB[(c + 2) % 5]
            o = (c // 5) % 4
            return psbH[b][:, o * 256:o * 256 + 128], [('ps', b)]

        sring = {'i': 0}

        def load_slab(src3, kdim, ncols, wkey):
            i = sring['i']
            sring['i'] = (i + 1) % 4
            view = slabs[i][:, 0:kdim * ncols].rearrange("p (k c) -> p k c", k=kdim)
            DMA('sp', view, src3, wkeys.get(wkey, []), [('slab', i)])
            return view, ('slab', i)

        def win_slab(l, c0, n):
            return load_slab(wb_in[l].rearrange("(k p) c -> p k c", p=128)[:, :, c0:c0 + n], KC, n, ('wb_in', l))

        stg = [slabs[i][:, :].bitcast(F32) for i in range(2)]
        kst_ = [('slab', 0), ('slab', 1)]
        DMA('sp', stg[0][:, 0:NCST], W['cst'], [], [kst_[0]])
        DMA('sp', stg[1][:, 0:2048], W['catt'], [], [kst_[1]])
        CP('dve', cstB[:], stg[0][:, 0:CS_M01], [kst_[0]], ['cstB'])
        CP('dve', cstF[:, 0:128], stg[0][:, CS_ID:CS_ID + 128], [kst_[0]], ['cstF'])
        CP('dve', cstF[:, 128:256], stg[0][:, CS_ONE:CS_ONE + 128], [kst_[0]], ['cstF'])
        CP('dve', cstF[:, 256:768], stg[0][:, CS_M01:CS_M01 + 512], [kst_[0]], ['cstF'])
        CP('dve', cstF[:, 768:1280], stg[0][:, CS_SEL:CS_SEL + 512], [kst_[0]], ['cstF'])
        CP('dve', attB[:], stg[1][:, 0:2048].rearrange("p (j c) -> p j c", j=4), [kst_[1]], ['attB'])
        DMA('sp', lnm[0:64, :], W['cst3'], [], ['lnm'])
        CP('dve', cm2[:], lnm[0:64, :].rearrange("p (j c) -> p j c", j=4), ['lnm'], ['cm2'])
        DMA('sp', PVX[:], W['pvx'], [], ['PVX'])
        for b in range(8):
            MSET('dve', psb[b][:], 0.0, [('ps', b)])
        pc = {'i': 0}
        ceng = ('act', 'pool', 'dve')

        def precast(dst2d, src2d, key):
            rows, cols = src2d.shape
            for r0 in range(0, rows, 128):
                for c0 in range(0, cols, 2048):
                    n = min(2048, cols - c0)
                    i = pc['i']
                    pc['i'] += 1
                    sf, kf_ = stg[i % 2], kst_[i % 2]
                    db, kd_ = slabs[2 + i % 2], ('slab', 2 + i % 2)
                    DMA('sp', sf[:, 0:n], src2d[r0:r0 + 128, c0:c0 + n], [], [kf_])
                    CP(ceng[i % 3], db[:, 0:n], sf[:, 0:n], [kf_], [kd_])
                    DMA('sp', dst2d[r0:r0 + 128, c0:c0 + n], db[:, 0:n], [kd_], [(key, pc['i'])])
                    wkeys.setdefault(key, []).append((key, pc['i']))

        wkeys = {}
        for l in range(L):
            precast(wb_in[l], W['w_in'][l], ('wb_in', l))
            for n in range(3):
                precast(wb_br[l, n], W['w_branch'][l, n], ('wb_br', l))
            precast(wb_out[l], W['w_out'][l], ('wb_out', l))
            precast(wb_pg[l], W['w_ple_gate'][l], ('wb_pg', l))
            precast(wb_ple[l], W['w_ple'][l], ('wb_ple', l))
        DMA('sp', PV[:], W['pvh'], [], ['PV'])
        DMA('sp', PVG[:], W['pvg'], [], ['PV'])
        for l in range(L):
            TS('dve', PV[:, l, PV_OMU:PV_OMU + 13], PV[:, l, PV_MU:PV_MU + 13], -1.0, ALU.mult, ['PV'], ['PV'], s2=1.0, op1=ALU.add)
            TS('dve', PV[:, l, PV_NBF:PV_NBF + 1], PV[:, l, PV_NBF:PV_NBF + 1], -1.0, ALU.mult, ['PV'], ['PV'])
            TS('dve', PV[:, l, PV_NBCQ:PV_NBCQ + 4], PV[:, l, PV_BCQ:PV_BCQ + 4], -0.125, ALU.mult, ['PV'], ['PV'])
            TS('dve', PV[:, l, PV_BCQ:PV_BCQ + 4], PV[:, l, PV_BCQ:PV_BCQ + 4], 0.125, ALU.mult, ['PV'], ['PV'])
        MSET('dve', vaug[:], 1.0, ['vaug'])
        MSET('dve', rmask[:], 0.0, ['rmask'])
        MSET('dve', rmask[0:64, 0:1], 1.0, ['rmask'])
        MSET('dve', rmask[64:128, 1:2], 1.0, ['rmask'])
        for l in range(L):
            TS('dve', PVX[:, l, X_OMU:X_OMU + 26], PVX[:, l, X_MU:X_MU + 26], -1.0, ALU.mult, ['PVX'], ['PVX'], s2=1.0, op1=ALU.add)

        def layernorm(TB, gcol, bcol, eps):
            s5, k5 = psbank(4)
            s6, k6 = psbank(5)
            for fc in range(KC):
                ACT(lnr[:, :TB], tmpF[fc][:, :TB], AF.Square, [('tF', fc)], ['lnr'])
                MM(s5[:, :TB], onesF, tmpF[fc][:, :TB], ['cstF', ('tF', fc)], k5, start=(fc == 0), stop=(fc == KC - 1))
                MM(s6[:, :TB], onesF, lnr[:, :TB], ['cstF', 'lnr'], k6, start=(fc == 0), stop=(fc == KC - 1))
            ACT(lnm[:, :TB], s5[:, :TB], AF.Copy, k5, ['lnm'], scale=1.0 / D)
            TT('dve', tmpF[8][:, :TB], lnm[:, :TB], lnm[:, :TB], ALU.mult, ['lnm'], [('tF', 8)])
            STT(tmpF[8][:, :TB], s6[:, :TB], 1.0 / D, tmpF[8][:, :TB], ALU.mult, ALU.subtract, k6 + [('tF', 8)], [('tF', 8)])
            TS('dve', tmpF[8][:, :TB], tmpF[8][:, :TB], 0.0, ALU.max, [('tF', 8)], [('tF', 8)], s2=eps, op1=ALU.add)
            ACT(tmpF[8][:, :TB], tmpF[8][:, :TB], AF.Sqrt, [('tF', 8)], [('tF', 8)])
            RECIP(lnr[:, :TB], tmpF[8][:, :TB], [('tF', 8)], ['lnr'])
            for fc in range(KC):
                TT('dve', tmpF[fc][:, :TB], tmpF[fc][:, :TB], lnm[:, :TB], ALU.subtract, [('tF', fc), 'lnm'], [('tF', fc)])
                TT('dve', tmpF[fc][:, :TB], tmpF[fc][:, :TB], lnr[:, :TB], ALU.mult, [('tF', fc), 'lnr'], [('tF', fc)])
                ACT(tmpF[fc][:, :TB], tmpF[fc][:, :TB], AF.Identity, [('tF', fc), 'PV'], [('tF', fc)],
                    bias=bcol(fc), scale=gcol(fc))

        for si, sq in enumerate(seqs):
            T, P = sq['T'], sq['P']
            TB = min(512, T)
            NBLK = T // TB
            TP = min(128, TB)
            NTL = TB // TP
            NCH = TB // 64
            CPT = TP // 64
            I, O, X = SI[si], SO[si], SX[si]
            KX = ('xres', si)
            KXD = ('xTd', si)

            for j in range(NBLK):
                t0 = j * TB
                for fc in range(KC):
                    DMA('sp', tmpF[fc][:, :TB], I['xT'][fc * 128:(fc + 1) * 128, t0:t0 + TB], [], [('tF', fc)])
                layernorm(TB, lambda fc: PVG[:, fc:fc + 1], lambda fc: PVG[:, 8 + fc:9 + fc], 1e-5)
                for fc in range(KC):
                    DMA('sp', X['xres'][fc, :, t0:t0 + TB], tmpF[fc][:, :TB], [('tF', fc)], [KX])
                    CP('pool', tmpH[fc][:, :TB], tmpF[fc][:, :TB], [('tF', fc)], [('tH', fc)])
                    DMA('sp', X['xTd'][fc, :, t0:t0 + TB], tmpH[fc][:, :TB], [('tH', fc)], [KXD])

            for l in range(L):
                pv = lambda c, n=1: PV[:, l, c:c + n]
                DMA('sp', lnm[0:64, :], W['w_decay_up'][l], [], ['lnm'])
                DMA('sp', lnr[0:64, :], W['w_iclr_up'][l], [], ['lnr'])
                CP('pool', upw[:, 0, :], lnm[0:64, :], ['lnm'], ['upw'])
                CP('pool', upw[:, 1, :], lnr[0:64, :], ['lnr'], ['upw'])
                bl = W['b_in'][l]
                for gi, c0 in enumerate((C_BV, C_BO, C_BZ, C_CK, C_CV)):
                    DMA('sp', lnr[0:1, :], bl[c0:c0 + 512].rearrange("(o c) -> o c", o=1), [], ['lnr'])
                    CP('pool', brow[0:1, gi * 512:(gi + 1) * 512], lnr[0:1, :], ['lnr'], ['brow'])
                DMA('sp', hnG[:], W['hnh'][l], [], ['hnG'])
                DMA('sp', gnx[:], W['gnh'][l], [], ['gnx'])
                DMA('sp', wcol[:], wb_in[l].rearrange("(k p) c -> p k c", p=128)[:, :, C_BI:C_BI + 8], ['wb_in'], ['wcol'], slow=True)
                for g in range(2):
                    for h in range(4):
                        CP('pool', wrep[:, g, :, 32 * h:32 * h + 32], wcol[:, :, 4 * g + h:4 * g + h + 1].broadcast_to([128, KC, 32]),
                           ['wcol'], ['wrep'])
                if P > 0:
                    DMA('sp', carX[:], I['shift'][l], [], ['carX'])
                    for p in range(4):
                        DMA('sp', Hf[p][:, :, :], I['wkv'][l, p], [], [('Hf', p)])
                        DMA('sp', CTf[p][:], I['ct'][l, p], [], [('CTf', p)])
                    DMA('sp', carB[:], I['conv'][l], [], ['carB'])
                    DMA('sp', mbuf[:, 63:64], I['m'][l], [], ['mbuf'])
                    CP('dve', carM[:, 1:2], mbuf[:, 63:64], ['mbuf'], ['carM'])
                    MSET('dve', carM[:, 0:1], 0.0, ['carM'])
                    ci = 0
                    for p in range(4):
                        for c0 in range(0, P, 512):
                            n = min(512, P - c0)
                            DMA('sp', tmpF[ci % 8][:, 0:n], I['ck'][l, p][:, c0:c0 + n], [], [('tF', ci % 8)])
                            CP(('act', 'pool', 'dve')[ci % 3], tmpH[ci % 8][:, 0:n], tmpF[ci % 8][:, 0:n], [('tF', ci % 8)], [('tH', ci % 8)])
                            DMA('sp', X['kTd'][p, :, c0:c0 + n], tmpH[ci % 8][:, 0:n], [('tH', ci % 8)], [('kTd', si)])
                            ci += 1
                    for r0 in range(0, P, 128):
                        n = min(128, P - r0)
                        DMA('sp', tmpF[ci % 8][0:n, :], I['cv'][l, r0:r0 + n, :], [], [('tF', ci % 8)])
                        CP(('act', 'pool', 'dve')[ci % 3], tmpH[ci % 8][0:n, :], tmpF[ci % 8][0:n, :], [('tF', ci % 8)], [('tH', ci % 8)])
                        DMA('sp', X['vd'][r0:r0 + n, :], tmpH[ci % 8][0:n, :], [('tH', ci % 8)], [('vd', si)])
                        ci += 1
                else:
                    MSET('pool', carX[:], 0.0, ['carX'])
                    MSET('pool', carB[:], 0.0, ['carB'])
                    MSET('pool', mbuf[:, 0:64], 0.0, ['mbuf'])
                    MSET('pool', carM[:], 0.0, ['carM'])
                    for p in range(4):
                        MSET('pool', Hf[p][:, :, :], 0.0, [('Hf', p)])
                        MSET('pool', CTf[p][:], 0.0, [('CTf', p)])
                for p in range(4):
                    CP('pool', Hb[p][:, :, :], Hf[p][:, :, :], [('Hf', p)], [('Hb', p)])
                    CP('pool', CTb[p][:], CTf[p][:], [('CTf', p)], [('CTb', p)])

                for j in range(NBLK):
                    t0 = j * TB
                    S.enabled = True
                    S.budget = cfg.get('cut')
                    xb = xTb[(j + l) % 2]
                    KXB = ('xTb', (j + l) % 2)
                    DMA('sp', xb[:, :, :TB], X['xTd'][:, :, t0:t0 + TB].rearrange("k p t -> p k t"), [KXD], [KXB])
                    DMA('sp', pstg[:, :, :TB], I['pT'][l, :, t0:t0 + TB].rearrange("(k p) t -> p k t", p=128), [], ['pstg'])
                    CP('pool', p16[:, :, :TB], pstg[:, :, :TB], ['pstg'], ['p16'])

                    def proj_fm(slab, skey, c0, M, out_ap, okeys):
                        for kc in range(KC):
                            MM(out_ap, slab[:, kc, c0:c0 + M], xb[:, kc, :TB], [skey, KXB], okeys, start=(kc == 0), stop=(kc == KC - 1))

                    def proj_tm(slab, skey, tt, out_ap, okeys, boff):
                        for kc in range(KC):
                            MM(out_ap, xb[:, kc, tt * TP:(tt + 1) * TP], slab[:, kc, :], [skey, KXB], okeys, start=(kc == 0), stop=False)
                        MM(out_ap, onesB[0:1, 0:TP], brow[0:1, boff:boff + 512], ['cstB', 'brow'], okeys, start=False, stop=True)

                    S.enabled = 'A' in cfg.get('ph', 'ABCG')
                    pvx = lambda c, n=1: PVX[:, l, c:c + n]
                    H64 = slice(0, 64)
                    sR, kR = win_slab(l, 0, 512)
                    sK, kK = win_slab(l, 512, 512)
                    sV, kV = win_slab(l, 1024, 512)
                    sW, kW = win_slab(l, 1536, 512)
                    sZ, kZ = None, None

                    def fmx(slab, skey, crel, xi, dst, dkey):
                        rb = rawA[xi % 2]
                        rk_ = ('rawA', xi % 2)
                        ps, pk = psq(4)
                        proj_fm(slab, skey, crel, 64, ps[H64, :TB], pk)
                        ACT(rb[:, 1:1 + TB], ps[H64, :TB], AF.Identity, pk + ['PVX'], [rk_], bias=pvx(X_B + xi))
                        CP('pool', rb[:, 0:1], carX[:, xi:xi + 1], ['carX'], [rk_])
                        TS('dve', dst[H64, :TB], rb[:, 0:TB], pvx(X_MU + xi), ALU.mult, [rk_, 'PVX'], [dkey])
                        STT(dst[H64, :TB], rb[:, 1:1 + TB], pvx(X_OMU + xi), dst[H64, :TB], ALU.mult, ALU.add, [rk_, 'PVX', dkey], [dkey])
                        CP('pool', carX[:, xi:xi + 1], rb[:, TB:TB + 1], [rk_], ['carX'])

                    fmx(sW, kW, 0, 24, tmpF[0], ('tF', 0))
                    ACT(lora[:, 0, :TB], tmpF[0][H64, :TB], AF.Tanh, [('tF', 0)], ['lora'])
                    fmx(sW, kW, 64, 25, tmpF[1], ('tF', 1))
                    CP('dve', lora[:, 1, :TB], tmpF[1][H64, :TB], [('tF', 1)], ['lora'])
                    for hg in range(2):
                        for hi in range(4):
                            h = hg * 4 + hi
                            pp, hh = hi // 2, hi % 2
                            xr, xk, xv = tmpF[0], tmpF[1], tmpF[2]
                            kr_, kk_, kv_ = ('tF', 0), ('tF', 1), ('tF', 2)
                            t0_, t1_, t2_, t3_, t4_, t5_ = tmpF[3], tmpF[4], tmpF[5], tmpF[6], tmpF[7], tmpF[8]
                            k0_, k1_, k2_, k3_, k4_, k5_ = [('tF', i) for i in range(3, 9)]
                            a16, b16, k16, r16, v16, rk16, az16 = [tmpH[q * 4 + hi] for q in range(7)]
                            ka16, kb16, kk16, kr16, kv16, krk16, kaz16 = [('tH', q * 4 + hi) for q in range(7)]
                            fmx(sR, kR, h * 64, h, xr, kr_)
                            fmx(sK, kK, h * 64, 8 + h, xk, kk_)
                            fmx(sV, kV, h * 64, 16 + h, xv, kv_)
                            ps, pk = psq(4)
                            if h < 6:
                                proj_fm(sW, kW, 128 + h * 64, 64, ps[H64, :TB], pk)
                            else:
                                if sZ is None:
                                    sZ, kZ = slabZ, 'slabZ'
                                    DMA('sp', slabZ[:, :, :], wb_in[l].rearrange("(k p) c -> p k c", p=128)[:, :, 2048:2176], wkeys.get(('wb_in', l), []), ['slabZ'])
                                proj_fm(sZ, kZ, (h - 6) * 64, 64, ps[H64, :TB], pk)
                            ACT(az16[H64, :TB], ps[H64, :TB], AF.Silu, pk + ['PVX'], [kaz16], bias=pvx(X_BZ + h))
                            ps, pk = psq(4)
                            MM(ps[H64, :TB], upw[:, 0, h * 64:(h + 1) * 64], lora[:, 0, :TB], ['upw', 'lora'], pk)
                            ACT(t0_[H64, :TB], ps[H64, :TB], AF.Sigmoid, pk + ['PVX'], [k0_], bias=pvx(X_W0 + h))
                            ps, pk = psq(4)
                            MM(ps[H64, :TB], upw[:, 1, h * 64:(h + 1) * 64], lora[:, 1, :TB], ['upw', 'lora'], pk)
                            ACT(t1_[H64, :TB], ps[H64, :TB], AF.Sigmoid, pk + ['PVX'], [k1_], bias=pvx(X_A0 + h))
                            SCAN(t2_[H64, :TB], m01[H64, :TB], t0_[H64, :TB], 0.0, ALU.mult, ALU.add, ['cstF', k0_], [k2_])
                            TT('dve', t3_[H64, :TB], t2_[H64, :TB], t0_[H64, :TB], ALU.subtract, [k2_, k0_], [k3_])
                            eg = egP[pp][:, hh, :]
                            keg = ('eg', pp)
                            ACT(eg[:, :TB], t2_[H64, :TB], AF.Exp, [k2_], [keg], scale=-DECAY_C)
                            ACT(t4_[H64, :TB], t2_[H64, :TB], AF.Exp, [k2_], [k4_], scale=DECAY_C)
                            ACT(t3_[H64, :TB], t3_[H64, :TB], AF.Exp, [k3_], [k3_], scale=-DECAY_C)
                            TS('dve', t5_[H64, :TB], xk[H64, :TB], pvx(X_KK + h), ALU.mult, [kk_, 'PVX'], [k5_])
                            ACT(sqh[:, :TB], t5_[H64, :TB], AF.Square, [k5_], ['sqh'])
                            ps, pk = psq(4)
                            MM(ps[H64, :TB], onesB[H64, 0:64], sqh[:, :TB], ['cstB', 'sqh'], pk)
                            ACT(t0_[H64, :TB], ps[H64, :TB], AF.Sqrt, pk, [k0_])
                            TS('dve', t0_[H64, :TB], t0_[H64, :TB], 1e-12, ALU.max, [k0_], [k0_])
                            RECIP(t0_[H64, :TB], t0_[H64, :TB], [k0_], [k0_])
                            TT('dve', t5_[H64, :TB], t5_[H64, :TB], t0_[H64, :TB], ALU.mult, [k5_, k0_], [k5_])
                            TS('dve', t2_[H64, :TB], t1_[H64, :TB], -1.0, ALU.add, [k1_, 'PVX'], [k2_], s2=pvx(X_KA + h), op1=ALU.mult)
                            STT(t2_[H64, :TB], t2_[H64, :TB], 1.0, xk[H64, :TB], ALU.add, ALU.mult, [k2_, kk_], [k2_])
                            TT('dve', b16[H64, :TB], t5_[H64, :TB], t3_[H64, :TB], ALU.mult, [k5_, k3_], [kb16])
                            TT('dve', t0_[H64, :TB], t1_[H64, :TB], t5_[H64, :TB], ALU.mult, [k1_, k5_], [k0_])
                            STT(a16[H64, :TB], t0_[H64, :TB], -1.0, t4_[H64, :TB], ALU.mult, ALU.mult, [k0_, k4_], [ka16])
                            TT('pool', k16[H64, :TB], t2_[H64, :TB], t4_[H64, :TB], ALU.mult, [k2_, k4_], [kk16])
                            TT('pool', r16[H64, :TB], xr[H64, :TB], eg[:, :TB], ALU.mult, [kr_, keg], [kr16])
                            CP('pool', v16[H64, :TB], xv[H64, :TB], [kv_], [kv16])
                            TT('dve', t0_[H64, :TB], xr[H64, :TB], t2_[H64, :TB], ALU.mult, [kr_, k2_], [k0_])
                            TS('dve', rk16[H64, :TB], t0_[H64, :TB], pvx(X_RK + h), ALU.mult, [k0_, 'PVX'], [krk16])

                        TH = lambda q, hi: tmpH[q * 4 + hi]
                        KH = lambda q, hi: ('tH', q * 4 + hi)
                        mSU2, mSL2, mIU2, mID2 = [cm2[:, i, :] for i in range(4)]
                        f2 = lambda t: t[:, :, :].rearrange("p a b -> p (a b)")
                        for c in range(NCH):
                            cs = slice(c * 64, (c + 1) * 64)
                            for pp in range(2):
                                R_ = RS[pp]
                                rk = lambda nm, pp=pp: ('rs', pp, nm)

                                def score(ql, qr, mask, dst):
                                    ps, pk = psq(1)
                                    for hh in range(2):
                                        hi = pp * 2 + hh
                                        MM(ps[H64, hh * 64:(hh + 1) * 64], TH(ql, hi)[H64, cs], TH(qr, hi)[H64, cs], [KH(ql, hi), KH(qr, hi)], pk)
                                    TT('dve', f2(R_[dst]), ps[H64, :], mask, ALU.mult, pk + ['cm2'], [rk(dst)])

                                score(0, 1, mSU2, 'Qa')
                                score(1, 0, mSL2, 'Pa')
                                score(2, 1, mSU2, 'Abk')
                                score(0, 3, mIU2, 'Ara')
                                score(2, 3, mIU2, 'Ark')
                                TT('pool', f2(R_['Za']), f2(R_['Qa']), mID2, ALU.add, [rk('Qa'), 'cm2'], [rk('Za')])
                                for q, dst in ((4, 'Vst'), (0, 'ast'), (2, 'kst'), (6, 'azst')):
                                    pt, ptk = pst()
                                    for hh in range(2):
                                        hi = pp * 2 + hh
                                        TR(pt[H64, hh * 64:(hh + 1) * 64], TH(q, hi)[H64, cs], idB[H64, 0:64], [KH(q, hi), 'cstB'], ptk)
                                    CP('act', f2(R_[dst]), pt[H64, :], ptk, [rk(dst)])
                                ps, pk = psq(1)
                                for hh in range(2):
                                    hi = pp * 2 + hh
                                    MM(ps[H64, hh:hh + 1], TH(5, hi)[H64, cs], onesB[H64, 0:1], [KH(5, hi), 'cstB'], pk)
                                CP('act', R_['sc'][:, 0, :], ps[H64, 0:2], pk, [rk('sc')])
                            qc, pc, zc = 'Qa', 'Pa', 'Za'
                            flip = {'Qa': 'Qb', 'Qb': 'Qa', 'Pa': 'Pb', 'Pb': 'Pa', 'Za': 'Zb', 'Zb': 'Za'}
                            for lev in range(1, 7):
                                qn, pn, zn = flip[qc], flip[pc], flip[zc]
                                for pp in range(2):
                                    R_ = RS[pp]
                                    rk = lambda nm, pp=pp: ('rs', pp, nm)

                                    def mm2(lk, rk2):
                                        ps, pk = psq(1)
                                        for hh in range(2):
                                            MM(ps[H64, hh * 64:(hh + 1) * 64], R_[lk][:, hh, :], R_[rk2][:, hh, :], [rk(lk), rk(rk2)], pk)
                                        return ps, pk

                                    if lev >= 2:
                                        ps, pk = mm2(pc, zc)
                                        TT('dve', f2(R_[zn]), ps[H64, :], f2(R_[zc]), ALU.add, pk + [rk(zc)], [rk(zn)])
                                    if lev <= 5:
                                        ps, pk = mm2(qc, pc)
                                        CP('act', f2(R_[pn]), ps[H64, :], pk, [rk(pn)])
                                    if lev <= 4:
                                        ps, pk = mm2(pc, qc)
                                        CP('act', f2(R_[qn]), ps[H64, :], pk, [rk(qn)])
                                if lev >= 2:
                                    zc = zn
                                if lev <= 5:
                                    pc = pn
                                if lev <= 4:
                                    qc = qn
                            Zf = zc
                            for pp in range(2):
                                R_ = RS[pp]
                                rk = lambda nm, pp=pp: ('rs', pp, nm)
                                gp = hg * 2 + pp
                                ps, pk = psq(1)
                                for hh in range(2):
                                    hi = pp * 2 + hh
                                    o_ = ps[H64, hh * 64:(hh + 1) * 64]
                                    MM(o_, TH(1, hi)[H64, cs], Hb[gp][:, hh, :], [KH(1, hi), ('Hb', gp)], pk, start=True, stop=False)
                                    MM(o_, R_['Abk'][:, hh, :], R_['Vst'][:, hh, :], [rk('Abk'), rk('Vst')], pk, start=False, stop=True)
                                CP('act', f2(R_['Xb']), ps[H64, :], pk, [rk('Xb')])
                            for pp in range(2):
                                R_ = RS[pp]
                                rk = lambda nm, pp=pp: ('rs', pp, nm)
                                ps, pk = psq(1)
                                for hh in range(2):
                                    MM(ps[H64, hh * 64:(hh + 1) * 64], R_[Zf][:, hh, :], R_['Xb'][:, hh, :], [rk(Zf), rk('Xb')], pk)
                                CP('act', f2(R_['Ub']), ps[H64, :], pk, [rk('Ub')])
                            for pp in range(2):
                                R_ = RS[pp]
                                rk = lambda nm, pp=pp: ('rs', pp, nm)
                                gp = hg * 2 + pp
                                keg = ('eg', pp)
                                psy, pky = psq(1)
                                psh, pkh = psq(1)
                                for hh in range(2):
                                    hi = pp * 2 + hh
                                    o_ = psy[H64, hh * 64:(hh + 1) * 64]
                                    MM(o_, TH(3, hi)[H64, cs], Hb[gp][:, hh, :], [KH(3, hi), ('Hb', gp)], pky, start=True, stop=False)
                                    MM(o_, R_['Ara'][:, hh, :], R_['Ub'][:, hh, :], [rk('Ara'), rk('Ub')], pky, start=False, stop=False)
                                    MM(o_, R_['Ark'][:, hh, :], R_['Vst'][:, hh, :], [rk('Ark'), rk('Vst')], pky, start=False, stop=True)
                                for hh in range(2):
                                    o_ = psh[H64, hh * 64:(hh + 1) * 64]
                                    MM(o_, R_['ast'][:, hh, :], R_['Ub'][:, hh, :], [rk('ast'), rk('Ub')], pkh, start=True, stop=False)
                                    MM(o_, R_['kst'][:, hh, :], R_['Vst'][:, hh, :], [rk('kst'), rk('Vst')], pkh, start=False, stop=True)
                                TT('dve', f2(R_['htmp']), psh[H64, :], f2(Hf[gp]), ALU.add, pkh + [('Hf', gp)], [rk('htmp')])
                                gam = egP[pp][:, :, c * 64 + 63:c * 64 + 64].broadcast_to([64, 2, 64])
                                TT('pool', Hf[gp][:, :, :], R_['htmp'][:, :, :], gam, ALU.mult, [rk('htmp'), keg], [('Hf', gp)])
                                TT('dve', Hb[gp][:, :, :], R_['htmp'][:, :, :], gam, ALU.mult, [rk('htmp'), keg], [('Hb', gp)])
                                e1, e2, e3, scr = R_['e1'], R_['e2'], R_['e3'], R_['sc']
                                y3 = psy[H64, :].rearrange("p (a b) -> p a b", a=2)
                                bc2 = lambda i: scr[:, i, :].unsqueeze(2).broadcast_to([64, 2, 64])
                                S.op('dve', lambda e, o=scr[:, 1, :], i=y3: e.reduce_sum(out=o, in_=i, axis=mybir.AxisListType.X), pky, [rk('sc')])
                                STT(e2[:, :, :], bc2(1), -1.0 / 64, y3, ALU.mult, ALU.add, pky + [rk('sc')], [rk('e2')])
                                ACT(f2(e1), f2(e2), AF.Square, [rk('e2')], [rk('e1')])
                                S.op('dve', lambda e, o=scr[:, 2, :], i=e1[:, :, :]: e.reduce_sum(out=o, in_=i, axis=mybir.AxisListType.X), [rk('e1')], [rk('sc')])
                                TS('dve', scr[:, 3, :], scr[:, 2, :], 1.0 / 64, ALU.mult, [rk('sc')], [rk('sc')], s2=64e-5, op1=ALU.add)
                                ACT(scr[:, 3, :], scr[:, 3, :], AF.Sqrt, [rk('sc')], [rk('sc')])
                                RECIP(scr[:, 4, :], scr[:, 3, :], [rk('sc')], [rk('sc')])
                                TT('dve', e3[:, :, :], e2[:, :, :], bc2(4), ALU.mult, [rk('e2'), rk('sc')], [rk('e3')])
                                TT('dve', e3[:, :, :], e3[:, :, :], gnx[:, 2 * gp:2 * gp + 2, :], ALU.mult, [rk('e3'), 'gnx'], [rk('e3')])
                                TT('dve', e3[:, :, :], e3[:, :, :], gnx[:, 8 + 2 * gp:8 + 2 * gp + 2, :], ALU.add, [rk('e3'), 'gnx'], [rk('e3')])
                                TT('pool', e1[:, :, :], R_['Vst'][:, :, :], bc2(0), ALU.mult, [rk('Vst'), rk('sc')], [rk('e1')])
                                TT('dve', e3[:, :, :], e3[:, :, :], e1[:, :, :], ALU.add, [rk('e3'), rk('e1')], [rk('e3')])
                                TT('dve', R_['ysa'][:, :, :], e3[:, :, :], R_['azst'][:, :, :], ALU.mult, [rk('e3'), rk('azst')], [rk('ysa')])
                                pt, ptk = pst()
                                for hh in range(2):
                                    TR(pt[hh * 64:(hh + 1) * 64, 0:64], R_['ysa'][:, hh, :], idB[H64, 0:64], [rk('ysa'), 'cstB'], ptk)
                                CP('act', ysT[:, gp, c * 64:c * 64 + 64], pt[:, 0:64], ptk, [('ysT', gp)])

                    S.enabled = 'B' in cfg.get('ph', 'ABCG')
                    sBQ, kBQ = win_slab(l, C_BQ, 512)
                    sBK, kBK = win_slab(l, C_BK, 512)
                    qT = [tmpH[i] for i in range(4)]
                    kTm = [tmpH[4 + i] for i in range(4)]
                    oz = [tmpH[8 + i] for i in range(4)]
                    ktg = [tmpH[12 + i] for i in range(8)]
                    ysb = [tmpH[20 + i] for i in range(4)]
                    for fc in range(8):
                        slab, skey = (sBQ, kBQ) if fc < 4 else (sBK, kBK)
                        rb = rawB[fc % 2]
                        rk_ = ('rawB', fc % 2)
                        ps, pk = psq(4)
                        proj_fm(slab, skey, (fc % 4) * 128, 128, ps[:, :TB], pk)
                        ACT(rb[:, 3:3 + TB], ps[:, :TB], AF.Identity, pk + ['PV'], [rk_], bias=pv(PV_BQK + fc))
                        CP('pool', rb[:, 0:3], carB[:, fc, :], ['carB'], [rk_])
                        tq = tmpF[fc % 2]
                        tk_ = ('tF', fc % 2)
                        ACT(tq[:, :TB], rb[:, 3:3 + TB], AF.Identity, [rk_, 'PV'], [tk_], scale=pv(PV_CW + 24 + fc), bias=pv(PV_CB + fc))
                        for jj in range(3):
                            STT(tq[:, :TB], rb[:, jj:jj + TB], pv(PV_CW + 8 * jj + fc), tq[:, :TB], ALU.mult, ALU.add, [rk_, 'PV', tk_], [tk_])
                        CP('pool', carB[:, fc, :], rb[:, TB:TB + 3], [rk_], ['carB'])
                        dst = qT[fc] if fc < 4 else kTm[fc - 4]
                        ACT(dst[:, :TB], tq[:, :TB], AF.Silu, [tk_], [('tH', fc)])
                    iv, sp_, fn_, bn_, x_, g_, t1_, t3_ = [tmpF[i] for i in range(8)]
                    K = [('tF', i) for i in range(8)]
                    ps, pk = psq(4)
                    for kc in range(KC):
                        MM(ps[:, :TB], wrep[:, 0, kc, :], xb[:, kc, :TB], ['wrep', KXB], pk, start=(kc == 0), stop=(kc == KC - 1))
                    ACT(iv[:, :TB], ps[:, :TB], AF.Identity, pk + ['PV'], [K[0]], bias=pv(PV_BI))
                    ps, pk = psq(4)
                    for kc in range(KC):
                        MM(ps[:, :TB], wrep[:, 1, kc, :], xb[:, kc, :TB], ['wrep', KXB], pk, start=(kc == 0), stop=(kc == KC - 1))
                    ACT(sp_[:, :TB], ps[:, :TB], AF.Exp, pk + ['PV'], [K[1]], bias=pv(PV_NBF), scale=-1.0)
                    ACT(sp_[:, :TB], sp_[:, :TB], AF.Ln, [K[1]], [K[1]], bias=1.0)
                    SCAN(fn_[:, :TB], sp_[:, :TB], sp_[:, :TB], carM[:, 0:1], ALU.add, ALU.bypass, [K[1], 'carM'], [K[2]])
                    SCAN(bn_[:, :TB], m01[:, :TB], sp_[:, :TB], 0.0, ALU.mult, ALU.add, ['cstF', K[1]], [K[3]])
                    TT('dve', x_[:, :TB], iv[:, :TB], fn_[:, :TB], ALU.add, [K[0], K[2]], [K[4]])
                    SCAN(g_[:, :TB], x_[:, :TB], x_[:, :TB], carM[:, 1:2], ALU.max, ALU.bypass, [K[4], 'carM'], [K[5]])
                    TT('dve', mbuf[:, 64:64 + TB], g_[:, :TB], fn_[:, :TB], ALU.subtract, [K[5], K[2]], ['mbuf'])
                    CP('pool', carM[:, 0:1], fn_[:, TB - 1:TB], [K[2]], ['carM'])
                    CP('pool', carM[:, 1:2], g_[:, TB - 1:TB], [K[5]], ['carM'])
                    mcur = mbuf[:, 64:64 + TB]
                    v3 = lambda ap: ap.rearrange("p (c t) -> p c t", t=64)
                    bc = lambda ap: v3(ap)[:, :, 63:64].broadcast_to([128, NCH, 64])
                    Ra, Rsc, Rcl, Rg, RgL = x_, g_, fn_, iv, t3_
                    STT(t1_[:, :TB], bn_[:, :TB], -1.0, mcur, ALU.mult, ALU.subtract, [K[3], 'mbuf'], [K[6]])
                    ACT(Ra[:, :TB], t1_[:, :TB], AF.Exp, [K[6]], [K[4]])
                    TT('dve', v3(t1_[:, :TB]), v3(t1_[:, :TB]), bc(mbuf[:, 0:TB]), ALU.add, [K[6], 'mbuf'], [K[6]])
                    ACT(Rsc[:, :TB], t1_[:, :TB], AF.Exp, [K[6]], [K[5]])
                    ACT(Rcl[:, :TB], mcur, AF.Exp, ['mbuf'], [K[2]], scale=-1.0)
                    TT('dve', t3_[:, :TB], iv[:, :TB], bn_[:, :TB], ALU.add, [K[0], K[3]], [K[7]])
                    ACT(Rg[:, :TB], t3_[:, :TB], AF.Exp, [K[7]], [K[0]], bias=LN_C)
                    TT('dve', v3(t3_[:, :TB]), v3(t3_[:, :TB]), bc(bn_[:, :TB]), ALU.subtract, [K[7], K[3]], [K[7]])
                    TT('dve', v3(t3_[:, :TB]), v3(t3_[:, :TB]), bc(mcur), ALU.subtract, [K[7], 'mbuf'], [K[7]])
                    ACT(RgL[:, :TB], t3_[:, :TB], AF.Exp, [K[7]], [K[7]], bias=LN_C)
                    RQ = [(Ra, K[4]), (Rsc, K[5]), (Rcl, K[2]), (Rg, K[0]), (RgL, K[7])]
                    for tt in range(NTL):
                        ps, pk = psq(4)
                        for qi in range(4):
                            MM(ps[0:TP, qi * 128:(qi + 1) * 128], RQ[qi][0][:, tt * TP:(tt + 1) * TP], idF, [RQ[qi][1], 'cstF'], pk)
                        CP('dve', tokS[0:TP, tt, 0:4, :], ps[0:TP, :].rearrange("p (q h r) -> p q h r", q=4, h=4)[:, :, :, 0], pk, [('tokS', tt)])
                        ps, pk = psq(1)
                        MM(ps[0:TP, :], RQ[4][0][:, tt * TP:(tt + 1) * TP], idF, [RQ[4][1], 'cstF'], pk)
                        CP('dve', tokS[0:TP, tt, 4, :], ps[0:TP, :].rearrange("p (h r) -> p h r", h=4)[:, :, 0], pk, [('tokS', tt)])
                        for jj in range(CPT):
                            TS('dve', tokS2[0:TP, tt, jj, :], tokS[0:TP, tt, 4, :], rmask[0:TP, jj:jj + 1], ALU.mult, [('tokS', tt), 'rmask'], [('tokS2', tt)])
                    ps, pk = psq(1)
                    for h in range(4):
                        MM(ps[:, h * NCH:(h + 1) * NCH], cstF[:, 768 + 128 * h:768 + 128 * (h + 1)],
                           v3(Rsc[:, :TB])[:, :, 63], ['cstF', K[5]], pk)
                    CP('dve', csB[:, :, 0:NCH], ps[:, 0:4 * NCH].rearrange("p (h c) -> p h c", h=4), pk, ['csB'])
                    CP('pool', mbuf[:, 63:64], mbuf[:, 63 + TB:64 + TB], ['mbuf'], ['mbuf'])
                    sV2, kV2 = win_slab(l, C_BV, 512)
                    sO, kO = win_slab(l, C_BO, 512)
                    for tt in range(NTL):
                        ps, pk = psq(4)
                        proj_tm(sV2, kV2, tt, ps[0:TP, :], pk, 0)
                        CP('act', vaug[0:TP, tt, :, 0:128], ps[0:TP, :].rearrange("p (h d) -> p h d", h=4), pk, [('vaug', tt)])
                    for tt in range(NTL):
                        ps, pk = psq(4)
                        proj_tm(sO, kO, tt, ps[0:TP, :], pk, 512)
                        ACT(oz[tt][0:TP, :], ps[0:TP, :], AF.Sigmoid, pk, [('tH', 8 + tt)])
                    sZ2, kZ2 = win_slab(l, C_BZ, 512)
                    for tt in range(NTL):
                        ps, pk = psq(4)
                        proj_tm(sZ2, kZ2, tt, ps[0:TP, :], pk, 1024)
                        ACT(tmpF[8][0:TP, :], ps[0:TP, :], AF.Silu, pk, [('tF', 8)])
                        TT('dve', oz[tt][0:TP, :], oz[tt][0:TP, :], tmpF[8][0:TP, :], ALU.mult, [('tH', 8 + tt), ('tF', 8)], [('tH', 8 + tt)])
                    for tt in range(NTL):
                        for h in range(4):
                            pt, ptk = pst()
                            TR(pt[0:TP, :], kTm[h][:, tt * TP:(tt + 1) * TP], idB, [('tH', 4 + h), 'cstB'], ptk)
                            for jj in range(CPT):
                                TS('dve', ktg[2 * tt + jj][0:TP, h * 128:(h + 1) * 128], pt[0:TP, :], tokS2[0:TP, tt, jj, h:h + 1], ALU.mult,
                                   ptk + [('tokS2', tt)], [('tH', 12 + 2 * tt + jj)])
                    for tt in range(NTL):
                        tsl = slice(tt * TP, (tt + 1) * TP)
                        for h in range(4):
                            hsl = slice(h * 128, (h + 1) * 128)
                            wt_ = mlh[h % 2]
                            kwt = ('mlh', h % 2)
                            ps, pk = psq(1)
                            MM(ps[0:TP, 0:TP], kTm[h][:, tsl], qT[h][:, tsl], [('tH', 4 + h), ('tH', h)], pk)
                            STT(wt_[0:TP, 0:TP], ps[0:TP, 0:TP], tokS[0:TP, tt, 3, h:h + 1], mIU[0:TP, 0:TP], ALU.mult, ALU.mult,
                                pk + [('tokS', tt), 'cstB'], [kwt])
                            psn, pkn = psq(2)
                            MM(psn[0:TP, 0:129], wt_[0:TP, 0:TP], vaug[0:TP, tt, h, :], [kwt, ('vaug', tt), 'vaug'], pkn)
                            psi, pki = psq(2)
                            for jj in range(CPT):
                                c = tt * CPT + jj
                                js = slice(jj * 64, jj * 64 + 64)
                                MM(psi[js, 0:129], qT[h][:, tt * TP + jj * 64:tt * TP + jj * 64 + 64], CTb[h][:], [('tH', h), ('CTb', h)], pki)
                                pss, pks = psq(2)
                                MM(pss[:, 0:129], ktg[2 * tt + jj][0:TP, hsl], vaug[0:TP, tt, h, :], [('tH', 12 + 2 * tt + jj), ('vaug', tt), 'vaug'], pks)
                                STT(CTf[h][:], CTf[h][:], csB[:, h, c:c + 1], pss[:, 0:129], ALU.mult, ALU.add, [('CTf', h), 'csB'] + pks, [('CTf', h)])
                                CP('pool', CTb[h][:], CTf[h][:], [('CTf', h)], [('CTb', h)])
                            m1, m2 = mlt[(2 * h) % 4], mlt[(2 * h + 1) % 4]
                            km1, km2 = ('mlt', (2 * h) % 4), ('mlt', (2 * h + 1) % 4)
                            sc_ = mls[:, h % 8, :] if False else mls[:, (tt * 4 + h) % 8, :]
                            ksc = ('mls', (tt * 4 + h) % 8)
                            TS('dve', m1[0:TP, 0:129], psi[0:TP, 0:129], tokS[0:TP, tt, 1, h:h + 1], ALU.mult, pki + [('tokS', tt)], [km1])
                            STT(m2[0:TP, 0:129], psn[0:TP, 0:129], tokS[0:TP, tt, 0, h:h + 1], m1[0:TP, 0:129], ALU.mult, ALU.add,
                                pkn + [('tokS', tt), km1], [km2])
                            ACT(sc_[0:TP, 0:1], m2[0:TP, 128:129], AF.Abs, [km2], [ksc])
                            TS('dve', sc_[0:TP, 0:1], sc_[0:TP, 0:1], tokS[0:TP, tt, 2, h:h + 1], ALU.max, [ksc, ('tokS', tt)], [ksc])
                            RECIP(sc_[0:TP, 1:2], sc_[0:TP, 0:1], [ksc], [ksc])
                            TS('dve', m1[0:TP, 0:128], m2[0:TP, 0:128], sc_[0:TP, 1:2], ALU.mult, [km2, ksc], [km1, ksc],
                               s2=0.0, op1=ALU.add, accum=sc_[0:TP, 2:3])
                            TS('dve', sc_[0:TP, 2:3], sc_[0:TP, 2:3], -1.0 / 128, ALU.mult, [ksc], [ksc])
                            TS('dve', m1[0:TP, 0:128], m1[0:TP, 0:128], sc_[0:TP, 2:3], ALU.add, [km1, ksc], [km1])
                            ACT(m2[0:TP, 0:128], m1[0:TP, 0:128], AF.Square, [km1], [km2, ksc], accum=sc_[0:TP, 3:4])
                            TS('dve', sc_[0:TP, 4:5], sc_[0:TP, 3:4], 1.0 / 128, ALU.mult, [ksc], [ksc], s2=1e-6, op1=ALU.add)
                            ACT(sc_[0:TP, 4:5], sc_[0:TP, 4:5], AF.Sqrt, [ksc], [ksc])
                            RECIP(sc_[0:TP, 5:6], sc_[0:TP, 4:5], [ksc], [ksc])
                            STT(m2[0:TP, 0:128], m1[0:TP, 0:128], sc_[0:TP, 5:6], hnG[0:TP, hsl], ALU.mult, ALU.mult, [km1, ksc, 'hnG'], [km2])
                            TT('dve', ysb[tt][0:TP, hsl], m2[0:TP, 0:128], oz[tt][0:TP, hsl], ALU.mult, [km2, ('tH', 8 + tt)], [('tH', 20 + tt)])
                        for h in range(4):
                            pt, ptk = pst()
                            TR(pt[:, 0:TP], ysb[tt][0:TP, h * 128:(h + 1) * 128], idB[0:TP, 0:TP], [('tH', 20 + tt), 'cstB'], ptk)
                            CP('act', ysT[:, 4 + h, tsl], pt[:, 0:TP], ptk, [('ysT', 4 + h)])

                    S.enabled = ('C' in cfg.get('ph', 'ABCG')) or ('c' in cfg.get('ph', 'ABCG'))
                    sQ, kQ = win_slab(l, C_CQ, 512)
                    sKc, kKc = win_slab(l, C_CK, 512)
                    qs16 = [tmpH[i] for i in range(8)]
                    nq16 = [tmpH[8 + i] for i in range(8)]
                    kf16 = [tmpH[16 + i] for i in range(4)]
                    for h in range(8):
                        oth = slice(64 - (h % 2) * 64, 128 - (h % 2) * 64)
                        MSET('pool', qs16[h][oth, :TB], 0.0, [('tH', h)])
                        MSET('pool', nq16[h][oth, :TB], 0.0, [('tH', 8 + h)])
                    for p in range(4):
                        ps, pk = psq(4)
                        proj_fm(sQ, kQ, p * 128, 128, ps[:, :TB], pk)
                        for hh in range(2):
                            h = 2 * p + hh
                            hs = slice(hh * 64, hh * 64 + 64)
                            ACT(qs16[h][hs, :TB], ps[hs, :TB], AF.Identity, pk + ['PV'], [('tH', h)], bias=PV[hs, l, PV_BCQ + p:PV_BCQ + p + 1], scale=0.125)
                            ACT(nq16[h][hs, :TB], ps[hs, :TB], AF.Identity, pk + ['PV'], [('tH', 8 + h)], bias=PV[hs, l, PV_NBCQ + p:PV_NBCQ + p + 1], scale=-0.125)
                    for p in range(4):
                        ps, pk = psq(4)
                        proj_fm(sKc, kKc, p * 128, 128, ps[:, :TB], pk)
                        ACT(kf16[p][:, :TB], ps[:, :TB], AF.Identity, pk + ['PV'], [('tH', 16 + p)], bias=pv(PV_BCK + p))
                        DMA('sp', X['kTd'][p, :, P + t0:P + t0 + TB], kf16[p][:, :TB], [('tH', 16 + p)], [('kTd', si)])
                    for tt in range(NTL):
                        ps, pk = psq(4)
                        proj_tm(sKc, kKc, tt, ps[0:TP, :], pk, 1536)
                        CP('act', tmpF[tt % 2][0:TP, :], ps[0:TP, :], pk, [('tF', tt % 2)])
                        DMA('sp', O['sbk'][l, t0 + tt * TP:t0 + (tt + 1) * TP, :], tmpF[tt % 2][0:TP, :], [('tF', tt % 2)], [], is_out=True)
                    sVc, kVc = win_slab(l, C_CV, 512)
                    for tt in range(NTL):
                        ps, pk = psq(4)
                        proj_tm(sVc, kVc, tt, ps[0:TP, :], pk, 2048)
                        CP('act', tmpF[2 + tt % 2][0:TP, :], ps[0:TP, :], pk, [('tF', 2 + tt % 2)])
                        CP('act', tmpH[20 + tt % 2][0:TP, :], ps[0:TP, :], pk, [('tH', 20 + tt % 2)])
                        DMA('sp', O['sbv'][l, t0 + tt * TP:t0 + (tt + 1) * TP, :], tmpF[2 + tt % 2][0:TP, :], [('tF', 2 + tt % 2)], [], is_out=True)
                        DMA('sp', X['vd'][P + t0 + tt * TP:P + t0 + (tt + 1) * TP, :], tmpH[20 + tt % 2][0:TP, :], [('tH', 20 + tt % 2)], [('vd', si)])
                    sZc, kZc = win_slab(l, C_CZ, 512)
                    S.enabled = 'C' in cfg.get('ph', 'ABCG')
                    q0 = P + t0
                    kend = q0 + TB
                    nkb = (kend + 127) // 128
                    for half in range(2):
                        accs = [psbank(4 + i) for i in range(2)]
                        SP = [tmpH[22 + i] for i in range(4)]
                        for ki, kb in enumerate(reversed(range(nkb))):
                            k0 = kb * 128
                            ks = min(128, kend - k0)
                            kt_, vt_ = kTblk[ki % 2], vblk[ki % 2]
                            kkt, kvt = ('kTblk', ki % 2), ('vblk', ki % 2)
                            DMA('sp', kt_[:, :, 0:ks], X['kTd'][:, :, k0:k0 + ks].rearrange("c p s -> p c s"), [('kTd', si)], [kkt])
                            DMA('sp', vt_[0:ks, :], X['vd'][k0:k0 + ks, :], [('vd', si)], [kvt])
                            masked = (k0 + ks > q0)
                            mk = attB[0:ks, (k0 - q0) // 128, 0:TB] if masked else None
                            for hi in range(4):
                                h = half * 4 + hi
                                p = h // 2
                                hs = slice((h % 2) * 64, (h % 2) * 64 + 64)
                                psz, pkz = psq(4)
                                MM(psz[0:ks, :TB], kt_[:, p, 0:ks], qs16[h][:, :TB], [kkt, ('tH', h)], pkz)
                                ef = tmpF[4 + hi % 4]
                                kef = ('tF', 4 + hi % 4)
                                sp16 = tmpH[26 + hi % 2]
                                ksp = ('tH', 26 + hi % 2)
                                A16 = mlh2[hi % 2]
                                kA = ('mlh2', hi % 2)
                                ACT(ef[0:ks, :TB], psz[0:ks, :TB], AF.Exp, pkz, [kef])
                                ACT(sp16[0:ks, :TB], ef[0:ks, :TB], AF.Ln, [kef], [ksp], bias=1.0)
                                if masked:
                                    TT('pool', sp16[0:ks, :TB], sp16[0:ks, :TB], mk, ALU.mult, [ksp, 'attB'], [ksp])
                                psa, pka = psq(4)
                                MM(psa[0:ks, :TB], triB[0:ks, 0:ks], sp16[0:ks, :TB], ['cstB', ksp], pka, start=True, stop=False)
                                if ki > 0:
                                    MM(psa[0:ks, :TB], onesB[:, 0:ks], SP[hi][:, :TB], ['cstB', ('tH', 22 + hi)], pka, start=False, stop=False)
                                MM(psa[0:ks, :TB], kt_[:, p, 0:ks], nq16[h][:, :TB], [kkt, ('tH', 8 + h)], pka, start=False, stop=True)
                                ACT(A16[0:ks, :TB], psa[0:ks, :TB], AF.Exp, pka, [kA], scale=-1.0)
                                if masked:
                                    TT('dve', A16[0:ks, :TB], A16[0:ks, :TB], mk, ALU.mult, [kA, 'attB'], [kA])
                                acc, kacc = accs[hi // 2]
                                MM(acc[hs, :TB], vt_[0:ks, h * 64:(h + 1) * 64], A16[0:ks, :TB], [kvt, kA], kacc,
                                   start=(ki == 0), stop=(ki == nkb - 1))
                                if ki == 0:
                                    if ks < 128:
                                        MSET('pool', SP[hi][:, :TB], 0.0, [('tH', 22 + hi)])
                                    CP('pool', SP[hi][0:ks, :TB], sp16[0:ks, :TB], [ksp], [('tH', 22 + hi)])
                                elif ki < nkb - 1:
                                    TT('pool', SP[hi][0:ks, :TB], SP[hi][0:ks, :TB], sp16[0:ks, :TB], ALU.add, [ksp, ('tH', 22 + hi)], [('tH', 22 + hi)])
                        for pi in range(2):
                            p = half * 2 + pi
                            ps, pk = psq(4)
                            proj_fm(sZc, kZc, p * 128, 128, ps[:, :TB], pk)
                            ACT(tmpH[21][:, :TB], ps[:, :TB], AF.Silu, pk + ['PV'], [('tH', 21)], bias=pv(PV_BCZ + p))
                            acc, kacc = accs[pi]
                            TT('dve', ysT[:, 8 + p, :TB], acc[:, :TB], tmpH[21][:, :TB], ALU.mult, kacc + [('tH', 21)], [('ysT', 8 + p)])

                    S.enabled = 'G' in cfg.get('ph', 'ABCG')
                    for n in range(3):
                        sB, kB = load_slab(wb_br[l, n].rearrange("(k p) c -> p k c", p=128), 4, 1024, ('wb_br', l))
                        for half in range(2):
                            sG, kG = win_slab(l, C_G0 + n * 1024 + half * 512, 512)
                            for f4 in range(4):
                                fc = half * 4 + f4
                                psg, pkg = psq(4)
                                proj_fm(sG, kG, f4 * 128, 128, psg[:, :TB], pkg)
                                psr, pkr = psq(4)
                                for kc in range(4):
                                    MM(psr[:, :TB], sB[:, kc, fc * 128:(fc + 1) * 128], ysT[:, 4 * n + kc, :TB], [kB, ('ysT', 4 * n + kc)], pkr,
                                       start=(kc == 0), stop=(kc == 3))
                                ACT(tmpF[8][:, :TB], psg[:, :TB], AF.Sigmoid, pkg + ['PV'], [('tF', 8)], bias=pv(PV_BG + n * 8 + fc))
                                if n == 0:
                                    TT('dve', tmpF[fc][:, :TB], tmpF[8][:, :TB], psr[:, :TB], ALU.mult, [('tF', 8)] + pkr, [('tF', fc)])
                                else:
                                    TT('dve', tmpF[8][:, :TB], tmpF[8][:, :TB], psr[:, :TB], ALU.mult, [('tF', 8)] + pkr, [('tF', 8)])
                                    if n == 1:
                                        TT('dve', tmpF[fc][:, :TB], tmpF[fc][:, :TB], tmpF[8][:, :TB], ALU.add, [('tF', fc), ('tF', 8)], [('tF', fc)])
                                    else:
                                        TT('dve', tmpH[fc][:, :TB], tmpF[fc][:, :TB], tmpF[8][:, :TB], ALU.add, [('tF', fc), ('tF', 8)], [('tH', fc)])
                    s5, k5 = psbank(4)
                    s6, k6 = psbank(5)
                    alpha = (2.0 * cfg.get('DEPTH', 4)) ** 0.25
                    for half in range(2):
                        sWo, kWo = load_slab(wb_out[l].rearrange("(k p) c -> p k c", p=128)[:, :, half * 512:(half + 1) * 512], KC, 512, ('wb_out', l))
                        for f4 in range(4):
                            fc = half * 4 + f4
                            psm, pkm = psq(4)
                            for kc in range(KC):
                                MM(psm[:, :TB], sWo[:, kc, f4 * 128:(f4 + 1) * 128], tmpH[kc][:, :TB], [kWo, ('tH', kc)], pkm,
                                   start=(kc == 0), stop=(kc == KC - 1))
                            DMA('sp', tmpF[8][:, :TB], X['xres'][fc, :, t0:t0 + TB], [KX], [('tF', 8)])
                            STT(tmpF[fc][:, :TB], tmpF[8][:, :TB], alpha, psm[:, :TB], ALU.mult, ALU.add, [('tF', 8)] + pkm, [('tF', fc)])
                    layernorm(TB, lambda fc: pv(PV_LNG + fc), lambda fc: pv(PV_LNB + fc), 1e-5)
                    for fc in range(KC):
                        CP('pool', ysT[:, fc, :TB], tmpF[fc][:, :TB], [('tF', fc)], [('ysT', fc)])
                    sPl, kPl = load_slab(wb_ple[l].rearrange("(k p) c -> p k c", p=128), 2, 1024, ('wb_ple', l))
                    last = (l == L - 1)
                    for half in range(2):
                        sPg, kPg = load_slab(wb_pg[l].rearrange("(k p) c -> p k c", p=128)[:, :, half * 512:(half + 1) * 512], KC, 512, ('wb_pg', l))
                        for f4 in range(4):
                            fc = half * 4 + f4
                            psg, pkg = psq(4)
                            for kc in range(KC):
                                MM(psg[:, :TB], sPg[:, kc, f4 * 128:(f4 + 1) * 128], ysT[:, kc, :TB], [kPg, ('ysT', kc)], pkg,
                                   start=(kc == 0), stop=(kc == KC - 1))
                            psp, pkp = psq(4)
                            for kc in range(2):
                                MM(psp[:, :TB], sPl[:, kc, fc * 128:(fc + 1) * 128], p16[:, kc, :TB], [kPl, 'p16'], pkp, start=(kc == 0), stop=(kc == 1))
                            ACT(tmpF[8][:, :TB], psg[:, :TB], AF.Sigmoid, pkg, [('tF', 8)])
                            TT('dve', tmpF[8][:, :TB], tmpF[8][:, :TB], psp[:, :TB], ALU.mult, [('tF', 8)] + pkp, [('tF', 8)])
                            TT('dve', tmpF[fc][:, :TB], tmpF[fc][:, :TB], tmpF[8][:, :TB], ALU.add, [('tF', fc), ('tF', 8)], [('tF', fc)])
                            if last:
                                DMA('sp', O['yT'][fc * 128:(fc + 1) * 128, t0:t0 + TB], tmpF[fc][:, :TB], [('tF', fc)], [], is_out=True)
                            else:
                                DMA('sp', X['xres'][fc, :, t0:t0 + TB], tmpF[fc][:, :TB], [('tF', fc)], [KX])
                                CP('pool', tmpH[16 + fc][:, :TB], tmpF[fc][:, :TB], [('tF', fc)], [('tH', 16 + fc)])
                                DMA('sp', X['xTd'][fc, :, t0:t0 + TB], tmpH[16 + fc][:, :TB], [('tH', 16 + fc)], [KXD])

                S.enabled = True
                S.budget = None
                DMA('sp', O['shift'][l], carX[:], ['carX'], [], is_out=True)
                DMA('sp', O['conv'][l], carB[:], ['carB'], [], is_out=True)
                DMA('sp', O['m'][l], mbuf[:, 63:64], ['mbuf'], [], is_out=True)
                for p in range(4):
                    DMA('sp', O['wkv'][l, p], Hf[p][:, :, :], [('Hf', p)], [], is_out=True)
                    DMA('sp', O['ct'][l, p], CTf[p][:], [('CTf', p)], [], is_out=True)

        S.emit(nc)
    return nc


N_CORES = 8
_CACHE = {}


def _seq_inputs(i, x, p, st):
    d = {f"xT{i}": np.ascontiguousarray(x.T), f"pT{i}": np.ascontiguousarray(p.transpose(0, 2, 1))}
    if st is not None:
        shift, wkv, conv, c, n, m, ck, cv = st
        L = shift.shape[0]
        d[f"shift{i}"] = np.ascontiguousarray(shift.reshape(L, 26, 64).transpose(0, 2, 1))
        d[f"wkv{i}"] = np.ascontiguousarray(wkv.reshape(L, 4, 2, 64, 64).transpose(0, 1, 4, 2, 3))
        d[f"conv{i}"] = np.ascontiguousarray(conv.reshape(L, 3, 8, 128).transpose(0, 3, 2, 1))
        d[f"ct{i}"] = np.ascontiguousarray(np.concatenate([c.transpose(0, 1, 3, 2), n[..., None]], axis=-1))
        d[f"m{i}"] = np.ascontiguousarray(np.repeat(m, 32, axis=1)[..., None])
        Pn = ck.shape[1]
        d[f"ck{i}"] = np.ascontiguousarray(ck.reshape(L, Pn, 4, 128).transpose(0, 2, 3, 1))
        d[f"cv{i}"] = np.ascontiguousarray(cv.reshape(L, Pn, 512))
    return d


def _seq_outputs(r, i, L, T):
    y = r[f"yT{i}"].T
    shift = r[f"shift_o{i}"].transpose(0, 2, 1).reshape(L, 1664)
    wkv = r[f"wkv_o{i}"].transpose(0, 1, 3, 4, 2).reshape(L, 8, 64, 64)
    conv = r[f"conv_o{i}"].transpose(0, 3, 2, 1).reshape(L, 3, 1024)
    ct = r[f"ct_o{i}"]
    c = ct[..., 0:128].transpose(0, 1, 3, 2)
    n = ct[..., 128]
    m = r[f"m_o{i}"][:, ::32, 0]
    sbk = r[f"sbk{i}"].reshape(L, T, 8, 64)
    sbv = r[f"sbv{i}"].reshape(L, T, 8, 64)
    return [y, shift, wkv, conv, c, n, m, sbk, sbv]


WNAMES = ['ln_in_g', 'ln_in_b', 'w_in', 'b_in', 'mu_a', 'w0_a', 'w_decay_up', 'a0_a', 'w_iclr_up', 'k_k', 'k_a', 'r_k',
          'gn_a_g', 'gn_a_b', 'conv_b_w', 'conv_b_b', 'hn_b_g', 'w_branch', 'w_out', 'ln_g', 'ln_b', 'w_ple', 'w_ple_gate']


def weight_inputs(w, L):
    f = lambda a: np.ascontiguousarray(np.asarray(a, dtype=np.float32))
    wd = {k: f(w[k]) for k in WNAMES}
    wd['r_k'] = wd['r_k'].reshape(L, 512)
    wd['cst'] = make_consts()
    wd['catt'] = make_att()
    col = lambda v: v.reshape(-1, 128).T
    pv = np.zeros((128, L, NPV), np.float32)
    for l in range(L):
        b = wd['b_in'][l]
        pv[:, l, PV_BA:PV_BA + 17] = col(b[0:2176])
        pv[:, l, PV_BQK:PV_BQK + 8] = col(b[C_BQ:C_BQ + 1024])
        pv[:, l, PV_BCQ:PV_BCQ + 4] = col(b[C_CQ:C_CQ + 512])
        pv[:, l, PV_BCK:PV_BCK + 4] = col(b[C_CK:C_CK + 512])
        pv[:, l, PV_BCZ:PV_BCZ + 4] = col(b[C_CZ:C_CZ + 512])
        pv[:, l, PV_BG:PV_BG + 24] = col(b[C_G0:C_G0 + 3072])
        pv[:, l, PV_BI] = np.repeat(b[C_BI:C_BI + 4], 32)
        pv[:, l, PV_NBF] = np.repeat(b[C_BF:C_BF + 4], 32)
        pv[:, l, PV_MU:PV_MU + 13] = col(wd['mu_a'][l])
        for nm, c in (('w0_a', PV_W0), ('a0_a', PV_A0), ('k_k', PV_KK), ('k_a', PV_KA), ('r_k', PV_RK)):
            pv[:, l, c:c + 4] = col(wd[nm][l])
        for j in range(4):
            pv[:, l, PV_CW + 8 * j:PV_CW + 8 * j + 8] = col(wd['conv_b_w'][l, j])
        pv[:, l, PV_CB:PV_CB + 8] = col(wd['conv_b_b'][l])
        pv[:, l, PV_LNG:PV_LNG + 8] = col(wd['ln_g'][l])
        pv[:, l, PV_LNB:PV_LNB + 8] = col(wd['ln_b'][l])
    wd['pvh'] = pv
    wd['pvg'] = np.ascontiguousarray(np.concatenate([col(wd['ln_in_g']), col(wd['ln_in_b'])], axis=1))
    gn = np.zeros((L, 64, 16, 64), np.float32)
    gn[:, :, 0:8, :] = wd['gn_a_g'].reshape(L, 1, 8, 64)
    gn[:, :, 8:16, :] = wd['gn_a_b'].reshape(L, 1, 8, 64)
    wd['gnh'] = gn
    c64 = lambda v: v.reshape(-1, 64).T
    pvx = np.zeros((64, L, NX), np.float32)
    for l in range(L):
        pvx[:, l, X_B:X_B + 26] = c64(wd['b_in'][l, 0:1664])
        pvx[:, l, X_MU:X_MU + 26] = c64(wd['mu_a'][l])
        pvx[:, l, X_BZ:X_BZ + 8] = c64(wd['b_in'][l, C_AZ:C_AZ + 512])
        for nm, c in (('w0_a', X_W0), ('a0_a', X_A0), ('k_k', X_KK), ('k_a', X_KA), ('r_k', X_RK)):
            pvx[:, l, c:c + 8] = c64(wd[nm][l])
    wd['pvx'] = pvx
    i64 = np.arange(64)
    r_, c_ = i64[:, None], i64[None, :]
    wd['cst3'] = np.ascontiguousarray(np.concatenate([np.tile(m, (1, 2)) for m in (r_ < c_, r_ > c_, r_ <= c_, r_ == c_)], axis=1).astype(np.float32))
    wd['hnh'] = np.ascontiguousarray(np.broadcast_to(wd['hn_b_g'][:, None, :], (L, 128, 512)))
    return wd


def kernel(x_prompt, x_sample, state_shift_a, state_wkv, state_conv_b, state_mlstm_c, state_mlstm_n,
           state_mlstm_m, cache_sb_k, cache_sb_v, p_prompt, p_sample, **weights):
    f = lambda a: np.ascontiguousarray(np.asarray(a, dtype=np.float32))
    x_prompt, x_sample, p_prompt, p_sample = f(x_prompt), f(x_sample), f(p_prompt), f(p_sample)
    L = p_prompt.shape[0]
    Bp, Tp = x_prompt.shape[:2]
    Bs, Ts = x_sample.shape[:2]
    Pn = cache_sb_k.shape[2]
    npc = Bp // N_CORES
    cfg = {'L': L, 'seqs': [{'T': Tp, 'P': 0}] * npc + [{'T': Ts, 'P': Pn}]}
    key = (L, Tp, Ts, Pn, npc)
    if key not in _CACHE:
        _CACHE[key] = build(cfg)
    nc = _CACHE[key]
    wd = weight_inputs(weights, L)
    sts = [f(a) for a in (state_shift_a, state_wkv, state_conv_b, state_mlstm_c, state_mlstm_n, state_mlstm_m, cache_sb_k, cache_sb_v)]
    in_maps = []
    for c in range(N_CORES):
        d = dict(wd)
        for i in range(npc):
            b = c * npc + i
            d.update(_seq_inputs(i, x_prompt[b], p_prompt[:, b], None))
        d.update(_seq_inputs(npc, x_sample[c], p_sample[:, c], [a[:, c] for a in sts]))
        in_maps.append(d)
    res = run_bass_kernel_spmd(nc, in_maps, core_ids=list(range(N_CORES)))
    pr = [[] for _ in range(9)]
    sr = [[] for _ in range(9)]
    for c in range(N_CORES):
        r = res.results[c]
        for i in range(npc):
            for k, v in enumerate(_seq_outputs(r, i, L, Tp)):
                pr[k].append(v)
        for k, v in enumerate(_seq_outputs(r, npc, L, Ts)):
            sr[k].append(v)
    outs = []
    for k in range(9):
        a = np.stack(pr[k], 0)
        b = np.stack(sr[k], 0)
        if k >= 1:
            a = np.moveaxis(a, 1, 0)
            b = np.moveaxis(b, 1, 0)
        outs.append((np.ascontiguousarray(a, dtype=np.float32), np.ascontiguousarray(b, dtype=np.float32)))
    y, sh, wkv, conv, cc, nn, mm, sbk, sbv = outs
    return (y[0], y[1], sh[0], sh[1], wkv[0], wkv[1], conv[0], conv[1], cc[0], cc[1], nn[0], nn[1], mm[0], mm[1],
            sbk[0], sbk[1], sbv[0], sbv[1])
```

```python
import contextlib
import math
import numpy as np
import concourse.bass as bass
import concourse.mybir as mybir
from concourse.bass_utils import run_bass_kernel_spmd

F32 = mybir.dt.float32
BF16 = mybir.dt.bfloat16
AF = mybir.ActivationFunctionType
ALU = mybir.AluOpType

ENG = ('pe', 'act', 'dve', 'pool', 'sp')
ROT = 16000
NDSEM = 12


class Sched:
    def __init__(self):
        self.q = {e: [] for e in ENG}
        self.cnt = {e: 0 for e in ENG}
        self.clock = {e: {} for e in ENG}
        self.lastw = {}
        self.readers = {}
        self.dcount = {'sp': 0, 'pool': 0, 'act': 0}
        self.dlast = {}
        self.out_dmas = []
        self.enabled = True
        self.budget = None

    def _collect(self, eng, reads, writes, is_dma):
        deps = []
        for k in reads:
            w = self.lastw.get(k)
            if w is not None:
                deps.append(w)
        for k in writes:
            w = self.lastw.get(k)
            if w is not None and not (eng == 'pe' and w[0] == 'eng' and w[1] == 'pe'):
                deps.append(w)
            for r in self.readers.get(k, ()):
                if not (eng == 'pe' and r[0] == 'eng' and r[1] == 'pe'):
                    deps.append(r)
        ck = self.clock[eng]
        best = {}
        for d in deps:
            src = (d[0], d[1])
            if ck.get(src, 0) >= d[2]:
                continue
            if src not in best or best[src][2] < d[2]:
                best[src] = d
        for src, d in best.items():
            ck[src] = d[2]
        return list(best.values())

    def op(self, eng, fn, reads=(), writes=()):
        if not self.enabled:
            return None
        if self.budget is not None:
            if self.budget <= 0:
                return None
            self.budget -= 1
        waits = self._collect(eng, reads, writes, False)
        self.cnt[eng] += 1
        n = self.cnt[eng]
        me = ('eng', eng, n)
        self.q[eng].append(('op', fn, waits, n))
        for k in reads:
            self.readers.setdefault(k, []).append(me)
        for k in writes:
            self.lastw[k] = me
            self.readers[k] = []
        return me

    def dma(self, eng, fn, reads=(), writes=(), is_out=False):
        if not self.enabled:
            return None
        if self.budget is not None:
            if self.budget <= 0:
                return None
            self.budget -= 1
        waits = self._collect(eng, reads, writes, True)
        i = self.dcount[eng]
        self.dcount[eng] += 1
        slot = (eng, i % NDSEM)
        prev = self.dlast.get(slot, 0)
        val = prev + 16
        ck = self.clock[eng]
        src = ('dma', slot)
        if prev > 0 and ck.get(src, 0) < prev:
            ck[src] = prev
            waits = [w for w in waits if (w[0], w[1]) != src] + [('dma', slot, prev)]
        self.dlast[slot] = val
        me = ('dma', slot, val)
        self.q[eng].append(('dma', fn, waits, slot))
        for k in reads:
            self.readers.setdefault(k, []).append(me)
        for k in writes:
            self.lastw[k] = me
            self.readers[k] = []
        if is_out:
            self.out_dmas.append(me)
        return me

    def emit(self, nc):
        with contextlib.ExitStack() as st:
            esem = {}
            for e in ENG:
                for r in range(self.cnt[e] // ROT + 1):
                    esem[(e, r)] = st.enter_context(nc.semaphore(f"s_{e}_{r}"))
            dsem = {}
            for slot in self.dlast:
                dsem[slot] = st.enter_context(nc.semaphore(f"d_{slot[0]}_{slot[1]}"))
            block = st.enter_context(nc.Block())

            def do_wait(engine, d):
                if d[0] == 'eng':
                    n = d[2]
                    engine.wait_ge(esem[(d[1], (n - 1) // ROT)], (n - 1) % ROT + 1)
                else:
                    engine.wait_ge(dsem[d[1]], d[2])

            def run(e, engine):
                for item in self.q[e]:
                    kind, fn, waits = item[0], item[1], item[2]
                    for d in waits:
                        do_wait(engine, d)
                    inst = fn(engine)
                    if kind == 'op':
                        n = item[3]
                        inst.then_inc(esem[(e, (n - 1) // ROT)], 1)
                    else:
                        inst.then_inc(dsem[item[3]], 16)
                if e == 'sp':
                    for slot, v in self.dlast.items():
                        engine.wait_ge(dsem[slot], v)

            @block.tensor
            def _(eng):
                run('pe', eng)

            @block.scalar
            def _(eng):
                run('act', eng)

            @block.vector
            def _(eng):
                run('dve', eng)

            @block.gpsimd
            def _(eng):
                run('pool', eng)

            @block.sync
            def _(eng):
                run('sp', eng)


D = 1024
KC = 8
NIN = 9864
DPLE = 256
C_A0, C_AZ, C_BQ, C_BK, C_BV, C_BI, C_BF, C_BO, C_BZ = 0, 1664, 2176, 2688, 3200, 3712, 3716, 3720, 4232
C_CQ, C_CK, C_CV, C_CZ, C_G0 = 4744, 5256, 5768, 6280, 6792
DECAY_C = math.exp(-0.5)
LN_C = -0.5 * math.log(128.0)

CS_ID = 0
CS_SU = 128
CS_SL = 256
CS_IU = 384
CS_BO = 512
CS_TRI = 640
CS_ONE = 768
CS_M01 = 896
CS_SEL = 1408
NCST = 1920


def make_consts():
    c = np.zeros((128, NCST), np.float32)
    i = np.arange(128)
    r, cc = i[:, None], i[None, :]
    same = (r // 64) == (cc // 64)
    c[:, CS_ID:CS_ID + 128] = (r == cc)
    c[:, CS_SU:CS_SU + 128] = same & (r < cc)
    c[:, CS_SL:CS_SL + 128] = same & (r > cc)
    c[:, CS_IU:CS_IU + 128] = same & (r <= cc)
    c[:, CS_BO:CS_BO + 128] = same
    c[:, CS_TRI:CS_TRI + 128] = (r >= cc)
    c[:, CS_ONE:CS_ONE + 128] = 1.0
    t = np.arange(512)
    c[:, CS_M01:CS_M01 + 512] = (t % 64 != 0)[None, :]
    for h in range(4):
        c[32 * h, CS_SEL + 128 * h:CS_SEL + 128 * (h + 1)] = 1.0
    return c


def make_att():
    a = np.zeros((128, 2048), np.float32)
    r = np.arange(128)[:, None]
    t = np.arange(512)
    for j in range(4):
        a[:, 512 * j:512 * (j + 1)] = (t[None, :] - r) > 128 * j
    return a


PV_BA, PV_BQK, PV_BCQ, PV_NBCQ, PV_BCK, PV_BCZ, PV_BG = 0, 17, 25, 29, 33, 37, 41
PV_BI, PV_NBF, PV_MU, PV_OMU, PV_W0, PV_A0, PV_KK, PV_KA, PV_RK = 65, 66, 67, 80, 93, 97, 101, 105, 109
PV_CW, PV_CB, PV_LNG, PV_LNB = 113, 145, 153, 161
NPV = 170
X_B, X_MU, X_OMU, X_BZ, X_W0, X_A0, X_KK, X_KA, X_RK = 0, 26, 52, 78, 86, 94, 102, 110, 118
NX = 126


def build(cfg):
    L = cfg['L']
    seqs = cfg['seqs']
    nc = bass.Bass("TRN2", target_bir_lowering=False)

    def din(name, shape, dt=F32):
        return nc.dram_tensor(name, list(shape), dt, kind="ExternalInput").ap()

    def dout(name, shape):
        return nc.dram_tensor(name, list(shape), F32, kind="ExternalOutput").ap()

    def dint(name, shape, dt):
        return nc.dram_tensor(name, list(shape), dt, kind="Internal").ap()

    W = {}
    for nm, shp in [('ln_in_g', [D]), ('ln_in_b', [D]), ('w_in', [L, D, NIN]), ('b_in', [L, NIN]), ('mu_a', [L, 1664]),
                    ('w0_a', [L, 512]), ('w_decay_up', [L, 64, 512]), ('a0_a', [L, 512]), ('w_iclr_up', [L, 64, 512]),
                    ('k_k', [L, 512]), ('k_a', [L, 512]), ('r_k', [L, 512]), ('gn_a_g', [L, 512]), ('gn_a_b', [L, 512]),
                    ('conv_b_w', [L, 4, D]), ('conv_b_b', [L, D]), ('hn_b_g', [L, 512]), ('w_branch', [L, 3, 512, D]),
                    ('w_out', [L, D, D]), ('ln_g', [L, D]), ('ln_b', [L, D]), ('w_ple', [L, DPLE, D]),
                    ('w_ple_gate', [L, D, D]), ('cst', [128, NCST]), ('catt', [128, 2048]), ('pvh', [128, L, NPV]), ('pvg', [128, 16]),
                    ('gnh', [L, 64, 16, 64]), ('hnh', [L, 128, 512]), ('cst3', [64, 512]), ('pvx', [64, L, NX])]:
        W[nm] = din(nm, shp)
    SI, SO, SX = [], [], []
    for i, sq in enumerate(seqs):
        T, P = sq['T'], sq['P']
        d = {'xT': din(f"xT{i}", [D, T]), 'pT': din(f"pT{i}", [L, DPLE, T])}
        if P > 0:
            d['shift'] = din(f"shift{i}", [L, 64, 26])
            d['wkv'] = din(f"wkv{i}", [L, 4, 64, 2, 64])
            d['conv'] = din(f"conv{i}", [L, 128, 8, 3])
            d['ct'] = din(f"ct{i}", [L, 4, 128, 129])
            d['m'] = din(f"m{i}", [L, 128, 1])
            d['ck'] = din(f"ck{i}", [L, 4, 128, P])
            d['cv'] = din(f"cv{i}", [L, P, 512])
        SI.append(d)
        SO.append({'yT': dout(f"yT{i}", [D, T]), 'shift': dout(f"shift_o{i}", [L, 64, 26]),
                   'wkv': dout(f"wkv_o{i}", [L, 4, 64, 2, 64]), 'conv': dout(f"conv_o{i}", [L, 128, 8, 3]),
                   'ct': dout(f"ct_o{i}", [L, 4, 128, 129]), 'm': dout(f"m_o{i}", [L, 128, 1]),
                   'sbk': dout(f"sbk{i}", [L, T, 512]), 'sbv': dout(f"sbv{i}", [L, T, 512])})
        SX.append({'xres': dint(f"xres{i}", [KC, 128, T], F32), 'kTd': dint(f"kTd{i}", [4, 128, P + T], BF16), 'xTd': dint(f"xTd{i}", [KC, 128, T], BF16),
                   'vd': dint(f"vd{i}", [P + T, 512], BF16)})
    wb_in = dint("wb_in", [L, D, NIN], BF16)
    wb_br = dint("wb_br", [L, 3, 512, D], BF16)
    wb_out = dint("wb_out", [L, D, D], BF16)
    wb_pg = dint("wb_pg", [L, D, D], BF16)
    wb_ple = dint("wb_ple", [L, DPLE, D], BF16)

    S = Sched()
    st = contextlib.ExitStack()
    with st:
        def sb(name, shape, dt=F32):
            return st.enter_context(nc.sbuf_tensor(name, list(shape), dt))

        cstF = sb("cstF", [128, 1280])
        cstB = sb("cstB", [128, CS_M01], BF16)
        attB = sb("attB", [128, 4, 512], BF16)
        PV = sb("PV", [128, L, NPV])
        PVG = sb("PVG", [128, 16])
        slabs = [sb(f"slab{i}", [128, 4096], BF16) for i in range(4)]
        xTb = [sb(f"xTb{i}", [128, KC, 512], BF16) for i in range(2)]
        slabZ = sb("slabZ", [128, KC, 128], BF16)
        tmpF = [sb(f"tmpF{i}", [128, 512]) for i in range(9)]
        lnm = sb("lnm", [128, 512])
        lnr = sb("lnr", [128, 512])
        tmpH = [sb(f"tmpH{i}", [128, 512], BF16) for i in range(28)]
        ysT = sb("ysT", [128, 12, 512], BF16)
        p16 = sb("p16", [128, 2, 512], BF16)
        pstg = sb("pstg", [128, 2, 512])
        kTblk = [sb(f"kTblk{i}", [128, 4, 128], BF16) for i in range(2)]
        vblk = [sb(f"vblk{i}", [128, 512], BF16) for i in range(2)]
        brow = sb("brow", [1, 2560], BF16)
        hnG = sb("hnG", [128, 512])
        gnx = sb("gnx", [64, 16, 64])
        upw = sb("upw", [64, 2, 512], BF16)
        wrep = sb("wrep", [128, 2, KC, 128], BF16)
        wcol = sb("wcol", [128, KC, 8], BF16)
        PVX = sb("PVX", [64, L, NX])
        cm2 = sb("cm2", [64, 4, 128], BF16)
        rawA = [sb(f"rawA{i}", [64, 513]) for i in range(2)]
        carX = sb("carX", [64, 26])
        lora = sb("lora", [64, 2, 512], BF16)
        sqh = sb("sqh", [64, 512], BF16)
        egP = [sb(f"egP{i}", [64, 2, 512]) for i in range(2)]
        Hf = [sb(f"Hf{i}", [64, 2, 64]) for i in range(4)]
        Hb = [sb(f"Hb{i}", [64, 2, 64], BF16) for i in range(4)]
        RS = []
        for p in range(2):
            d = {}
            for nm in ('Qa', 'Qb', 'Pa', 'Pb', 'Za', 'Zb', 'Abk', 'Ara', 'Ark', 'ast', 'kst', 'Vst', 'Xb', 'Ub', 'ysa'):
                d[nm] = sb(f"r{nm}{p}", [64, 2, 64], BF16)
            for nm in ('azst', 'e1', 'e2', 'e3', 'htmp'):
                d[nm] = sb(f"r{nm}{p}", [64, 2, 64])
            d['sc'] = sb(f"rsc{p}", [64, 8, 2])
            RS.append(d)
        rawB = [sb(f"rawB{i}", [128, 515]) for i in range(2)]
        carB = sb("carB", [128, 8, 3])
        mbuf = sb("mbuf", [128, 576])
        carM = sb("carM", [128, 2])
        vaug = sb("vaug", [128, 4, 4, 129], BF16)
        tokS = sb("tokS", [128, 4, 5, 4])
        tokS2 = sb("tokS2", [128, 4, 2, 4])
        rmask = sb("rmask", [128, 2])
        csB = sb("csB", [128, 4, 8])
        CTf = [sb(f"CTf{i}", [128, 129]) for i in range(4)]
        CTb = [sb(f"CTb{i}", [128, 129], BF16) for i in range(4)]
        mlt = [sb(f"mlt{i}", [128, 132]) for i in range(4)]
        mlh = [sb(f"mlh{i}", [128, 128], BF16) for i in range(2)]
        mlh2 = [sb(f"mlh2{i}", [128, 512], BF16) for i in range(2)]
        mls = sb("mls", [128, 8, 8])
        psb = [st.enter_context(nc.psum_tensor(f"psb{i}", [128, 512], F32)) for i in range(8)]
        psbH = [psb[i][:, :].bitcast(BF16) for i in range(8)]

        idF = cstF[:, 0:128]
        onesF = cstF[:, 128:256]
        m01 = cstF[:, 256:768]
        idB = cstB[:, CS_ID:CS_ID + 128]
        mIU = cstB[:, CS_IU:CS_IU + 128]
        blkones = cstB[:, CS_BO:CS_BO + 128]
        triB = cstB[:, CS_TRI:CS_TRI + 128]
        onesB = cstB[:, CS_ONE:CS_ONE + 128]

        def MM(out, lhsT, rhs, r, w, start=True, stop=True):
            S.op('pe', lambda e: e.matmul(out, lhsT=lhsT, rhs=rhs, start=start, stop=stop), r, w)

        def TR(out, in_, ident, r, w):
            S.op('pe', lambda e: e.transpose(out, in_, ident), r, w)

        def ACT(out, in_, func, r, w, bias=None, scale=None, accum=None):
            kw = {}
            if bias is not None:
                kw['bias'] = bias
            if scale is not None:
                kw['scale'] = scale
            if accum is not None:
                kw['accum_out'] = accum
            S.op('act', lambda e: e.activation(out=out, in_=in_, func=func, **kw), r, w)

        def TT(eng, out, in0, in1, op, r, w):
            S.op(eng, lambda e: e.tensor_tensor(out=out, in0=in0, in1=in1, op=op), r, w)

        def TS(eng, out, in0, s1, op0, r, w, s2=None, op1=None, accum=None):
            kw = {}
            if op1 is not None:
                kw['op1'] = op1
            if accum is not None:
                kw['accum_out'] = accum
            S.op(eng, lambda e: e.tensor_scalar(out=out, in0=in0, scalar1=s1, scalar2=s2, op0=op0, **kw), r, w)

        def STT(out, in0, scalar, in1, op0, op1, r, w):
            S.op('dve', lambda e: e.scalar_tensor_tensor(out=out, in0=in0, scalar=scalar, in1=in1, op0=op0, op1=op1), r, w)

        def CP(eng, out, in_, r, w):
            if eng == 'act':
                S.op('act', lambda e: e.activation(out=out, in_=in_, func=AF.Copy), r, w)
            else:
                S.op(eng, lambda e: e.tensor_copy(out=out, in_=in_), r, w)

        def MSET(eng, ap, val, w):
            S.op(eng, lambda e: e.memset(ap, val), (), w)

        def SCAN(out, d0, d1, init, op0, op1, r, w):
            S.op('dve', lambda e: e.tensor_tensor_scan(out=out, data0=d0, data1=d1, initial=init, op0=op0, op1=op1), r, w)

        def RECIP(out, in_, r, w):
            S.op('dve', lambda e: e.reciprocal(out=out, in_=in_), r, w)

        def DMA(eng, out, in_, r, w, is_out=False, slow=False):
            if slow:
                S.dma(eng, lambda e: e.dma_start(out=out, in_=in_, allow_slow_non_contiguous=True), r, w, is_out)
            else:
                S.dma(eng, lambda e: e.dma_start(out=out, in_=in_), r, w, is_out)

        RB = [0, 1, 2, 3, 7]
        ring = {1: 0, 2: 0, 4: 0, 't': 0}

        def psq(n=1):
            c = ring[n]
            ring[n] = c + 1
            b = RB[c % 5]
            if n == 1:
                o = (c // 5) % 4
            elif n == 2:
                o = 2 * ((c // 5) % 2)
            else:
                o = 0
            return psb[b][:, o * 128:(o + n) * 128], [('ps', b)]

        def psbank(b):
            return psb[b][:, :], [('ps', b)]

        def pst():
            c = ring['t']
            ring['t'] = c + 1
            b = RB[(c + 2) % 5]
            o = (c // 5) % 4
            return psbH[b][:, o * 256:o * 256 + 128], [('ps', b)]

        sring = {'i': 0}

        def load_slab(src3, kdim, ncols, wkey):
            i = sring['i']
            sring['i'] = (i + 1) % 4
            view = slabs[i][:, 0:kdim * ncols].rearrange("p (k c) -> p k c", k=kdim)
            DMA('sp', view, src3, wkeys.get(wkey, []), [('slab', i)])
            return view, ('slab', i)

        def win_slab(l, c0, n):
            return load_slab(wb_in[l].rearrange("(k p) c -> p k c", p=128)[:, :, c0:c0 + n], KC, n, ('wb_in', l))

        stg = [slabs[i][:, :].bitcast(F32) for i in range(2)]
        kst_ = [('slab', 0), ('slab', 1)]
        DMA('sp', stg[0][:, 0:NCST], W['cst'], [], [kst_[0]])
        DMA('sp', stg[1][:, 0:2048], W['catt'], [], [kst_[1]])
        CP('dve', cstB[:], stg[0][:, 0:CS_M01], [kst_[0]], ['cstB'])
        CP('dve', cstF[:, 0:128], stg[0][:, CS_ID:CS_ID + 128], [kst_[0]], ['cstF'])
        CP('dve', cstF[:, 128:256], stg[0][:, CS_ONE:CS_ONE + 128], [kst_[0]], ['cstF'])
        CP('dve', cstF[:, 256:768], stg[0][:, CS_M01:CS_M01 + 512], [kst_[0]], ['cstF'])
        CP('dve', cstF[:, 768:1280], stg[0][:, CS_SEL:CS_SEL + 512], [kst_[0]], ['cstF'])
        CP('dve', attB[:], stg[1][:, 0:2048].rearrange("p (j c) -> p j c", j=4), [kst_[1]], ['attB'])
        DMA('sp', lnm[0:64, :], W['cst3'], [], ['lnm'])
        CP('dve', cm2[:], lnm[0:64, :].rearrange("p (j c) -> p j c", j=4), ['lnm'], ['cm2'])
        DMA('sp', PVX[:], W['pvx'], [], ['PVX'])
        for b in range(8):
            MSET('dve', psb[b][:], 0.0, [('ps', b)])
        pc = {'i': 0}
        ceng = ('act', 'pool', 'dve')

        def precast(dst2d, src2d, key):
            rows, cols = src2d.shape
            for r0 in range(0, rows, 128):
                for c0 in range(0, cols, 2048):
                    n = min(2048, cols - c0)
                    i = pc['i']
                    pc['i'] += 1
                    sf, kf_ = stg[i % 2], kst_[i % 2]
                    db, kd_ = slabs[2 + i % 2], ('slab', 2 + i % 2)
                    DMA('sp', sf[:, 0:n], src2d[r0:r0 + 128, c0:c0 + n], [], [kf_])
                    CP(ceng[i % 3], db[:, 0:n], sf[:, 0:n], [kf_], [kd_])
                    DMA('sp', dst2d[r0:r0 + 128, c0:c0 + n], db[:, 0:n], [kd_], [(key, pc['i'])])
                    wkeys.setdefault(key, []).append((key, pc['i']))

        wkeys = {}
        for l in range(L):
            precast(wb_in[l], W['w_in'][l], ('wb_in', l))
            for n in range(3):
                precast(wb_br[l, n], W['w_branch'][l, n], ('wb_br', l))
            precast(wb_out[l], W['w_out'][l], ('wb_out', l))
            precast(wb_pg[l], W['w_ple_gate'][l], ('wb_pg', l))
            precast(wb_ple[l], W['w_ple'][l], ('wb_ple', l))
        DMA('sp', PV[:], W['pvh'], [], ['PV'])
        DMA('sp', PVG[:], W['pvg'], [], ['PV'])
        for l in range(L):
            TS('dve', PV[:, l, PV_OMU:PV_OMU + 13], PV[:, l, PV_MU:PV_MU + 13], -1.0, ALU.mult, ['PV'], ['PV'], s2=1.0, op1=ALU.add)
            TS('dve', PV[:, l, PV_NBF:PV_NBF + 1], PV[:, l, PV_NBF:PV_NBF + 1], -1.0, ALU.mult, ['PV'], ['PV'])
            TS('dve', PV[:, l, PV_NBCQ:PV_NBCQ + 4], PV[:, l, PV_BCQ:PV_BCQ + 4], -0.125, ALU.mult, ['PV'], ['PV'])
            TS('dve', PV[:, l, PV_BCQ:PV_BCQ + 4], PV[:, l, PV_BCQ:PV_BCQ + 4], 0.125, ALU.mult, ['PV'], ['PV'])
        MSET('dve', vaug[:], 1.0, ['vaug'])
        MSET('dve', rmask[:], 0.0, ['rmask'])
        MSET('dve', rmask[0:64, 0:1], 1.0, ['rmask'])
        MSET('dve', rmask[64:128, 1:2], 1.0, ['rmask'])
        for l in range(L):
            TS('dve', PVX[:, l, X_OMU:X_OMU + 26], PVX[:, l, X_MU:X_MU + 26], -1.0, ALU.mult, ['PVX'], ['PVX'], s2=1.0, op1=ALU.add)

        def layernorm(TB, gcol, bcol, eps):
            s5, k5 = psbank(4)
            s6, k6 = psbank(5)
            for fc in range(KC):
                ACT(lnr[:, :TB], tmpF[fc][:, :TB], AF.Square, [('tF', fc)], ['lnr'])
                MM(s5[:, :TB], onesF, tmpF[fc][:, :TB], ['cstF', ('tF', fc)], k5, start=(fc == 0), stop=(fc == KC - 1))
                MM(s6[:, :TB], onesF, lnr[:, :TB], ['cstF', 'lnr'], k6, start=(fc == 0), stop=(fc == KC - 1))
            ACT(lnm[:, :TB], s5[:, :TB], AF.Copy, k5, ['lnm'], scale=1.0 / D)
            TT('dve', tmpF[8][:, :TB], lnm[:, :TB], lnm[:, :TB], ALU.mult, ['lnm'], [('tF', 8)])
            STT(tmpF[8][:, :TB], s6[:, :TB], 1.0 / D, tmpF[8][:, :TB], ALU.mult, ALU.subtract, k6 + [('tF', 8)], [('tF', 8)])
            TS('dve', tmpF[8][:, :TB], tmpF[8][:, :TB], 0.0, ALU.max, [('tF', 8)], [('tF', 8)], s2=eps, op1=ALU.add)
            ACT(tmpF[8][:, :TB], tmpF[8][:, :TB], AF.Sqrt, [('tF', 8)], [('tF', 8)])
            RECIP(lnr[:, :TB], tmpF[8][:, :TB], [('tF', 8)], ['lnr'])
            for fc in range(KC):
                TT('dve', tmpF[fc][:, :TB], tmpF[fc][:, :TB], lnm[:, :TB], ALU.subtract, [('tF', fc), 'lnm'], [('tF', fc)])
                TT('dve', tmpF[fc][:, :TB], tmpF[fc][:, :TB], lnr[:, :TB], ALU.mult, [('tF', fc), 'lnr'], [('tF', fc)])
                ACT(tmpF[fc][:, :TB], tmpF[fc][:, :TB], AF.Identity, [('tF', fc), 'PV'], [('tF', fc)],
                    bias=bcol(fc), scale=gcol(fc))

        for si, sq in enumerate(seqs):
            T, P = sq['T'], sq['P']
            TB = min(512, T)
            NBLK = T // TB
            TP = min(128, TB)
            NTL = TB // TP
            NCH = TB // 64
            CPT = TP // 64
            I, O, X = SI[si], SO[si], SX[si]
            KX = ('xres', si)
            KXD = ('xTd', si)

            for j in range(NBLK):
                t0 = j * TB
                for fc in range(KC):
                    DMA('sp', tmpF[fc][:, :TB], I['xT'][fc * 128:(fc + 1) * 128, t0:t0 + TB], [], [('tF', fc)])
                layernorm(TB, lambda fc: PVG[:, fc:fc + 1], lambda fc: PVG[:, 8 + fc:9 + fc], 1e-5)
                for fc in range(KC):
                    DMA('sp', X['xres'][fc, :, t0:t0 + TB], tmpF[fc][:, :TB], [('tF', fc)], [KX])
                    CP('pool', tmpH[fc][:, :TB], tmpF[fc][:, :TB], [('tF', fc)], [('tH', fc)])
                    DMA('sp', X['xTd'][fc, :, t0:t0 + TB], tmpH[fc][:, :TB], [('tH', fc)], [KXD])

            for l in range(L):
                pv = lambda c, n=1: PV[:, l, c:c + n]
                DMA('sp', lnm[0:64, :], W['w_decay_up'][l], [], ['lnm'])
                DMA('sp', lnr[0:64, :], W['w_iclr_up'][l], [], ['lnr'])
                CP('pool', upw[:, 0, :], lnm[0:64, :], ['lnm'], ['upw'])
                CP('pool', upw[:, 1, :], lnr[0:64, :], ['lnr'], ['upw'])
                bl = W['b_in'][l]
                for gi, c0 in enumerate((C_BV, C_BO, C_BZ, C_CK, C_CV)):
                    DMA('sp', lnr[0:1, :], bl[c0:c0 + 512].rearrange("(o c) -> o c", o=1), [], ['lnr'])
                    CP('pool', brow[0:1, gi * 512:(gi + 1) * 512], lnr[0:1, :], ['lnr'], ['brow'])
                DMA('sp', hnG[:], W['hnh'][l], [], ['hnG'])
                DMA('sp', gnx[:], W['gnh'][l], [], ['gnx'])
                DMA('sp', wcol[:], wb_in[l].rearrange("(k p) c -> p k c", p=128)[:, :, C_BI:C_BI + 8], ['wb_in'], ['wcol'], slow=True)
                for g in range(2):
                    for h in range(4):
                        CP('pool', wrep[:, g, :, 32 * h:32 * h + 32], wcol[:, :, 4 * g + h:4 * g + h + 1].broadcast_to([128, KC, 32]),
                           ['wcol'], ['wrep'])
                if P > 0:
                    DMA('sp', carX[:], I['shift'][l], [], ['carX'])
                    for p in range(4):
                        DMA('sp', Hf[p][:, :, :], I['wkv'][l, p], [], [('Hf', p)])
                        DMA('sp', CTf[p][:], I['ct'][l, p], [], [('CTf', p)])
                    DMA('sp', carB[:], I['conv'][l], [], ['carB'])
                    DMA('sp', mbuf[:, 63:64], I['m'][l], [], ['mbuf'])
                    CP('dve', carM[:, 1:2], mbuf[:, 63:64], ['mbuf'], ['carM'])
                    MSET('dve', carM[:, 0:1], 0.0, ['carM'])
                    ci = 0
                    for p in range(4):
                        for c0 in range(0, P, 512):
                            n = min(512, P - c0)
                            DMA('sp', tmpF[ci % 8][:, 0:n], I['ck'][l, p][:, c0:c0 + n], [], [('tF', ci % 8)])
                            CP(('act', 'pool', 'dve')[ci % 3], tmpH[ci % 8][:, 0:n], tmpF[ci % 8][:, 0:n], [('tF', ci % 8)], [('tH', ci % 8)])
                            DMA('sp', X['kTd'][p, :, c0:c0 + n], tmpH[ci % 8][:, 0:n], [('tH', ci % 8)], [('kTd', si)])
                            ci += 1
                    for r0 in range(0, P, 128):
                        n = min(128, P - r0)
                        DMA('sp', tmpF[ci % 8][0:n, :], I['cv'][l, r0:r0 + n, :], [], [('tF', ci % 8)])
                        CP(('act', 'pool', 'dve')[ci % 3], tmpH[ci % 8][0:n, :], tmpF[ci % 8][0:n, :], [('tF', ci % 8)], [('tH', ci % 8)])
                        DMA('sp', X['vd'][r0:r0 + n, :], tmpH[ci % 8][0:n, :], [('tH', ci % 8)], [('vd', si)])
                        ci += 1
                else:
                    MSET('pool', carX[:], 0.0, ['carX'])
                    MSET('pool', carB[:], 0.0, ['carB'])
                    MSET('pool', mbuf[:, 0:64], 0.0, ['mbuf'])
                    MSET('pool', carM[:], 0.0, ['carM'])
                    for p in range(4):
                        MSET('pool', Hf[p][:, :, :], 0.0, [('Hf', p)])
                        MSET('pool', CTf[p][:], 0.0, [('CTf', p)])
                for p in range(4):
                    CP('pool', Hb[p][:, :, :], Hf[p][:, :, :], [('Hf', p)], [('Hb', p)])
                    CP('pool', CTb[p][:], CTf[p][:], [('CTf', p)], [('CTb', p)])

                for j in range(NBLK):
                    t0 = j * TB
                    S.enabled = True
                    S.budget = cfg.get('cut')
                    xb = xTb[(j + l) % 2]
                    KXB = ('xTb', (j + l) % 2)
                    DMA('sp', xb[:, :, :TB], X['xTd'][:, :, t0:t0 + TB].rearrange("k p t -> p k t"), [KXD], [KXB])
                    DMA('sp', pstg[:, :, :TB], I['pT'][l, :, t0:t0 + TB].rearrange("(k p) t -> p k t", p=128), [], ['pstg'])
                    CP('pool', p16[:, :, :TB], pstg[:, :, :TB], ['pstg'], ['p16'])

                    def proj_fm(slab, skey, c0, M, out_ap, okeys):
                        for kc in range(KC):
                            MM(out_ap, slab[:, kc, c0:c0 + M], xb[:, kc, :TB], [skey, KXB], okeys, start=(kc == 0), stop=(kc == KC - 1))

                    def proj_tm(slab, skey, tt, out_ap, okeys, boff):
                        for kc in range(KC):
                            MM(out_ap, xb[:, kc, tt * TP:(tt + 1) * TP], slab[:, kc, :], [skey, KXB], okeys, start=(kc == 0), stop=False)
                        MM(out_ap, onesB[0:1, 0:TP], brow[0:1, boff:boff + 512], ['cstB', 'brow'], okeys, start=False, stop=True)

                    S.enabled = 'A' in cfg.get('ph', 'ABCG')
                    pvx = lambda c, n=1: PVX[:, l, c:c + n]
                    H64 = slice(0, 64)
                    sR, kR = win_slab(l, 0, 512)
                    sK, kK = win_slab(l, 512, 512)
                    sV, kV = win_slab(l, 1024, 512)
                    sW, kW = win_slab(l, 1536, 512)
                    sZ, kZ = None, None

                    def fmx(slab, skey, crel, xi, dst, dkey):
                        rb = rawA[xi % 2]
                        rk_ = ('rawA', xi % 2)
                        ps, pk = psq(4)
                        proj_fm(slab, skey, crel, 64, ps[H64, :TB], pk)
                        ACT(rb[:, 1:1 + TB], ps[H64, :TB], AF.Identity, pk + ['PVX'], [rk_], bias=pvx(X_B + xi))
                        CP('pool', rb[:, 0:1], carX[:, xi:xi + 1], ['carX'], [rk_])
                        TS('dve', dst[H64, :TB], rb[:, 0:TB], pvx(X_MU + xi), ALU.mult, [rk_, 'PVX'], [dkey])
                        STT(dst[H64, :TB], rb[:, 1:1 + TB], pvx(X_OMU + xi), dst[H64, :TB], ALU.mult, ALU.add, [rk_, 'PVX', dkey], [dkey])
                        CP('pool', carX[:, xi:xi + 1], rb[:, TB:TB + 1], [rk_], ['carX'])

                    fmx(sW, kW, 0, 24, tmpF[0], ('tF', 0))
                    ACT(lora[:, 0, :TB], tmpF[0][H64, :TB], AF.Tanh, [('tF', 0)], ['lora'])
                    fmx(sW, kW, 64, 25, tmpF[1], ('tF', 1))
                    CP('dve', lora[:, 1, :TB], tmpF[1][H64, :TB], [('tF', 1)], ['lora'])
                    for hg in range(2):
                        for hi in range(4):
                            h = hg * 4 + hi
                            pp, hh = hi // 2, hi % 2
                            xr, xk, xv = tmpF[0], tmpF[1], tmpF[2]
                            kr_, kk_, kv_ = ('tF', 0), ('tF', 1), ('tF', 2)
                            t0_, t1_, t2_, t3_, t4_, t5_ = tmpF[3], tmpF[4], tmpF[5], tmpF[6], tmpF[7], tmpF[8]
                            k0_, k1_, k2_, k3_, k4_, k5_ = [('tF', i) for i in range(3, 9)]
                            a16, b16, k16, r16, v16, rk16, az16 = [tmpH[q * 4 + hi] for q in range(7)]
                            ka16, kb16, kk16, kr16, kv16, krk16, kaz16 = [('tH', q * 4 + hi) for q in range(7)]
                            fmx(sR, kR, h * 64, h, xr, kr_)
                            fmx(sK, kK, h * 64, 8 + h, xk, kk_)
                            fmx(sV, kV, h * 64, 16 + h, xv, kv_)
                            ps, pk = psq(4)
                            if h < 6:
                                proj_fm(sW, kW, 128 + h * 64, 64, ps[H64, :TB], pk)
                            else:
                                if sZ is None:
                                    sZ, kZ = slabZ, 'slabZ'
                                    DMA('sp', slabZ[:, :, :], wb_in[l].rearrange("(k p) c -> p k c", p=128)[:, :, 2048:2176], wkeys.get(('wb_in', l), []), ['slabZ'])
                                proj_fm(sZ, kZ, (h - 6) * 64, 64, ps[H64, :TB], pk)
                            ACT(az16[H64, :TB], ps[H64, :TB], AF.Silu, pk + ['PVX'], [kaz16], bias=pvx(X_BZ + h))
                            ps, pk = psq(4)
                            MM(ps[H64, :TB], upw[:, 0, h * 64:(h + 1) * 64], lora[:, 0, :TB], ['upw', 'lora'], pk)
                            ACT(t0_[H64, :TB], ps[H64, :TB], AF.Sigmoid, pk + ['PVX'], [k0_], bias=pvx(X_W0 + h))
                            ps, pk = psq(4)
                            MM(ps[H64, :TB], upw[:, 1, h * 64:(h + 1) * 64], lora[:, 1, :TB], ['upw', 'lora'], pk)
                            ACT(t1_[H64, :TB], ps[H64, :TB], AF.Sigmoid, pk + ['PVX'], [k1_], bias=pvx(X_A0 + h))
                            SCAN(t2_[H64, :TB], m01[H64, :TB], t0_[H64, :TB], 0.0, ALU.mult, ALU.add, ['cstF', k0_], [k2_])
                            TT('dve', t3_[H64, :TB], t2_[H64, :TB], t0_[H64, :TB], ALU.subtract, [k2_, k0_], [k3_])
                            eg = egP[pp][:, hh, :]
                            keg = ('eg', pp)
                            ACT(eg[:, :TB], t2_[H64, :TB], AF.Exp, [k2_], [keg], scale=-DECAY_C)
                            ACT(t4_[H64, :TB], t2_[H64, :TB], AF.Exp, [k2_], [k4_], scale=DECAY_C)
                            ACT(t3_[H64, :TB], t3_[H64, :TB], AF.Exp, [k3_], [k3_], scale=-DECAY_C)
                            TS('dve', t5_[H64, :TB], xk[H64, :TB], pvx(X_KK + h), ALU.mult, [kk_, 'PVX'], [k5_])
                            ACT(sqh[:, :TB], t5_[H64, :TB], AF.Square, [k5_], ['sqh'])
                            ps, pk = psq(4)
                            MM(ps[H64, :TB], onesB[H64, 0:64], sqh[:, :TB], ['cstB', 'sqh'], pk)
                            ACT(t0_[H64, :TB], ps[H64, :TB], AF.Sqrt, pk, [k0_])
                            TS('dve', t0_[H64, :TB], t0_[H64, :TB], 1e-12, ALU.max, [k0_], [k0_])
                            RECIP(t0_[H64, :TB], t0_[H64, :TB], [k0_], [k0_])
                            TT('dve', t5_[H64, :TB], t5_[H64, :TB], t0_[H64, :TB], ALU.mult, [k5_, k0_], [k5_])
                            TS('dve', t2_[H64, :TB], t1_[H64, :TB], -1.0, ALU.add, [k1_, 'PVX'], [k2_], s2=pvx(X_KA + h), op1=ALU.mult)
                            STT(t2_[H64, :TB], t2_[H64, :TB], 1.0, xk[H64, :TB], ALU.add, ALU.mult, [k2_, kk_], [k2_])
                            TT('dve', b16[H64, :TB], t5_[H64, :TB], t3_[H64, :TB], ALU.mult, [k5_, k3_], [kb16])
                            TT('dve', t0_[H64, :TB], t1_[H64, :TB], t5_[H64, :TB], ALU.mult, [k1_, k5_], [k0_])
                            STT(a16[H64, :TB], t0_[H64, :TB], -1.0, t4_[H64, :TB], ALU.mult, ALU.mult, [k0_, k4_], [ka16])
                            TT('pool', k16[H64, :TB], t2_[H64, :TB], t4_[H64, :TB], ALU.mult, [k2_, k4_], [kk16])
                            TT('pool', r16[H64, :TB], xr[H64, :TB], eg[:, :TB], ALU.mult, [kr_, keg], [kr16])
                            CP('pool', v16[H64, :TB], xv[H64, :TB], [kv_], [kv16])
                            TT('dve', t0_[H64, :TB], xr[H64, :TB], t2_[H64, :TB], ALU.mult, [kr_, k2_], [k0_])
                            TS('dve', rk16[H64, :TB], t0_[H64, :TB], pvx(X_RK + h), ALU.mult, [k0_, 'PVX'], [krk16])

                        TH = lambda q, hi: tmpH[q * 4 + hi]
                        KH = lambda q, hi: ('tH', q * 4 + hi)
                        mSU2, mSL2, mIU2, mID2 = [cm2[:, i, :] for i in range(4)]
                        f2 = lambda t: t[:, :, :].rearrange("p a b -> p (a b)")
                        for c in range(NCH):
                            cs = slice(c * 64, (c + 1) * 64)
                            for pp in range(2):
                                R_ = RS[pp]
                                rk = lambda nm, pp=pp: ('rs', pp, nm)

                                def score(ql, qr, mask, dst):
                                    ps, pk = psq(1)
                                    for hh in range(2):
                                        hi = pp * 2 + hh
                                        MM(ps[H64, hh * 64:(hh + 1) * 64], TH(ql, hi)[H64, cs], TH(qr, hi)[H64, cs], [KH(ql, hi), KH(qr, hi)], pk)
                                    TT('dve', f2(R_[dst]), ps[H64, :], mask, ALU.mult, pk + ['cm2'], [rk(dst)])

                                score(0, 1, mSU2, 'Qa')
                                score(1, 0, mSL2, 'Pa')
                                score(2, 1, mSU2, 'Abk')
                                score(0, 3, mIU2, 'Ara')
                                score(2, 3, mIU2, 'Ark')
                                TT('pool', f2(R_['Za']), f2(R_['Qa']), mID2, ALU.add, [rk('Qa'), 'cm2'], [rk('Za')])
                                for q, dst in ((4, 'Vst'), (0, 'ast'), (2, 'kst'), (6, 'azst')):
                                    pt, ptk = pst()
                                    for hh in range(2):
                                        hi = pp * 2 + hh
                                        TR(pt[H64, hh * 64:(hh + 1) * 64], TH(q, hi)[H64, cs], idB[H64, 0:64], [KH(q, hi), 'cstB'], ptk)
                                    CP('act', f2(R_[dst]), pt[H64, :], ptk, [rk(dst)])
                                ps, pk = psq(1)
                                for hh in range(2):
                                    hi = pp * 2 + hh
                                    MM(ps[H64, hh:hh + 1], TH(5, hi)[H64, cs], onesB[H64, 0:1], [KH(5, hi), 'cstB'], pk)
                                CP('act', R_['sc'][:, 0, :], ps[H64, 0:2], pk, [rk('sc')])
                            qc, pc, zc = 'Qa', 'Pa', 'Za'
                            flip = {'Qa': 'Qb', 'Qb': 'Qa', 'Pa': 'Pb', 'Pb': 'Pa', 'Za': 'Zb', 'Zb': 'Za'}
                            for lev in range(1, 7):
                                qn, pn, zn = flip[qc], flip[pc], flip[zc]
                                for pp in range(2):
                                    R_ = RS[pp]
                                    rk = lambda nm, pp=pp: ('rs', pp, nm)

                                    def mm2(lk, rk2):
                                        ps, pk = psq(1)
                                        for hh in range(2):
                                            MM(ps[H64, hh * 64:(hh + 1) * 64], R_[lk][:, hh, :], R_[rk2][:, hh, :], [rk(lk), rk(rk2)], pk)
                                        return ps, pk

                                    if lev >= 2:
                                        ps, pk = mm2(pc, zc)
                                        TT('dve', f2(R_[zn]), ps[H64, :], f2(R_[zc]), ALU.add, pk + [rk(zc)], [rk(zn)])
                                    if lev <= 5:
                                        ps, pk = mm2(qc, pc)
                                        CP('act', f2(R_[pn]), ps[H64, :], pk, [rk(pn)])
                                    if lev <= 4:
                                        ps, pk = mm2(pc, qc)
                                        CP('act', f2(R_[qn]), ps[H64, :], pk, [rk(qn)])
                                if lev >= 2:
                                    zc = zn
                                if lev <= 5:
                                    pc = pn
                                if lev <= 4:
                                    qc = qn
                            Zf = zc
                            for pp in range(2):
                                R_ = RS[pp]
                                rk = lambda nm, pp=pp: ('rs', pp, nm)
                                gp = hg * 2 + pp
                                ps, pk = psq(1)
                                for hh in range(2):
                                    hi = pp * 2 + hh
                                    o_ = ps[H64, hh * 64:(hh + 1) * 64]
                                    MM(o_, TH(1, hi)[H64, cs], Hb[gp][:, hh, :], [KH(1, hi), ('Hb', gp)], pk, start=True, stop=False)
                                    MM(o_, R_['Abk'][:, hh, :], R_['Vst'][:, hh, :], [rk('Abk'), rk('Vst')], pk, start=False, stop=True)
                                CP('act', f2(R_['Xb']), ps[H64, :], pk, [rk('Xb')])
                            for pp in range(2):
                                R_ = RS[pp]
                                rk = lambda nm, pp=pp: ('rs', pp, nm)
                                ps, pk = psq(1)
                                for hh in range(2):
                                    MM(ps[H64, hh * 64:(hh + 1) * 64], R_[Zf][:, hh, :], R_['Xb'][:, hh, :], [rk(Zf), rk('Xb')], pk)
                                CP('act', f2(R_['Ub']), ps[H64, :], pk, [rk('Ub')])
                            for pp in range(2):
                                R_ = RS[pp]
                                rk = lambda nm, pp=pp: ('rs', pp, nm)
                                gp = hg * 2 + pp
                                keg = ('eg', pp)
                                psy, pky = psq(1)
                                psh, pkh = psq(1)
                                for hh in range(2):
                                    hi = pp * 2 + hh
                                    o_ = psy[H64, hh * 64:(hh + 1) * 64]
                                    MM(o_, TH(3, hi)[H64, cs], Hb[gp][:, hh, :], [KH(3, hi), ('Hb', gp)], pky, start=True, stop=False)
                                    MM(o_, R_['Ara'][:, hh, :], R_['Ub'][:, hh, :], [rk('Ara'), rk('Ub')], pky, start=False, stop=False)
                                    MM(o_, R_['Ark'][:, hh, :], R_['Vst'][:, hh, :], [rk('Ark'), rk('Vst')], pky, start=False, stop=True)
                                for hh in range(2):
                                    o_ = psh[H64, hh * 64:(hh + 1) * 64]
                                    MM(o_, R_['ast'][:, hh, :], R_['Ub'][:, hh, :], [rk('ast'), rk('Ub')], pkh, start=True, stop=False)
                                    MM(o_, R_['kst'][:, hh, :], R_['Vst'][:, hh, :], [rk('kst'), rk('Vst')], pkh, start=False, stop=True)
                                TT('dve', f2(R_['htmp']), psh[H64, :], f2(Hf[gp]), ALU.add, pkh + [('Hf', gp)], [rk('htmp')])
                                gam = egP[pp][:, :, c * 64 + 63:c * 64 + 64].broadcast_to([64, 2, 64])
                                TT('pool', Hf[gp][:, :, :], R_['htmp'][:, :, :], gam, ALU.mult, [rk('htmp'), keg], [('Hf', gp)])
                                TT('dve', Hb[gp][:, :, :], R_['htmp'][:, :, :], gam, ALU.mult, [rk('htmp'), keg], [('Hb', gp)])
                                e1, e2, e3, scr = R_['e1'], R_['e2'], R_['e3'], R_['sc']
                                y3 = psy[H64, :].rearrange("p (a b) -> p a b", a=2)
                                bc2 = lambda i: scr[:, i, :].unsqueeze(2).broadcast_to([64, 2, 64])
                                S.op('dve', lambda e, o=scr[:, 1, :], i=y3: e.reduce_sum(out=o, in_=i, axis=mybir.AxisListType.X), pky, [rk('sc')])
                                STT(e2[:, :, :], bc2(1), -1.0 / 64, y3, ALU.mult, ALU.add, pky + [rk('sc')], [rk('e2')])
                                ACT(f2(e1), f2(e2), AF.Square, [rk('e2')], [rk('e1')])
                                S.op('dve', lambda e, o=scr[:, 2, :], i=e1[:, :, :]: e.reduce_sum(out=o, in_=i, axis=mybir.AxisListType.X), [rk('e1')], [rk('sc')])
                                TS('dve', scr[:, 3, :], scr[:, 2, :], 1.0 / 64, ALU.mult, [rk('sc')], [rk('sc')], s2=64e-5, op1=ALU.add)
                                ACT(scr[:, 3, :], scr[:, 3, :], AF.Sqrt, [rk('sc')], [rk('sc')])
                                RECIP(scr[:, 4, :], scr[:, 3, :], [rk('sc')], [rk('sc')])
                                TT('dve', e3[:, :, :], e2[:, :, :], bc2(4), ALU.mult, [rk('e2'), rk('sc')], [rk('e3')])
                                TT('dve', e3[:, :, :], e3[:, :, :], gnx[:, 2 * gp:2 * gp + 2, :], ALU.mult, [rk('e3'), 'gnx'], [rk('e3')])
                                TT('dve', e3[:, :, :], e3[:, :, :], gnx[:, 8 + 2 * gp:8 + 2 * gp + 2, :], ALU.add, [rk('e3'), 'gnx'], [rk('e3')])
                                TT('pool', e1[:, :, :], R_['Vst'][:, :, :], bc2(0), ALU.mult, [rk('Vst'), rk('sc')], [rk('e1')])
                                TT('dve', e3[:, :, :], e3[:, :, :], e1[:, :, :], ALU.add, [rk('e3'), rk('e1')], [rk('e3')])
                                TT('dve', R_['ysa'][:, :, :], e3[:, :, :], R_['azst'][:, :, :], ALU.mult, [rk('e3'), rk('azst')], [rk('ysa')])
                                pt, ptk = pst()
                                for hh in range(2):
                                    TR(pt[hh * 64:(hh + 1) * 64, 0:64], R_['ysa'][:, hh, :], idB[H64, 0:64], [rk('ysa'), 'cstB'], ptk)
                                CP('act', ysT[:, gp, c * 64:c * 64 + 64], pt[:, 0:64], ptk, [('ysT', gp)])

                    S.enabled = 'B' in cfg.get('ph', 'ABCG')
                    sBQ, kBQ = win_slab(l, C_BQ, 512)
                    sBK, kBK = win_slab(l, C_BK, 512)
                    qT = [tmpH[i] for i in range(4)]
                    kTm = [tmpH[4 + i] for i in range(4)]
                    oz = [tmpH[8 + i] for i in range(4)]
                    ktg = [tmpH[12 + i] for i in range(8)]
                    ysb = [tmpH[20 + i] for i in range(4)]
                    for fc in range(8):
                        slab, skey = (sBQ, kBQ) if fc < 4 else (sBK, kBK)
                        rb = rawB[fc % 2]
                        rk_ = ('rawB', fc % 2)
                        ps, pk = psq(4)
                        proj_fm(slab, skey, (fc % 4) * 128, 128, ps[:, :TB], pk)
                        ACT(rb[:, 3:3 + TB], ps[:, :TB], AF.Identity, pk + ['PV'], [rk_], bias=pv(PV_BQK + fc))
                        CP('pool', rb[:, 0:3], carB[:, fc, :], ['carB'], [rk_])
                        tq = tmpF[fc % 2]
                        tk_ = ('tF', fc % 2)
                        ACT(tq[:, :TB], rb[:, 3:3 + TB], AF.Identity, [rk_, 'PV'], [tk_], scale=pv(PV_CW + 24 + fc), bias=pv(PV_CB + fc))
                        for jj in range(3):
                            STT(tq[:, :TB], rb[:, jj:jj + TB], pv(PV_CW + 8 * jj + fc), tq[:, :TB], ALU.mult, ALU.add, [rk_, 'PV', tk_], [tk_])
                        CP('pool', carB[:, fc, :], rb[:, TB:TB + 3], [rk_], ['carB'])
                        dst = qT[fc] if fc < 4 else kTm[fc - 4]
                        ACT(dst[:, :TB], tq[:, :TB], AF.Silu, [tk_], [('tH', fc)])
                    iv, sp_, fn_, bn_, x_, g_, t1_, t3_ = [tmpF[i] for i in range(8)]
                    K = [('tF', i) for i in range(8)]
                    ps, pk = psq(4)
                    for kc in range(KC):
                        MM(ps[:, :TB], wrep[:, 0, kc, :], xb[:, kc, :TB], ['wrep', KXB], pk, start=(kc == 0), stop=(kc == KC - 1))
                    ACT(iv[:, :TB], ps[:, :TB], AF.Identity, pk + ['PV'], [K[0]], bias=pv(PV_BI))
                    ps, pk = psq(4)
                    for kc in range(KC):
                        MM(ps[:, :TB], wrep[:, 1, kc, :], xb[:, kc, :TB], ['wrep', KXB], pk, start=(kc == 0), stop=(kc == KC - 1))
                    ACT(sp_[:, :TB], ps[:, :TB], AF.Exp, pk + ['PV'], [K[1]], bias=pv(PV_NBF), scale=-1.0)
                    ACT(sp_[:, :TB], sp_[:, :TB], AF.Ln, [K[1]], [K[1]], bias=1.0)
                    SCAN(fn_[:, :TB], sp_[:, :TB], sp_[:, :TB], carM[:, 0:1], ALU.add, ALU.bypass, [K[1], 'carM'], [K[2]])
                    SCAN(bn_[:, :TB], m01[:, :TB], sp_[:, :TB], 0.0, ALU.mult, ALU.add, ['cstF', K[1]], [K[3]])
                    TT('dve', x_[:, :TB], iv[:, :TB], fn_[:, :TB], ALU.add, [K[0], K[2]], [K[4]])
                    SCAN(g_[:, :TB], x_[:, :TB], x_[:, :TB], carM[:, 1:2], ALU.max, ALU.bypass, [K[4], 'carM'], [K[5]])
                    TT('dve', mbuf[:, 64:64 + TB], g_[:, :TB], fn_[:, :TB], ALU.subtract, [K[5], K[2]], ['mbuf'])
                    CP('pool', carM[:, 0:1], fn_[:, TB - 1:TB], [K[2]], ['carM'])
                    CP('pool', carM[:, 1:2], g_[:, TB - 1:TB], [K[5]], ['carM'])
                    mcur = mbuf[:, 64:64 + TB]
                    v3 = lambda ap: ap.rearrange("p (c t) -> p c t", t=64)
                    bc = lambda ap: v3(ap)[:, :, 63:64].broadcast_to([128, NCH, 64])
                    Ra, Rsc, Rcl, Rg, RgL = x_, g_, fn_, iv, t3_
                    STT(t1_[:, :TB], bn_[:, :TB], -1.0, mcur, ALU.mult, ALU.subtract, [K[3], 'mbuf'], [K[6]])
                    ACT(Ra[:, :TB], t1_[:, :TB], AF.Exp, [K[6]], [K[4]])
                    TT('dve', v3(t1_[:, :TB]), v3(t1_[:, :TB]), bc(mbuf[:, 0:TB]), ALU.add, [K[6], 'mbuf'], [K[6]])
                    ACT(Rsc[:, :TB], t1_[:, :TB], AF.Exp, [K[6]], [K[5]])
                    ACT(Rcl[:, :TB], mcur, AF.Exp, ['mbuf'], [K[2]], scale=-1.0)
                    TT('dve', t3_[:, :TB], iv[:, :TB], bn_[:, :TB], ALU.add, [K[0], K[3]], [K[7]])
                    ACT(Rg[:, :TB], t3_[:, :TB], AF.Exp, [K[7]], [K[0]], bias=LN_C)
                    TT('dve', v3(t3_[:, :TB]), v3(t3_[:, :TB]), bc(bn_[:, :TB]), ALU.subtract, [K[7], K[3]], [K[7]])
                    TT('dve', v3(t3_[:, :TB]), v3(t3_[:, :TB]), bc(mcur), ALU.subtract, [K[7], 'mbuf'], [K[7]])
                    ACT(RgL[:, :TB], t3_[:, :TB], AF.Exp, [K[7]], [K[7]], bias=LN_C)
                    RQ = [(Ra, K[4]), (Rsc, K[5]), (Rcl, K[2]), (Rg, K[0]), (RgL, K[7])]
                    for tt in range(NTL):
                        ps, pk = psq(4)
                        for qi in range(4):
                            MM(ps[0:TP, qi * 128:(qi + 1) * 128], RQ[qi][0][:, tt * TP:(tt + 1) * TP], idF, [RQ[qi][1], 'cstF'], pk)
                        CP('dve', tokS[0:TP, tt, 0:4, :], ps[0:TP, :].rearrange("p (q h r) -> p q h r", q=4, h=4)[:, :, :, 0], pk, [('tokS', tt)])
                        ps, pk = psq(1)
                        MM(ps[0:TP, :], RQ[4][0][:, tt * TP:(tt + 1) * TP], idF, [RQ[4][1], 'cstF'], pk)
                        CP('dve', tokS[0:TP, tt, 4, :], ps[0:TP, :].rearrange("p (h r) -> p h r", h=4)[:, :, 0], pk, [('tokS', tt)])
                        for jj in range(CPT):
                            TS('dve', tokS2[0:TP, tt, jj, :], tokS[0:TP, tt, 4, :], rmask[0:TP, jj:jj + 1], ALU.mult, [('tokS', tt), 'rmask'], [('tokS2', tt)])
                    ps, pk = psq(1)
                    for h in range(4):
                        MM(ps[:, h * NCH:(h + 1) * NCH], cstF[:, 768 + 128 * h:768 + 128 * (h + 1)],
                           v3(Rsc[:, :TB])[:, :, 63], ['cstF', K[5]], pk)
                    CP('dve', csB[:, :, 0:NCH], ps[:, 0:4 * NCH].rearrange("p (h c) -> p h c", h=4), pk, ['csB'])
                    CP('pool', mbuf[:, 63:64], mbuf[:, 63 + TB:64 + TB], ['mbuf'], ['mbuf'])
                    sV2, kV2 = win_slab(l, C_BV, 512)
                    sO, kO = win_slab(l, C_BO, 512)
                    for tt in range(NTL):
                        ps, pk = psq(4)
                        proj_tm(sV2, kV2, tt, ps[0:TP, :], pk, 0)
                        CP('act', vaug[0:TP, tt, :, 0:128], ps[0:TP, :].rearrange("p (h d) -> p h d", h=4), pk, [('vaug', tt)])
                    for tt in range(NTL):
                        ps, pk = psq(4)
                        proj_tm(sO, kO, tt, ps[0:TP, :], pk, 512)
                        ACT(oz[tt][0:TP, :], ps[0:TP, :], AF.Sigmoid, pk, [('tH', 8 + tt)])
                    sZ2, kZ2 = win_slab(l, C_BZ, 512)
                    for tt in range(NTL):
                        ps, pk = psq(4)
                        proj_tm(sZ2, kZ2, tt, ps[0:TP, :], pk, 1024)
                        ACT(tmpF[8][0:TP, :], ps[0:TP, :], AF.Silu, pk, [('tF', 8)])
                        TT('dve', oz[tt][0:TP, :], oz[tt][0:TP, :], tmpF[8][0:TP, :], ALU.mult, [('tH', 8 + tt), ('tF', 8)], [('tH', 8 + tt)])
                    for tt in range(NTL):
                        for h in range(4):
                            pt, ptk = pst()
                            TR(pt[0:TP, :], kTm[h][:, tt * TP:(tt + 1) * TP], idB, [('tH', 4 + h), 'cstB'], ptk)
                            for jj in range(CPT):
                                TS('dve', ktg[2 * tt + jj][0:TP, h * 128:(h + 1) * 128], pt[0:TP, :], tokS2[0:TP, tt, jj, h:h + 1], ALU.mult,
                                   ptk + [('tokS2', tt)], [('tH', 12 + 2 * tt + jj)])
                    for tt in range(NTL):
                        tsl = slice(tt * TP, (tt + 1) * TP)
                        for h in range(4):
                            hsl = slice(h * 128, (h + 1) * 128)
                            wt_ = mlh[h % 2]
                            kwt = ('mlh', h % 2)
                            ps, pk = psq(1)
                            MM(ps[0:TP, 0:TP], kTm[h][:, tsl], qT[h][:, tsl], [('tH', 4 + h), ('tH', h)], pk)
                            STT(wt_[0:TP, 0:TP], ps[0:TP, 0:TP], tokS[0:TP, tt, 3, h:h + 1], mIU[0:TP, 0:TP], ALU.mult, ALU.mult,
                                pk + [('tokS', tt), 'cstB'], [kwt])
                            psn, pkn = psq(2)
                            MM(psn[0:TP, 0:129], wt_[0:TP, 0:TP], vaug[0:TP, tt, h, :], [kwt, ('vaug', tt), 'vaug'], pkn)
                            psi, pki = psq(2)
                            for jj in range(CPT):
                                c = tt * CPT + jj
                                js = slice(jj * 64, jj * 64 + 64)
                                MM(psi[js, 0:129], qT[h][:, tt * TP + jj * 64:tt * TP + jj * 64 + 64], CTb[h][:], [('tH', h), ('CTb', h)], pki)
                                pss, pks = psq(2)
                                MM(pss[:, 0:129], ktg[2 * tt + jj][0:TP, hsl], vaug[0:TP, tt, h, :], [('tH', 12 + 2 * tt + jj), ('vaug', tt), 'vaug'], pks)
                                STT(CTf[h][:], CTf[h][:], csB[:, h, c:c + 1], pss[:, 0:129], ALU.mult, ALU.add, [('CTf', h), 'csB'] + pks, [('CTf', h)])
                                CP('pool', CTb[h][:], CTf[h][:], [('CTf', h)], [('CTb', h)])
                            m1, m2 = mlt[(2 * h) % 4], mlt[(2 * h + 1) % 4]
                            km1, km2 = ('mlt', (2 * h) % 4), ('mlt', (2 * h + 1) % 4)
                            sc_ = mls[:, h % 8, :] if False else mls[:, (tt * 4 + h) % 8, :]
                            ksc = ('mls', (tt * 4 + h) % 8)
                            TS('dve', m1[0:TP, 0:129], psi[0:TP, 0:129], tokS[0:TP, tt, 1, h:h + 1], ALU.mult, pki + [('tokS', tt)], [km1])
                            STT(m2[0:TP, 0:129], psn[0:TP, 0:129], tokS[0:TP, tt, 0, h:h + 1], m1[0:TP, 0:129], ALU.mult, ALU.add,
                                pkn + [('tokS', tt), km1], [km2])
                            ACT(sc_[0:TP, 0:1], m2[0:TP, 128:129], AF.Abs, [km2], [ksc])
                            TS('dve', sc_[0:TP, 0:1], sc_[0:TP, 0:1], tokS[0:TP, tt, 2, h:h + 1], ALU.max, [ksc, ('tokS', tt)], [ksc])
                            RECIP(sc_[0:TP, 1:2], sc_[0:TP, 0:1], [ksc], [ksc])
                            TS('dve', m1[0:TP, 0:128], m2[0:TP, 0:128], sc_[0:TP, 1:2], ALU.mult, [km2, ksc], [km1, ksc],
                               s2=0.0, op1=ALU.add, accum=sc_[0:TP, 2:3])
                            TS('dve', sc_[0:TP, 2:3], sc_[0:TP, 2:3], -1.0 / 128, ALU.mult, [ksc], [ksc])
                            TS('dve', m1[0:TP, 0:128], m1[0:TP, 0:128], sc_[0:TP, 2:3], ALU.add, [km1, ksc], [km1])
                            ACT(m2[0:TP, 0:128], m1[0:TP, 0:128], AF.Square, [km1], [km2, ksc], accum=sc_[0:TP, 3:4])
                            TS('dve', sc_[0:TP, 4:5], sc_[0:TP, 3:4], 1.0 / 128, ALU.mult, [ksc], [ksc], s2=1e-6, op1=ALU.add)
                            ACT(sc_[0:TP, 4:5], sc_[0:TP, 4:5], AF.Sqrt, [ksc], [ksc])
                            RECIP(sc_[0:TP, 5:6], sc_[0:TP, 4:5], [ksc], [ksc])
                            STT(m2[0:TP, 0:128], m1[0:TP, 0:128], sc_[0:TP, 5:6], hnG[0:TP, hsl], ALU.mult, ALU.mult, [km1, ksc, 'hnG'], [km2])
                            TT('dve', ysb[tt][0:TP, hsl], m2[0:TP, 0:128], oz[tt][0:TP, hsl], ALU.mult, [km2, ('tH', 8 + tt)], [('tH', 20 + tt)])
                        for h in range(4):
                            pt, ptk = pst()
                            TR(pt[:, 0:TP], ysb[tt][0:TP, h * 128:(h + 1) * 128], idB[0:TP, 0:TP], [('tH', 20 + tt), 'cstB'], ptk)
                            CP('act', ysT[:, 4 + h, tsl], pt[:, 0:TP], ptk, [('ysT', 4 + h)])

                    S.enabled = ('C' in cfg.get('ph', 'ABCG')) or ('c' in cfg.get('ph', 'ABCG'))
                    sQ, kQ = win_slab(l, C_CQ, 512)
                    sKc, kKc = win_slab(l, C_CK, 512)
                    qs16 = [tmpH[i] for i in range(8)]
                    nq16 = [tmpH[8 + i] for i in range(8)]
                    kf16 = [tmpH[16 + i] for i in range(4)]
                    for h in range(8):
                        oth = slice(64 - (h % 2) * 64, 128 - (h % 2) * 64)
                        MSET('pool', qs16[h][oth, :TB], 0.0, [('tH', h)])
                        MSET('pool', nq16[h][oth, :TB], 0.0, [('tH', 8 + h)])
                    for p in range(4):
                        ps, pk = psq(4)
                        proj_fm(sQ, kQ, p * 128, 128, ps[:, :TB], pk)
                        for hh in range(2):
                            h = 2 * p + hh
                            hs = slice(hh * 64, hh * 64 + 64)
                            ACT(qs16[h][hs, :TB], ps[hs, :TB], AF.Identity, pk + ['PV'], [('tH', h)], bias=PV[hs, l, PV_BCQ + p:PV_BCQ + p + 1], scale=0.125)
                            ACT(nq16[h][hs, :TB], ps[hs, :TB], AF.Identity, pk + ['PV'], [('tH', 8 + h)], bias=PV[hs, l, PV_NBCQ + p:PV_NBCQ + p + 1], scale=-0.125)
                    for p in range(4):
                        ps, pk = psq(4)
                        proj_fm(sKc, kKc, p * 128, 128, ps[:, :TB], pk)
                        ACT(kf16[p][:, :TB], ps[:, :TB], AF.Identity, pk + ['PV'], [('tH', 16 + p)], bias=pv(PV_BCK + p))
                        DMA('pool', X['kTd'][p, :, P + t0:P + t0 + TB], kf16[p][:, :TB], [('tH', 16 + p)], [('kTd', si)])
                    for tt in range(NTL):
                        ps, pk = psq(4)
                        proj_tm(sKc, kKc, tt, ps[0:TP, :], pk, 1536)
                        CP('act', tmpF[tt % 2][0:TP, :], ps[0:TP, :], pk, [('tF', tt % 2)])
                        DMA('pool', O['sbk'][l, t0 + tt * TP:t0 + (tt + 1) * TP, :], tmpF[tt % 2][0:TP, :], [('tF', tt % 2)], [], is_out=True)
                    sVc, kVc = win_slab(l, C_CV, 512)
                    for tt in range(NTL):
                        ps, pk = psq(4)
                        proj_tm(sVc, kVc, tt, ps[0:TP, :], pk, 2048)
                        CP('act', tmpF[2 + tt % 2][0:TP, :], ps[0:TP, :], pk, [('tF', 2 + tt % 2)])
                        CP('act', tmpH[20 + tt % 2][0:TP, :], ps[0:TP, :], pk, [('tH', 20 + tt % 2)])
                        DMA('pool', O['sbv'][l, t0 + tt * TP:t0 + (tt + 1) * TP, :], tmpF[2 + tt % 2][0:TP, :], [('tF', 2 + tt % 2)], [], is_out=True)
                        DMA('pool', X['vd'][P + t0 + tt * TP:P + t0 + (tt + 1) * TP, :], tmpH[20 + tt % 2][0:TP, :], [('tH', 20 + tt % 2)], [('vd', si)])
                    sZc, kZc = win_slab(l, C_CZ, 512)
                    S.enabled = 'C' in cfg.get('ph', 'ABCG')
                    q0 = P + t0
                    kend = q0 + TB
                    nkb = (kend + 127) // 128
                    for half in range(2):
                        accs = [psbank(4 + i) for i in range(2)]
                        SP = [tmpH[22 + i] for i in range(4)]
                        for ki, kb in enumerate(reversed(range(nkb))):
                            k0 = kb * 128
                            ks = min(128, kend - k0)
                            kt_, vt_ = kTblk[ki % 2], vblk[ki % 2]
                            kkt, kvt = ('kTblk', ki % 2), ('vblk', ki % 2)
                            DMA('sp', kt_[:, :, 0:ks], X['kTd'][:, :, k0:k0 + ks].rearrange("c p s -> p c s"), [('kTd', si)], [kkt])
                            DMA('sp', vt_[0:ks, :], X['vd'][k0:k0 + ks, :], [('vd', si)], [kvt])
                            masked = (k0 + ks > q0)
                            mk = attB[0:ks, (k0 - q0) // 128, 0:TB] if masked else None
                            for hi in range(4):
                                h = half * 4 + hi
                                p = h // 2
                                hs = slice((h % 2) * 64, (h % 2) * 64 + 64)
                                psz, pkz = psq(4)
                                MM(psz[0:ks, :TB], kt_[:, p, 0:ks], qs16[h][:, :TB], [kkt, ('tH', h)], pkz)
                                ef = tmpF[4 + hi % 4]
                                kef = ('tF', 4 + hi % 4)
                                sp16 = tmpH[26 + hi % 2]
                                ksp = ('tH', 26 + hi % 2)
                                A16 = mlh2[hi % 2]
                                kA = ('mlh2', hi % 2)
                                ACT(ef[0:ks, :TB], psz[0:ks, :TB], AF.Exp, pkz, [kef])
                                ACT(sp16[0:ks, :TB], ef[0:ks, :TB], AF.Ln, [kef], [ksp], bias=1.0)
                                if masked:
                                    TT('pool', sp16[0:ks, :TB], sp16[0:ks, :TB], mk, ALU.mult, [ksp, 'attB'], [ksp])
                                psa, pka = psq(4)
                                MM(psa[0:ks, :TB], triB[0:ks, 0:ks], sp16[0:ks, :TB], ['cstB', ksp], pka, start=True, stop=False)
                                if ki > 0:
                                    MM(psa[0:ks, :TB], onesB[:, 0:ks], SP[hi][:, :TB], ['cstB', ('tH', 22 + hi)], pka, start=False, stop=False)
                                MM(psa[0:ks, :TB], kt_[:, p, 0:ks], nq16[h][:, :TB], [kkt, ('tH', 8 + h)], pka, start=False, stop=True)
                                ACT(A16[0:ks, :TB], psa[0:ks, :TB], AF.Exp, pka, [kA], scale=-1.0)
                                if masked:
                                    TT('dve', A16[0:ks, :TB], A16[0:ks, :TB], mk, ALU.mult, [kA, 'attB'], [kA])
                                acc, kacc = accs[hi // 2]
                                MM(acc[hs, :TB], vt_[0:ks, h * 64:(h + 1) * 64], A16[0:ks, :TB], [kvt, kA], kacc,
                                   start=(ki == 0), stop=(ki == nkb - 1))
                                if ki == 0:
                                    if ks < 128:
                                        MSET('pool', SP[hi][:, :TB], 0.0, [('tH', 22 + hi)])
                                    CP('pool', SP[hi][0:ks, :TB], sp16[0:ks, :TB], [ksp], [('tH', 22 + hi)])
                                elif ki < nkb - 1:
                                    TT('pool', SP[hi][0:ks, :TB], SP[hi][0:ks, :TB], sp16[0:ks, :TB], ALU.add, [ksp, ('tH', 22 + hi)], [('tH', 22 + hi)])
                        for pi in range(2):
                            p = half * 2 + pi
                            ps, pk = psq(4)
                            proj_fm(sZc, kZc, p * 128, 128, ps[:, :TB], pk)
                            ACT(tmpH[21][:, :TB], ps[:, :TB], AF.Silu, pk + ['PV'], [('tH', 21)], bias=pv(PV_BCZ + p))
                            acc, kacc = accs[pi]
                            TT('dve', ysT[:, 8 + p, :TB], acc[:, :TB], tmpH[21][:, :TB], ALU.mult, kacc + [('tH', 21)], [('ysT', 8 + p)])

                    S.enabled = 'G' in cfg.get('ph', 'ABCG')
                    for n in range(3):
                        sB, kB = load_slab(wb_br[l, n].rearrange("(k p) c -> p k c", p=128), 4, 1024, ('wb_br', l))
                        for half in range(2):
                            sG, kG = win_slab(l, C_G0 + n * 1024 + half * 512, 512)
                            for f4 in range(4):
                                fc = half * 4 + f4
                                psg, pkg = psq(4)
                                proj_fm(sG, kG, f4 * 128, 128, psg[:, :TB], pkg)
                                psr, pkr = psq(4)
                                for kc in range(4):
                                    MM(psr[:, :TB], sB[:, kc, fc * 128:(fc + 1) * 128], ysT[:, 4 * n + kc, :TB], [kB, ('ysT', 4 * n + kc)], pkr,
                                       start=(kc == 0), stop=(kc == 3))
                                ACT(tmpF[8][:, :TB], psg[:, :TB], AF.Sigmoid, pkg + ['PV'], [('tF', 8)], bias=pv(PV_BG + n * 8 + fc))
                                if n == 0:
                                    TT('dve', tmpF[fc][:, :TB], tmpF[8][:, :TB], psr[:, :TB], ALU.mult, [('tF', 8)] + pkr, [('tF', fc)])
                                else:
                                    TT('dve', tmpF[8][:, :TB], tmpF[8][:, :TB], psr[:, :TB], ALU.mult, [('tF', 8)] + pkr, [('tF', 8)])
                                    if n == 1:
                                        TT('dve', tmpF[fc][:, :TB], tmpF[fc][:, :TB], tmpF[8][:, :TB], ALU.add, [('tF', fc), ('tF', 8)], [('tF', fc)])
                                    else:
                                        TT('dve', tmpH[fc][:, :TB], tmpF[fc][:, :TB], tmpF[8][:, :TB], ALU.add, [('tF', fc), ('tF', 8)], [('tH', fc)])
                    s5, k5 = psbank(4)
                    s6, k6 = psbank(5)
                    alpha = (2.0 * cfg.get('DEPTH', 4)) ** 0.25
                    for half in range(2):
                        sWo, kWo = load_slab(wb_out[l].rearrange("(k p) c -> p k c", p=128)[:, :, half * 512:(half + 1) * 512], KC, 512, ('wb_out', l))
                        for f4 in range(4):
                            fc = half * 4 + f4
                            psm, pkm = psq(4)
                            for kc in range(KC):
                                MM(psm[:, :TB], sWo[:, kc, f4 * 128:(f4 + 1) * 128], tmpH[kc][:, :TB], [kWo, ('tH', kc)], pkm,
                                   start=(kc == 0), stop=(kc == KC - 1))
                            DMA('sp', tmpF[8][:, :TB], X['xres'][fc, :, t0:t0 + TB], [KX], [('tF', 8)])
                            STT(tmpF[fc][:, :TB], tmpF[8][:, :TB], alpha, psm[:, :TB], ALU.mult, ALU.add, [('tF', 8)] + pkm, [('tF', fc)])
                    layernorm(TB, lambda fc: pv(PV_LNG + fc), lambda fc: pv(PV_LNB + fc), 1e-5)
                    for fc in range(KC):
                        CP('pool', ysT[:, fc, :TB], tmpF[fc][:, :TB], [('tF', fc)], [('ysT', fc)])
                    sPl, kPl = load_slab(wb_ple[l].rearrange("(k p) c -> p k c", p=128), 2, 1024, ('wb_ple', l))
                    last = (l == L - 1)
                    for half in range(2):
                        sPg, kPg = load_slab(wb_pg[l].rearrange("(k p) c -> p k c", p=128)[:, :, half * 512:(half + 1) * 512], KC, 512, ('wb_pg', l))
                        for f4 in range(4):
                            fc = half * 4 + f4
                            psg, pkg = psq(4)
                            for kc in range(KC):
                                MM(psg[:, :TB], sPg[:, kc, f4 * 128:(f4 + 1) * 128], ysT[:, kc, :TB], [kPg, ('ysT', kc)], pkg,
                                   start=(kc == 0), stop=(kc == KC - 1))
                            psp, pkp = psq(4)
                            for kc in range(2):
                                MM(psp[:, :TB], sPl[:, kc, fc * 128:(fc + 1) * 128], p16[:, kc, :TB], [kPl, 'p16'], pkp, start=(kc == 0), stop=(kc == 1))
                            ACT(tmpF[8][:, :TB], psg[:, :TB], AF.Sigmoid, pkg, [('tF', 8)])
                            TT('dve', tmpF[8][:, :TB], tmpF[8][:, :TB], psp[:, :TB], ALU.mult, [('tF', 8)] + pkp, [('tF', 8)])
                            TT('dve', tmpF[fc][:, :TB], tmpF[fc][:, :TB], tmpF[8][:, :TB], ALU.add, [('tF', fc), ('tF', 8)], [('tF', fc)])
                            if last:
                                DMA('pool', O['yT'][fc * 128:(fc + 1) * 128, t0:t0 + TB], tmpF[fc][:, :TB], [('tF', fc)], [], is_out=True)
                            else:
                                DMA('pool', X['xres'][fc, :, t0:t0 + TB], tmpF[fc][:, :TB], [('tF', fc)], [KX])
                                CP('pool', tmpH[16 + fc][:, :TB], tmpF[fc][:, :TB], [('tF', fc)], [('tH', 16 + fc)])
                                DMA('pool', X['xTd'][fc, :, t0:t0 + TB], tmpH[16 + fc][:, :TB], [('tH', 16 + fc)], [KXD])

                S.enabled = True
                S.budget = None
                DMA('sp', O['shift'][l], carX[:], ['carX'], [], is_out=True)
                DMA('sp', O['conv'][l], carB[:], ['carB'], [], is_out=True)
                DMA('sp', O['m'][l], mbuf[:, 63:64], ['mbuf'], [], is_out=True)
                for p in range(4):
                    DMA('sp', O['wkv'][l, p], Hf[p][:, :, :], [('Hf', p)], [], is_out=True)
                    DMA('sp', O['ct'][l, p], CTf[p][:], [('CTf', p)], [], is_out=True)

        S.emit(nc)
    return nc


N_CORES = 8
_CACHE = {}


def _seq_inputs(i, x, p, st):
    d = {f"xT{i}": np.ascontiguousarray(x.T), f"pT{i}": np.ascontiguousarray(p.transpose(0, 2, 1))}
    if st is not None:
        shift, wkv, conv, c, n, m, ck, cv = st
        L = shift.shape[0]
        d[f"shift{i}"] = np.ascontiguousarray(shift.reshape(L, 26, 64).transpose(0, 2, 1))
        d[f"wkv{i}"] = np.ascontiguousarray(wkv.reshape(L, 4, 2, 64, 64).transpose(0, 1, 4, 2, 3))
        d[f"conv{i}"] = np.ascontiguousarray(conv.reshape(L, 3, 8, 128).transpose(0, 3, 2, 1))
        d[f"ct{i}"] = np.ascontiguousarray(np.concatenate([c.transpose(0, 1, 3, 2), n[..., None]], axis=-1))
        d[f"m{i}"] = np.ascontiguousarray(np.repeat(m, 32, axis=1)[..., None])
        Pn = ck.shape[1]
        d[f"ck{i}"] = np.ascontiguousarray(ck.reshape(L, Pn, 4, 128).transpose(0, 2, 3, 1))
        d[f"cv{i}"] = np.ascontiguousarray(cv.reshape(L, Pn, 512))
    return d


def _seq_outputs(r, i, L, T):
    y = r[f"yT{i}"].T
    shift = r[f"shift_o{i}"].transpose(0, 2, 1).reshape(L, 1664)
    wkv = r[f"wkv_o{i}"].transpose(0, 1, 3, 4, 2).reshape(L, 8, 64, 64)
    conv = r[f"conv_o{i}"].transpose(0, 3, 2, 1).reshape(L, 3, 1024)
    ct = r[f"ct_o{i}"]
    c = ct[..., 0:128].transpose(0, 1, 3, 2)
    n = ct[..., 128]
    m = r[f"m_o{i}"][:, ::32, 0]
    sbk = r[f"sbk{i}"].reshape(L, T, 8, 64)
    sbv = r[f"sbv{i}"].reshape(L, T, 8, 64)
    return [y, shift, wkv, conv, c, n, m, sbk, sbv]


WNAMES = ['ln_in_g', 'ln_in_b', 'w_in', 'b_in', 'mu_a', 'w0_a', 'w_decay_up', 'a0_a', 'w_iclr_up', 'k_k', 'k_a', 'r_k',
          'gn_a_g', 'gn_a_b', 'conv_b_w', 'conv_b_b', 'hn_b_g', 'w_branch', 'w_out', 'ln_g', 'ln_b', 'w_ple', 'w_ple_gate']


def weight_inputs(w, L):
    f = lambda a: np.ascontiguousarray(np.asarray(a, dtype=np.float32))
    wd = {k: f(w[k]) for k in WNAMES}
    wd['r_k'] = wd['r_k'].reshape(L, 512)
    wd['cst'] = make_consts()
    wd['catt'] = make_att()
    col = lambda v: v.reshape(-1, 128).T
    pv = np.zeros((128, L, NPV), np.float32)
    for l in range(L):
        b = wd['b_in'][l]
        pv[:, l, PV_BA:PV_BA + 17] = col(b[0:2176])
        pv[:, l, PV_BQK:PV_BQK + 8] = col(b[C_BQ:C_BQ + 1024])
        pv[:, l, PV_BCQ:PV_BCQ + 4] = col(b[C_CQ:C_CQ + 512])
        pv[:, l, PV_BCK:PV_BCK + 4] = col(b[C_CK:C_CK + 512])
        pv[:, l, PV_BCZ:PV_BCZ + 4] = col(b[C_CZ:C_CZ + 512])
        pv[:, l, PV_BG:PV_BG + 24] = col(b[C_G0:C_G0 + 3072])
        pv[:, l, PV_BI] = np.repeat(b[C_BI:C_BI + 4], 32)
        pv[:, l, PV_NBF] = np.repeat(b[C_BF:C_BF + 4], 32)
        pv[:, l, PV_MU:PV_MU + 13] = col(wd['mu_a'][l])
        for nm, c in (('w0_a', PV_W0), ('a0_a', PV_A0), ('k_k', PV_KK), ('k_a', PV_KA), ('r_k', PV_RK)):
            pv[:, l, c:c + 4] = col(wd[nm][l])
        for j in range(4):
            pv[:, l, PV_CW + 8 * j:PV_CW + 8 * j + 8] = col(wd['conv_b_w'][l, j])
        pv[:, l, PV_CB:PV_CB + 8] = col(wd['conv_b_b'][l])
        pv[:, l, PV_LNG:PV_LNG + 8] = col(wd['ln_g'][l])
        pv[:, l, PV_LNB:PV_LNB + 8] = col(wd['ln_b'][l])
    wd['pvh'] = pv
    wd['pvg'] = np.ascontiguousarray(np.concatenate([col(wd['ln_in_g']), col(wd['ln_in_b'])], axis=1))
    gn = np.zeros((L, 64, 16, 64), np.float32)
    gn[:, :, 0:8, :] = wd['gn_a_g'].reshape(L, 1, 8, 64)
    gn[:, :, 8:16, :] = wd['gn_a_b'].reshape(L, 1, 8, 64)
    wd['gnh'] = gn
    c64 = lambda v: v.reshape(-1, 64).T
    pvx = np.zeros((64, L, NX), np.float32)
    for l in range(L):
        pvx[:, l, X_B:X_B + 26] = c64(wd['b_in'][l, 0:1664])
        pvx[:, l, X_MU:X_MU + 26] = c64(wd['mu_a'][l])
        pvx[:, l, X_BZ:X_BZ + 8] = c64(wd['b_in'][l, C_AZ:C_AZ + 512])
        for nm, c in (('w0_a', X_W0), ('a0_a', X_A0), ('k_k', X_KK), ('k_a', X_KA), ('r_k', X_RK)):
            pvx[:, l, c:c + 8] = c64(wd[nm][l])
    wd['pvx'] = pvx
    i64 = np.arange(64)
    r_, c_ = i64[:, None], i64[None, :]
    wd['cst3'] = np.ascontiguousarray(np.concatenate([np.tile(m, (1, 2)) for m in (r_ < c_, r_ > c_, r_ <= c_, r_ == c_)], axis=1).astype(np.float32))
    wd['hnh'] = np.ascontiguousarray(np.broadcast_to(wd['hn_b_g'][:, None, :], (L, 128, 512)))
    return wd


def kernel(x_prompt, x_sample, state_shift_a, state_wkv, state_conv_b, state_mlstm_c, state_mlstm_n,
           state_mlstm_m, cache_sb_k, cache_sb_v, p_prompt, p_sample, **weights):
    f = lambda a: np.ascontiguousarray(np.asarray(a, dtype=np.float32))
    x_prompt, x_sample, p_prompt, p_sample = f(x_prompt), f(x_sample), f(p_prompt), f(p_sample)
    L = p_prompt.shape[0]
    Bp, Tp = x_prompt.shape[:2]
    Bs, Ts = x_sample.shape[:2]
    Pn = cache_sb_k.shape[2]
    npc = Bp // N_CORES
    cfg = {'L': L, 'seqs': [{'T': Tp, 'P': 0}] * npc + [{'T': Ts, 'P': Pn}]}
    key = (L, Tp, Ts, Pn, npc)
    if key not in _CACHE:
        _CACHE[key] = build(cfg)
    nc = _CACHE[key]
    wd = weight_inputs(weights, L)
    sts = [f(a) for a in (state_shift_a, state_wkv, state_conv_b, state_mlstm_c, state_mlstm_n, state_mlstm_m, cache_sb_k, cache_sb_v)]
    in_maps = []
    for c in range(N_CORES):
        d = dict(wd)
        for i in range(npc):
            b = c * npc + i
            d.update(_seq_inputs(i, x_prompt[b], p_prompt[:, b], None))
        d.update(_seq_inputs(npc, x_sample[c], p_sample[:, c], [a[:, c] for a in sts]))
        in_maps.append(d)
    res = run_bass_kernel_spmd(nc, in_maps, core_ids=list(range(N_CORES)))
    pr = [[] for _ in range(9)]
    sr = [[] for _ in range(9)]
    for c in range(N_CORES):
        r = res.results[c]
        for i in range(npc):
            for k, v in enumerate(_seq_outputs(r, i, L, Tp)):
                pr[k].append(v)
        for k, v in enumerate(_seq_outputs(r, npc, L, Ts)):
            sr[k].append(v)
    outs = []
    for k in range(9):
        a = np.stack(pr[k], 0)
        b = np.stack(sr[k], 0)
        if k >= 1:
            a = np.moveaxis(a, 1, 0)
            b = np.moveaxis(b, 1, 0)
        outs.append((np.ascontiguousarray(a, dtype=np.float32), np.ascontiguousarray(b, dtype=np.float32)))
    y, sh, wkv, conv, cc, nn, mm, sbk, sbv = outs
    return (y[0], y[1], sh[0], sh[1], wkv[0], wkv[1], conv[0], conv[1], cc[0], cc[1], nn[0], nn[1], mm[0], mm[1],
            sbk[0], sbk[1], sbv[0], sbv[1])
```

```python
import contextlib
import math
import numpy as np
import concourse.bass as bass
import concourse.mybir as mybir
from concourse.bass_utils import run_bass_kernel_spmd

F32 = mybir.dt.float32
BF16 = mybir.dt.bfloat16
AF = mybir.ActivationFunctionType
ALU = mybir.AluOpType

ENG = ('pe', 'act', 'dve', 'pool', 'sp')
ROT = 16000
NDSEM = 12


class Sched:
    def __init__(self):
        self.q = {e: [] for e in ENG}
        self.cnt = {e: 0 for e in ENG}
        self.clock = {e: {} for e in ENG}
        self.lastw = {}
        self.readers = {}
        self.dcount = {'sp': 0, 'pool': 0, 'act': 0}
        self.dlast = {}
        self.out_dmas = []
        self.enabled = True
        self.budget = None

    def _collect(self, eng, reads, writes, is_dma):
        deps = []
        for k in reads:
            w = self.lastw.get(k)
            if w is not None:
                deps.append(w)
        for k in writes:
            w = self.lastw.get(k)
            if w is not None and not (eng == 'pe' and w[0] == 'eng' and w[1] == 'pe'):
                deps.append(w)
            for r in self.readers.get(k, ()):
                if not (eng == 'pe' and r[0] == 'eng' and r[1] == 'pe'):
                    deps.append(r)
        ck = self.clock[eng]
        best = {}
        for d in deps:
            src = (d[0], d[1])
            if ck.get(src, 0) >= d[2]:
                continue
            if src not in best or best[src][2] < d[2]:
                best[src] = d
        for src, d in best.items():
            ck[src] = d[2]
        return list(best.values())

    def op(self, eng, fn, reads=(), writes=()):
        if not self.enabled:
            return None
        if self.budget is not None:
            if self.budget <= 0:
                return None
            self.budget -= 1
        waits = self._collect(eng, reads, writes, False)
        self.cnt[eng] += 1
        n = self.cnt[eng]
        me = ('eng', eng, n)
        self.q[eng].append(('op', fn, waits, n))
        for k in reads:
            self.readers.setdefault(k, []).append(me)
        for k in writes:
            self.lastw[k] = me
            self.readers[k] = []
        return me

    def dma(self, eng, fn, reads=(), writes=(), is_out=False):
        if not self.enabled:
            return None
        if self.budget is not None:
            if self.budget <= 0:
                return None
            self.budget -= 1
        waits = self._collect(eng, reads, writes, True)
        i = self.dcount[eng]
        self.dcount[eng] += 1
        slot = (eng, i % NDSEM)
        prev = self.dlast.get(slot, 0)
        val = prev + 16
        ck = self.clock[eng]
        src = ('dma', slot)
        if prev > 0 and ck.get(src, 0) < prev:
            ck[src] = prev
            waits = [w for w in waits if (w[0], w[1]) != src] + [('dma', slot, prev)]
        self.dlast[slot] = val
        me = ('dma', slot, val)
        self.q[eng].append(('dma', fn, waits, slot))
        for k in reads:
            self.readers.setdefault(k, []).append(me)
        for k in writes:
            self.lastw[k] = me
            self.readers[k] = []
        if is_out:
            self.out_dmas.append(me)
        return me

    def emit(self, nc):
        with contextlib.ExitStack() as st:
            esem = {}
            for e in ENG:
                for r in range(self.cnt[e] // ROT + 1):
                    esem[(e, r)] = st.enter_context(nc.semaphore(f"s_{e}_{r}"))
            dsem = {}
            for slot in self.dlast:
                dsem[slot] = st.enter_context(nc.semaphore(f"d_{slot[0]}_{slot[1]}"))
            block = st.enter_context(nc.Block())

            def do_wait(engine, d):
                if d[0] == 'eng':
                    n = d[2]
                    engine.wait_ge(esem[(d[1], (n - 1) // ROT)], (n - 1) % ROT + 1)
                else:
                    engine.wait_ge(dsem[d[1]], d[2])

            def run(e, engine):
                for item in self.q[e]:
                    kind, fn, waits = item[0], item[1], item[2]
                    for d in waits:
                        do_wait(engine, d)
                    inst = fn(engine)
                    if kind == 'op':
                        n = item[3]
                        inst.then_inc(esem[(e, (n - 1) // ROT)], 1)
                    else:
                        inst.then_inc(dsem[item[3]], 16)
                if e == 'sp':
                    for slot, v in self.dlast.items():
                        engine.wait_ge(dsem[slot], v)

            @block.tensor
            def _(eng):
                run('pe', eng)

            @block.scalar
            def _(eng):
                run('act', eng)

            @block.vector
            def _(eng):
                run('dve', eng)

            @block.gpsimd
            def _(eng):
                run('pool', eng)

            @block.sync
            def _(eng):
                run('sp', eng)


D = 1024
KC = 8
NIN = 9864
DPLE = 256
C_A0, C_AZ, C_BQ, C_BK, C_BV, C_BI, C_BF, C_BO, C_BZ = 0, 1664, 2176, 2688, 3200, 3712, 3716, 3720, 4232
C_CQ, C_CK, C_CV, C_CZ, C_G0 = 4744, 5256, 5768, 6280, 6792
DECAY_C = math.exp(-0.5)
LN_C = -0.5 * math.log(128.0)

CS_ID = 0
CS_SU = 128
CS_SL = 256
CS_IU = 384
CS_BO = 512
CS_TRI = 640
CS_ONE = 768
CS_M01 = 896
CS_SEL = 1408
NCST = 1920


def make_consts():
    c = np.zeros((128, NCST), np.float32)
    i = np.arange(128)
    r, cc = i[:, None], i[None, :]
    same = (r // 64) == (cc // 64)
    c[:, CS_ID:CS_ID + 128] = (r == cc)
    c[:, CS_SU:CS_SU + 128] = same & (r < cc)
    c[:, CS_SL:CS_SL + 128] = same & (r > cc)
    c[:, CS_IU:CS_IU + 128] = same & (r <= cc)
    c[:, CS_BO:CS_BO + 128] = same
    c[:, CS_TRI:CS_TRI + 128] = (r >= cc)
    c[:, CS_ONE:CS_ONE + 128] = 1.0
    t = np.arange(512)
    c[:, CS_M01:CS_M01 + 512] = (t % 64 != 0)[None, :]
    for h in range(4):
        c[32 * h, CS_SEL + 128 * h:CS_SEL + 128 * (h + 1)] = 1.0
    return c


def make_att():
    a = np.zeros((128, 2048), np.float32)
    r = np.arange(128)[:, None]
    t = np.arange(512)
    for j in range(4):
        a[:, 512 * j:512 * (j + 1)] = (t[None, :] - r) > 128 * j
    return a


PV_BA, PV_BQK, PV_BCQ, PV_NBCQ, PV_BCK, PV_BCZ, PV_BG = 0, 17, 25, 29, 33, 37, 41
PV_BI, PV_NBF, PV_MU, PV_OMU, PV_W0, PV_A0, PV_KK, PV_KA, PV_RK = 65, 66, 67, 80, 93, 97, 101, 105, 109
PV_CW, PV_CB, PV_LNG, PV_LNB = 113, 145, 153, 161
NPV = 170
X_B, X_MU, X_OMU, X_BZ, X_W0, X_A0, X_KK, X_KA, X_RK = 0, 26, 52, 78, 86, 94, 102, 110, 118
NX = 126


def build(cfg):
    L = cfg['L']
    seqs = cfg['seqs']
    nc = bass.Bass("TRN2", target_bir_lowering=False)

    def din(name, shape, dt=F32):
        return nc.dram_tensor(name, list(shape), dt, kind="ExternalInput").ap()

    def dout(name, shape):
        return nc.dram_tensor(name, list(shape), F32, kind="ExternalOutput").ap()

    def dint(name, shape, dt):
        return nc.dram_tensor(name, list(shape), dt, kind="Internal").ap()

    W = {}
    for nm, shp in [('ln_in_g', [D]), ('ln_in_b', [D]), ('w_in', [L, D, NIN]), ('b_in', [L, NIN]), ('mu_a', [L, 1664]),
                    ('w0_a', [L, 512]), ('w_decay_up', [L, 64, 512]), ('a0_a', [L, 512]), ('w_iclr_up', [L, 64, 512]),
                    ('k_k', [L, 512]), ('k_a', [L, 512]), ('r_k', [L, 512]), ('gn_a_g', [L, 512]), ('gn_a_b', [L, 512]),
                    ('conv_b_w', [L, 4, D]), ('conv_b_b', [L, D]), ('hn_b_g', [L, 512]), ('w_branch', [L, 3, 512, D]),
                    ('w_out', [L, D, D]), ('ln_g', [L, D]), ('ln_b', [L, D]), ('w_ple', [L, DPLE, D]),
                    ('w_ple_gate', [L, D, D]), ('cst', [128, NCST]), ('catt', [128, 2048]), ('pvh', [128, L, NPV]), ('pvg', [128, 16]),
                    ('gnh', [L, 64, 16, 64]), ('hnh', [L, 128, 512]), ('cst3', [64, 512]), ('pvx', [64, L, NX])]:
        W[nm] = din(nm, shp)
    SI, SO, SX = [], [], []
    for i, sq in enumerate(seqs):
        T, P = sq['T'], sq['P']
        d = {'xT': din(f"xT{i}", [D, T]), 'pT': din(f"pT{i}", [L, DPLE, T])}
        if P > 0:
            d['shift'] = din(f"shift{i}", [L, 64, 26])
            d['wkv'] = din(f"wkv{i}", [L, 4, 64, 2, 64])
            d['conv'] = din(f"conv{i}", [L, 128, 8, 3])
            d['ct'] = din(f"ct{i}", [L, 4, 128, 129])
            d['m'] = din(f"m{i}", [L, 128, 1])
            d['ck'] = din(f"ck{i}", [L, 4, 128, P])
            d['cv'] = din(f"cv{i}", [L, P, 512])
        SI.append(d)
        SO.append({'yT': dout(f"yT{i}", [D, T]), 'shift': dout(f"shift_o{i}", [L, 64, 26]),
                   'wkv': dout(f"wkv_o{i}", [L, 4, 64, 2, 64]), 'conv': dout(f"conv_o{i}", [L, 128, 8, 3]),
                   'ct': dout(f"ct_o{i}", [L, 4, 128, 129]), 'm': dout(f"m_o{i}", [L, 128, 1]),
                   'sbk': dout(f"sbk{i}", [L, T, 512]), 'sbv': dout(f"sbv{i}", [L, T, 512])})
        SX.append({'xres': dint(f"xres{i}", [KC, 128, T], F32), 'kTd': dint(f"kTd{i}", [4, 128, P + T], BF16), 'xTd': dint(f"xTd{i}", [KC, 128, T], BF16),
                   'vd': dint(f"vd{i}", [P + T, 512], BF16)})
    wb_in = dint("wb_in", [L, D, NIN], BF16)
    wb_br = dint("wb_br", [L, 3, 512, D], BF16)
    wb_out = dint("wb_out", [L, D, D], BF16)
    wb_pg = dint("wb_pg", [L, D, D], BF16)
    wb_ple = dint("wb_ple", [L, DPLE, D], BF16)

    S = Sched()
    st = contextlib.ExitStack()
    with st:
        def sb(name, shape, dt=F32):
            return st.enter_context(nc.sbuf_tensor(name, list(shape), dt))

        cstF = sb("cstF", [128, 1280])
        cstB = sb("cstB", [128, CS_M01], BF16)
        attB = sb("attB", [128, 4, 512], BF16)
        PV = sb("PV", [128, L, NPV])
        PVG = sb("PVG", [128, 16])
        slabs = [sb(f"slab{i}", [128, 4096], BF16) for i in range(4)]
        xTb = [sb(f"xTb{i}", [128, KC, 512], BF16) for i in range(2)]
        slabZ = sb("slabZ", [128, KC, 128], BF16)
        tmpF = [sb(f"tmpF{i}", [128, 512]) for i in range(9)]
        lnm = sb("lnm", [128, 512])
        lnr = sb("lnr", [128, 512])
        tmpH = [sb(f"tmpH{i}", [128, 512], BF16) for i in range(28)]
        ysT = sb("ysT", [128, 12, 512], BF16)
        p16 = sb("p16", [128, 2, 512], BF16)
        pstg = sb("pstg", [128, 2, 512])
        kTblk = [sb(f"kTblk{i}", [128, 4, 128], BF16) for i in range(2)]
        vblk = [sb(f"vblk{i}", [128, 512], BF16) for i in range(2)]
        brow = sb("brow", [1, 2560], BF16)
        hnG = sb("hnG", [128, 512])
        gnx = sb("gnx", [64, 16, 64])
        upw = sb("upw", [64, 2, 512], BF16)
        wrep = sb("wrep", [128, 2, KC, 128], BF16)
        wcol = sb("wcol", [128, KC, 8], BF16)
        PVX = sb("PVX", [64, L, NX])
        cm2 = sb("cm2", [64, 4, 128], BF16)
        rawA = [sb(f"rawA{i}", [64, 513]) for i in range(2)]
        carX = sb("carX", [64, 26])
        lora = sb("lora", [64, 2, 512], BF16)
        sqh = sb("sqh", [64, 512], BF16)
        egP = [sb(f"egP{i}", [64, 2, 512]) for i in range(2)]
        Hf = [sb(f"Hf{i}", [64, 2, 64]) for i in range(4)]
        Hb = [sb(f"Hb{i}", [64, 2, 64], BF16) for i in range(4)]
        RS = []
        for p in range(2):
            d = {}
            for nm in ('Qa', 'Qb', 'Pa', 'Pb', 'Za', 'Zb', 'Abk', 'Ara', 'Ark', 'ast', 'kst', 'Vst', 'Xb', 'Ub', 'ysa'):
                d[nm] = sb(f"r{nm}{p}", [64, 2, 64], BF16)
            for nm in ('azst', 'e1', 'e2', 'e3', 'htmp'):
                d[nm] = sb(f"r{nm}{p}", [64, 2, 64])
            d['sc'] = sb(f"rsc{p}", [64, 8, 2])
            RS.append(d)
        rawB = [sb(f"rawB{i}", [128, 515]) for i in range(2)]
        carB = sb("carB", [128, 8, 3])
        mbuf = sb("mbuf", [128, 576])
        carM = sb("carM", [128, 2])
        vaug = sb("vaug", [128, 4, 4, 129], BF16)
        tokS = sb("tokS", [128, 4, 5, 4])
        tokS2 = sb("tokS2", [128, 4, 2, 4])
        rmask = sb("rmask", [128, 2])
        csB = sb("csB", [128, 4, 8])
        CTf = [sb(f"CTf{i}", [128, 129]) for i in range(4)]
        CTb = [sb(f"CTb{i}", [128, 129], BF16) for i in range(4)]
        mlt = [sb(f"mlt{i}", [128, 132]) for i in range(4)]
        mlh = [sb(f"mlh{i}", [128, 128], BF16) for i in range(2)]
        mlh2 = [sb(f"mlh2{i}", [128, 512], BF16) for i in range(2)]
        mls = sb("mls", [128, 8, 8])
        psb = [st.enter_context(nc.psum_tensor(f"psb{i}", [128, 512], F32)) for i in range(8)]
        psbH = [psb[i][:, :].bitcast(BF16) for i in range(8)]

        idF = cstF[:, 0:128]
        onesF = cstF[:, 128:256]
        m01 = cstF[:, 256:768]
        idB = cstB[:, CS_ID:CS_ID + 128]
        mIU = cstB[:, CS_IU:CS_IU + 128]
        blkones = cstB[:, CS_BO:CS_BO + 128]
        triB = cstB[:, CS_TRI:CS_TRI + 128]
        onesB = cstB[:, CS_ONE:CS_ONE + 128]

        def MM(out, lhsT, rhs, r, w, start=True, stop=True):
            S.op('pe', lambda e: e.matmul(out, lhsT=lhsT, rhs=rhs, start=start, stop=stop), r, w)

        def TR(out, in_, ident, r, w):
            S.op('pe', lambda e: e.transpose(out, in_, ident), r, w)

        def ACT(out, in_, func, r, w, bias=None, scale=None, accum=None):
            kw = {}
            if bias is not None:
                kw['bias'] = bias
            if scale is not None:
                kw['scale'] = scale
            if accum is not None:
                kw['accum_out'] = accum
            S.op('act', lambda e: e.activation(out=out, in_=in_, func=func, **kw), r, w)

        def TT(eng, out, in0, in1, op, r, w):
            S.op(eng, lambda e: e.tensor_tensor(out=out, in0=in0, in1=in1, op=op), r, w)

        def TS(eng, out, in0, s1, op0, r, w, s2=None, op1=None, accum=None):
            kw = {}
            if op1 is not None:
                kw['op1'] = op1
            if accum is not None:
                kw['accum_out'] = accum
            S.op(eng, lambda e: e.tensor_scalar(out=out, in0=in0, scalar1=s1, scalar2=s2, op0=op0, **kw), r, w)

        def STT(out, in0, scalar, in1, op0, op1, r, w):
            S.op('dve', lambda e: e.scalar_tensor_tensor(out=out, in0=in0, scalar=scalar, in1=in1, op0=op0, op1=op1), r, w)

        def CP(eng, out, in_, r, w):
            if eng == 'act':
                S.op('act', lambda e: e.activation(out=out, in_=in_, func=AF.Copy), r, w)
            else:
                S.op(eng, lambda e: e.tensor_copy(out=out, in_=in_), r, w)

        def MSET(eng, ap, val, w):
            S.op(eng, lambda e: e.memset(ap, val), (), w)

        def SCAN(out, d0, d1, init, op0, op1, r, w):
            S.op('dve', lambda e: e.tensor_tensor_scan(out=out, data0=d0, data1=d1, initial=init, op0=op0, op1=op1), r, w)

        def RECIP(out, in_, r, w):
            S.op('dve', lambda e: e.reciprocal(out=out, in_=in_), r, w)

        def DMA(eng, out, in_, r, w, is_out=False, slow=False):
            if slow:
                S.dma(eng, lambda e: e.dma_start(out=out, in_=in_, allow_slow_non_contiguous=True), r, w, is_out)
            else:
                S.dma(eng, lambda e: e.dma_start(out=out, in_=in_), r, w, is_out)

        RB = [0, 1, 2, 3, 7]
        ring = {1: 0, 2: 0, 4: 0, 't': 0}

        def psq(n=1):
            c = ring[n]
            ring[n] = c + 1
            b = RB[c % 5]
            if n == 1:
                o = (c // 5) % 4
            elif n == 2:
                o = 2 * ((c // 5) % 2)
            else:
                o = 0
            return psb[b][:, o * 128:(o + n) * 128], [('ps', b)]

        def psbank(b):
            return psb[b][:, :], [('ps', b)]

        def pst():
            c = ring['t']
            ring['t'] = c + 1
            b = RB[(c + 2) % 5]
            o = (c // 5) % 4
            return psbH[b][:, o * 256:o * 256 + 128], [('ps', b)]

        sring = {'i': 0}

        def load_slab(src3, kdim, ncols, wkey):
            i = sring['i']
            sring['i'] = (i + 1) % 4
            view = slabs[i][:, 0:kdim * ncols].rearrange("p (k c) -> p k c", k=kdim)
            DMA('sp', view, src3, wkeys.get(wkey, []), [('slab', i)])
            return view, ('slab', i)

        def win_slab(l, c0, n):
            return load_slab(wb_in[l].rearrange("(k p) c -> p k c", p=128)[:, :, c0:c0 + n], KC, n, ('wb_in', l))

        stg = [slabs[i][:, :].bitcast(F32) for i in range(2)]
        kst_ = [('slab', 0), ('slab', 1)]
        DMA('sp', stg[0][:, 0:NCST], W['cst'], [], [kst_[0]])
        DMA('sp', stg[1][:, 0:2048], W['catt'], [], [kst_[1]])
        CP('dve', cstB[:], stg[0][:, 0:CS_M01], [kst_[0]], ['cstB'])
        CP('dve', cstF[:, 0:128], stg[0][:, CS_ID:CS_ID + 128], [kst_[0]], ['cstF'])
        CP('dve', cstF[:, 128:256], stg[0][:, CS_ONE:CS_ONE + 128], [kst_[0]], ['cstF'])
        CP('dve', cstF[:, 256:768], stg[0][:, CS_M01:CS_M01 + 512], [kst_[0]], ['cstF'])
        CP('dve', cstF[:, 768:1280], stg[0][:, CS_SEL:CS_SEL + 512], [kst_[0]], ['cstF'])
        CP('dve', attB[:], stg[1][:, 0:2048].rearrange("p (j c) -> p j c", j=4), [kst_[1]], ['attB'])
        DMA('sp', lnm[0:64, :], W['cst3'], [], ['lnm'])
        CP('dve', cm2[:], lnm[0:64, :].rearrange("p (j c) -> p j c", j=4), ['lnm'], ['cm2'])
        DMA('sp', PVX[:], W['pvx'], [], ['PVX'])
        for b in range(8):
            MSET('dve', psb[b][:], 0.0, [('ps', b)])
        pc = {'i': 0}
        ceng = ('act', 'pool', 'dve')

        def precast(dst2d, src2d, key):
            rows, cols = src2d.shape
            for r0 in range(0, rows, 128):
                for c0 in range(0, cols, 2048):
                    n = min(2048, cols - c0)
                    i = pc['i']
                    pc['i'] += 1
                    sf, kf_ = stg[i % 2], kst_[i % 2]
                    db, kd_ = slabs[2 + i % 2], ('slab', 2 + i % 2)
                    DMA('sp', sf[:, 0:n], src2d[r0:r0 + 128, c0:c0 + n], [], [kf_])
                    CP(ceng[i % 3], db[:, 0:n], sf[:, 0:n], [kf_], [kd_])
                    DMA('sp', dst2d[r0:r0 + 128, c0:c0 + n], db[:, 0:n], [kd_], [(key, pc['i'])])
                    wkeys.setdefault(key, []).append((key, pc['i']))

        wkeys = {}
        for l in range(L):
            precast(wb_in[l], W['w_in'][l], ('wb_in', l))
            for n in range(3):
                precast(wb_br[l, n], W['w_branch'][l, n], ('wb_br', l))
            precast(wb_out[l], W['w_out'][l], ('wb_out', l))
            precast(wb_pg[l], W['w_ple_gate'][l], ('wb_pg', l))
            precast(wb_ple[l], W['w_ple'][l], ('wb_ple', l))
        DMA('sp', PV[:], W['pvh'], [], ['PV'])
        DMA('sp', PVG[:], W['pvg'], [], ['PV'])
        for l in range(L):
            TS('dve', PV[:, l, PV_OMU:PV_OMU + 13], PV[:, l, PV_MU:PV_MU + 13], -1.0, ALU.mult, ['PV'], ['PV'], s2=1.0, op1=ALU.add)
            TS('dve', PV[:, l, PV_NBF:PV_NBF + 1], PV[:, l, PV_NBF:PV_NBF + 1], -1.0, ALU.mult, ['PV'], ['PV'])
            TS('dve', PV[:, l, PV_NBCQ:PV_NBCQ + 4], PV[:, l, PV_BCQ:PV_BCQ + 4], -0.125, ALU.mult, ['PV'], ['PV'])
            TS('dve', PV[:, l, PV_BCQ:PV_BCQ + 4], PV[:, l, PV_BCQ:PV_BCQ + 4], 0.125, ALU.mult, ['PV'], ['PV'])
        MSET('dve', vaug[:], 1.0, ['vaug'])
        MSET('dve', rmask[:], 0.0, ['rmask'])
        MSET('dve', rmask[0:64, 0:1], 1.0, ['rmask'])
        MSET('dve', rmask[64:128, 1:2], 1.0, ['rmask'])
        for l in range(L):
            TS('dve', PVX[:, l, X_OMU:X_OMU + 26], PVX[:, l, X_MU:X_MU + 26], -1.0, ALU.mult, ['PVX'], ['PVX'], s2=1.0, op1=ALU.add)

        def layernorm(TB, gcol, bcol, eps):
            s5, k5 = psbank(4)
            s6, k6 = psbank(5)
            for fc in range(KC):
                ACT(lnr[:, :TB], tmpF[fc][:, :TB], AF.Square, [('tF', fc)], ['lnr'])
                MM(s5[:, :TB], onesF, tmpF[fc][:, :TB], ['cstF', ('tF', fc)], k5, start=(fc == 0), stop=(fc == KC - 1))
                MM(s6[:, :TB], onesF, lnr[:, :TB], ['cstF', 'lnr'], k6, start=(fc == 0), stop=(fc == KC - 1))
            ACT(lnm[:, :TB], s5[:, :TB], AF.Copy, k5, ['lnm'], scale=1.0 / D)
            TT('dve', tmpF[8][:, :TB], lnm[:, :TB], lnm[:, :TB], ALU.mult, ['lnm'], [('tF', 8)])
            STT(tmpF[8][:, :TB], s6[:, :TB], 1.0 / D, tmpF[8][:, :TB], ALU.mult, ALU.subtract, k6 + [('tF', 8)], [('tF', 8)])
            TS('dve', tmpF[8][:, :TB], tmpF[8][:, :TB], 0.0, ALU.max, [('tF', 8)], [('tF', 8)], s2=eps, op1=ALU.add)
            ACT(tmpF[8][:, :TB], tmpF[8][:, :TB], AF.Sqrt, [('tF', 8)], [('tF', 8)])
            RECIP(lnr[:, :TB], tmpF[8][:, :TB], [('tF', 8)], ['lnr'])
            for fc in range(KC):
                TT('dve', tmpF[fc][:, :TB], tmpF[fc][:, :TB], lnm[:, :TB], ALU.subtract, [('tF', fc), 'lnm'], [('tF', fc)])
                TT('dve', tmpF[fc][:, :TB], tmpF[fc][:, :TB], lnr[:, :TB], ALU.mult, [('tF', fc), 'lnr'], [('tF', fc)])
                ACT(tmpF[fc][:, :TB], tmpF[fc][:, :TB], AF.Identity, [('tF', fc), 'PV'], [('tF', fc)],
                    bias=bcol(fc), scale=gcol(fc))

        for si, sq in enumerate(seqs):
            T, P = sq['T'], sq['P']
            TB = min(512, T)
            NBLK = T // TB
            TP = min(128, TB)
            NTL = TB // TP
            NCH = TB // 64
            CPT = TP // 64
            I, O, X = SI[si], SO[si], SX[si]
            KX = ('xres', si)
            KXD = ('xTd', si)

            for j in range(NBLK):
                t0 = j * TB
                for fc in range(KC):
                    DMA('sp', tmpF[fc][:, :TB], I['xT'][fc * 128:(fc + 1) * 128, t0:t0 + TB], [], [('tF', fc)])
                layernorm(TB, lambda fc: PVG[:, fc:fc + 1], lambda fc: PVG[:, 8 + fc:9 + fc], 1e-5)
                for fc in range(KC):
                    DMA('sp', X['xres'][fc, :, t0:t0 + TB], tmpF[fc][:, :TB], [('tF', fc)], [KX])
                    CP('pool', tmpH[fc][:, :TB], tmpF[fc][:, :TB], [('tF', fc)], [('tH', fc)])
                    DMA('sp', X['xTd'][fc, :, t0:t0 + TB], tmpH[fc][:, :TB], [('tH', fc)], [KXD])

            for l in range(L):
                pv = lambda c, n=1: PV[:, l, c:c + n]
                DMA('sp', lnm[0:64, :], W['w_decay_up'][l], [], ['lnm'])
                DMA('sp', lnr[0:64, :], W['w_iclr_up'][l], [], ['lnr'])
                CP('pool', upw[:, 0, :], lnm[0:64, :], ['lnm'], ['upw'])
                CP('pool', upw[:, 1, :], lnr[0:64, :], ['lnr'], ['upw'])
                bl = W['b_in'][l]
                for gi, c0 in enumerate((C_BV, C_BO, C_BZ, C_CK, C_CV)):
                    DMA('sp', lnr[0:1, :], bl[c0:c0 + 512].rearrange("(o c) -> o c", o=1), [], ['lnr'])
                    CP('pool', brow[0:1, gi * 512:(gi + 1) * 512], lnr[0:1, :], ['lnr'], ['brow'])
                DMA('sp', hnG[:], W['hnh'][l], [], ['hnG'])
                DMA('sp', gnx[:], W['gnh'][l], [], ['gnx'])
                DMA('sp', wcol[:], wb_in[l].rearrange("(k p) c -> p k c", p=128)[:, :, C_BI:C_BI + 8], ['wb_in'], ['wcol'], slow=True)
                for g in range(2):
                    for h in range(4):
                        CP('pool', wrep[:, g, :, 32 * h:32 * h + 32], wcol[:, :, 4 * g + h:4 * g + h + 1].broadcast_to([128, KC, 32]),
                           ['wcol'], ['wrep'])
                if P > 0:
                    DMA('sp', carX[:], I['shift'][l], [], ['carX'])
                    for p in range(4):
                        DMA('sp', Hf[p][:, :, :], I['wkv'][l, p], [], [('Hf', p)])
                        DMA('sp', CTf[p][:], I['ct'][l, p], [], [('CTf', p)])
                    DMA('sp', carB[:], I['conv'][l], [], ['carB'])
                    DMA('sp', mbuf[:, 63:64], I['m'][l], [], ['mbuf'])
                    CP('dve', carM[:, 1:2], mbuf[:, 63:64], ['mbuf'], ['carM'])
                    MSET('dve', carM[:, 0:1], 0.0, ['carM'])
                    ci = 0
                    for p in range(4):
                        for c0 in range(0, P, 512):
                            n = min(512, P - c0)
                            DMA('sp', tmpF[ci % 8][:, 0:n], I['ck'][l, p][:, c0:c0 + n], [], [('tF', ci % 8)])
                            CP(('act', 'pool', 'dve')[ci % 3], tmpH[ci % 8][:, 0:n], tmpF[ci % 8][:, 0:n], [('tF', ci % 8)], [('tH', ci % 8)])
                            DMA('sp', X['kTd'][p, :, c0:c0 + n], tmpH[ci % 8][:, 0:n], [('tH', ci % 8)], [('kTd', si)])
                            ci += 1
                    for r0 in range(0, P, 128):
                        n = min(128, P - r0)
                        DMA('sp', tmpF[ci % 8][0:n, :], I['cv'][l, r0:r0 + n, :], [], [('tF', ci % 8)])
                        CP(('act', 'pool', 'dve')[ci % 3], tmpH[ci % 8][0:n, :], tmpF[ci % 8][0:n, :], [('tF', ci % 8)], [('tH', ci % 8)])
                        DMA('sp', X['vd'][r0:r0 + n, :], tmpH[ci % 8][0:n, :], [('tH', ci % 8)], [('vd', si)])
                        ci += 1
                else:
                    MSET('pool', carX[:], 0.0, ['carX'])
                    MSET('pool', carB[:], 0.0, ['carB'])
                    MSET('pool', mbuf[:, 0:64], 0.0, ['mbuf'])
                    MSET('pool', carM[:], 0.0, ['carM'])
                    for p in range(4):
                        MSET('pool', Hf[p][:, :, :], 0.0, [('Hf', p)])
                        MSET('pool', CTf[p][:], 0.0, [('CTf', p)])
                for p in range(4):
                    CP('pool', Hb[p][:, :, :], Hf[p][:, :, :], [('Hf', p)], [('Hb', p)])
                    CP('pool', CTb[p][:], CTf[p][:], [('CTf', p)], [('CTb', p)])

                for j in range(NBLK):
                    t0 = j * TB
                    S.enabled = True
                    S.budget = cfg.get('cut')
                    xb = xTb[(j + l) % 2]
                    KXB = ('xTb', (j + l) % 2)
                    DMA('sp', xb[:, :, :TB], X['xTd'][:, :, t0:t0 + TB].rearrange("k p t -> p k t"), [KXD], [KXB])
                    DMA('sp', pstg[:, :, :TB], I['pT'][l, :, t0:t0 + TB].rearrange("(k p) t -> p k t", p=128), [], ['pstg'])
                    CP('pool', p16[:, :, :TB], pstg[:, :, :TB], ['pstg'], ['p16'])

                    def proj_fm(slab, skey, c0, M, out_ap, okeys):
                        for kc in range(KC):
                            MM(out_ap, slab[:, kc, c0:c0 + M], xb[:, kc, :TB], [skey, KXB], okeys, start=(kc == 0), stop=(kc == KC - 1))

                    def proj_tm(slab, skey, tt, out_ap, okeys, boff):
                        for kc in range(KC):
                            MM(out_ap, xb[:, kc, tt * TP:(tt + 1) * TP], slab[:, kc, :], [skey, KXB], okeys, start=(kc == 0), stop=False)
                        MM(out_ap, onesB[0:1, 0:TP], brow[0:1, boff:boff + 512], ['cstB', 'brow'], okeys, start=False, stop=True)

                    S.enabled = 'A' in cfg.get('ph', 'ABCG')
                    pvx = lambda c, n=1: PVX[:, l, c:c + n]
                    H64 = slice(0, 64)
                    sR, kR = win_slab(l, 0, 512)
                    sK, kK = win_slab(l, 512, 512)
                    sV, kV = win_slab(l, 1024, 512)
                    sW, kW = win_slab(l, 1536, 512)
                    sZ, kZ = None, None

                    def fmx(slab, skey, crel, xi, dst, dkey):
                        rb = rawA[xi % 2]
                        rk_ = ('rawA', xi % 2)
                        ps, pk = psq(4)
                        proj_fm(slab, skey, crel, 64, ps[H64, :TB], pk)
                        ACT(rb[:, 1:1 + TB], ps[H64, :TB], AF.Identity, pk + ['PVX'], [rk_], bias=pvx(X_B + xi))
                        CP('pool', rb[:, 0:1], carX[:, xi:xi + 1], ['carX'], [rk_])
                        TS('dve', dst[H64, :TB], rb[:, 0:TB], pvx(X_MU + xi), ALU.mult, [rk_, 'PVX'], [dkey])
                        STT(dst[H64, :TB], rb[:, 1:1 + TB], pvx(X_OMU + xi), dst[H64, :TB], ALU.mult, ALU.add, [rk_, 'PVX', dkey], [dkey])
                        CP('pool', carX[:, xi:xi + 1], rb[:, TB:TB + 1], [rk_], ['carX'])

                    fmx(sW, kW, 0, 24, tmpF[0], ('tF', 0))
                    ACT(lora[:, 0, :TB], tmpF[0][H64, :TB], AF.Tanh, [('tF', 0)], ['lora'])
                    fmx(sW, kW, 64, 25, tmpF[1], ('tF', 1))
                    CP('dve', lora[:, 1, :TB], tmpF[1][H64, :TB], [('tF', 1)], ['lora'])
                    for hg in range(2):
                        for hi in range(4):
                            h = hg * 4 + hi
                            pp, hh = hi // 2, hi % 2
                            xr, xk, xv = tmpF[0], tmpF[1], tmpF[2]
                            kr_, kk_, kv_ = ('tF', 0), ('tF', 1), ('tF', 2)
                            t0_, t1_, t2_, t3_, t4_, t5_ = tmpF[3], tmpF[4], tmpF[5], tmpF[6], tmpF[7], tmpF[8]
                            k0_, k1_, k2_, k3_, k4_, k5_ = [('tF', i) for i in range(3, 9)]
                            a16, b16, k16, r16, v16, rk16, az16 = [tmpH[q * 4 + hi] for q in range(7)]
                            ka16, kb16, kk16, kr16, kv16, krk16, kaz16 = [('tH', q * 4 + hi) for q in range(7)]
                            fmx(sR, kR, h * 64, h, xr, kr_)
                            fmx(sK, kK, h * 64, 8 + h, xk, kk_)
                            fmx(sV, kV, h * 64, 16 + h, xv, kv_)
                            ps, pk = psq(4)
                            if h < 6:
                                proj_fm(sW, kW, 128 + h * 64, 64, ps[H64, :TB], pk)
                            else:
                                if sZ is None:
                                    sZ, kZ = slabZ, 'slabZ'
                                    DMA('sp', slabZ[:, :, :], wb_in[l].rearrange("(k p) c -> p k c", p=128)[:, :, 2048:2176], wkeys.get(('wb_in', l), []), ['slabZ'])
                                proj_fm(sZ, kZ, (h - 6) * 64, 64, ps[H64, :TB], pk)
                            ACT(az16[H64, :TB], ps[H64, :TB], AF.Silu, pk + ['PVX'], [kaz16], bias=pvx(X_BZ + h))
                            ps, pk = psq(4)
                            MM(ps[H64, :TB], upw[:, 0, h * 64:(h + 1) * 64], lora[:, 0, :TB], ['upw', 'lora'], pk)
                            ACT(t0_[H64, :TB], ps[H64, :TB], AF.Sigmoid, pk + ['PVX'], [k0_], bias=pvx(X_W0 + h))
                            ps, pk = psq(4)
                            MM(ps[H64, :TB], upw[:, 1, h * 64:(h + 1) * 64], lora[:, 1, :TB], ['upw', 'lora'], pk)
                            ACT(t1_[H64, :TB], ps[H64, :TB], AF.Sigmoid, pk + ['PVX'], [k1_], bias=pvx(X_A0 + h))
                            SCAN(t2_[H64, :TB], m01[H64, :TB], t0_[H64, :TB], 0.0, ALU.mult, ALU.add, ['cstF', k0_], [k2_])
                            TT('dve', t3_[H64, :TB], t2_[H64, :TB], t0_[H64, :TB], ALU.subtract, [k2_, k0_], [k3_])
                            eg = egP[pp][:, hh, :]
                            keg = ('eg', pp)
                            ACT(eg[:, :TB], t2_[H64, :TB], AF.Exp, [k2_], [keg], scale=-DECAY_C)
                            ACT(t4_[H64, :TB], t2_[H64, :TB], AF.Exp, [k2_], [k4_], scale=DECAY_C)
                            ACT(t3_[H64, :TB], t3_[H64, :TB], AF.Exp, [k3_], [k3_], scale=-DECAY_C)
                            TS('dve', t5_[H64, :TB], xk[H64, :TB], pvx(X_KK + h), ALU.mult, [kk_, 'PVX'], [k5_])
                            ACT(sqh[:, :TB], t5_[H64, :TB], AF.Square, [k5_], ['sqh'])
                            ps, pk = psq(4)
                            MM(ps[H64, :TB], onesB[H64, 0:64], sqh[:, :TB], ['cstB', 'sqh'], pk)
                            ACT(t0_[H64, :TB], ps[H64, :TB], AF.Sqrt, pk, [k0_])
                            TS('dve', t0_[H64, :TB], t0_[H64, :TB], 1e-12, ALU.max, [k0_], [k0_])
                            RECIP(t0_[H64, :TB], t0_[H64, :TB], [k0_], [k0_])
                            TT('dve', t5_[H64, :TB], t5_[H64, :TB], t0_[H64, :TB], ALU.mult, [k5_, k0_], [k5_])
                            TS('dve', t2_[H64, :TB], t1_[H64, :TB], -1.0, ALU.add, [k1_, 'PVX'], [k2_], s2=pvx(X_KA + h), op1=ALU.mult)
                            STT(t2_[H64, :TB], t2_[H64, :TB], 1.0, xk[H64, :TB], ALU.add, ALU.mult, [k2_, kk_], [k2_])
                            TT('dve', b16[H64, :TB], t5_[H64, :TB], t3_[H64, :TB], ALU.mult, [k5_, k3_], [kb16])
                            TT('dve', t0_[H64, :TB], t1_[H64, :TB], t5_[H64, :TB], ALU.mult, [k1_, k5_], [k0_])
                            STT(a16[H64, :TB], t0_[H64, :TB], -1.0, t4_[H64, :TB], ALU.mult, ALU.mult, [k0_, k4_], [ka16])
                            TT('pool', k16[H64, :TB], t2_[H64, :TB], t4_[H64, :TB], ALU.mult, [k2_, k4_], [kk16])
                            TT('pool', r16[H64, :TB], xr[H64, :TB], eg[:, :TB], ALU.mult, [kr_, keg], [kr16])
                            CP('pool', v16[H64, :TB], xv[H64, :TB], [kv_], [kv16])
                            TT('dve', t0_[H64, :TB], xr[H64, :TB], t2_[H64, :TB], ALU.mult, [kr_, k2_], [k0_])
                            TS('dve', rk16[H64, :TB], t0_[H64, :TB], pvx(X_RK + h), ALU.mult, [k0_, 'PVX'], [krk16])

                        TH = lambda q, hi: tmpH[q * 4 + hi]
                        KH = lambda q, hi: ('tH', q * 4 + hi)
                        mSU2, mSL2, mIU2, mID2 = [cm2[:, i, :] for i in range(4)]
                        f2 = lambda t: t[:, :, :].rearrange("p a b -> p (a b)")
                        for c in range(NCH):
                            cs = slice(c * 64, (c + 1) * 64)
                            for pp in range(2):
                                R_ = RS[pp]
                                rk = lambda nm, pp=pp: ('rs', pp, nm)

                                def score(ql, qr, mask, dst):
                                    ps, pk = psq(1)
                                    for hh in range(2):
                                        hi = pp * 2 + hh
                                        MM(ps[H64, hh * 64:(hh + 1) * 64], TH(ql, hi)[H64, cs], TH(qr, hi)[H64, cs], [KH(ql, hi), KH(qr, hi)], pk)
                                    TT('dve', f2(R_[dst]), ps[H64, :], mask, ALU.mult, pk + ['cm2'], [rk(dst)])

                                score(0, 1, mSU2, 'Qa')
                                score(1, 0, mSL2, 'Pa')
                                score(2, 1, mSU2, 'Abk')
                                score(0, 3, mIU2, 'Ara')
                                score(2, 3, mIU2, 'Ark')
                                TT('pool', f2(R_['Za']), f2(R_['Qa']), mID2, ALU.add, [rk('Qa'), 'cm2'], [rk('Za')])
                                for q, dst in ((4, 'Vst'), (0, 'ast'), (2, 'kst'), (6, 'azst')):
                                    pt, ptk = pst()
                                    for hh in range(2):
                                        hi = pp * 2 + hh
                                        TR(pt[H64, hh * 64:(hh + 1) * 64], TH(q, hi)[H64, cs], idB[H64, 0:64], [KH(q, hi), 'cstB'], ptk)
                                    CP('act', f2(R_[dst]), pt[H64, :], ptk, [rk(dst)])
                                ps, pk = psq(1)
                                for hh in range(2):
                                    hi = pp * 2 + hh
                                    MM(ps[H64, hh:hh + 1], TH(5, hi)[H64, cs], onesB[H64, 0:1], [KH(5, hi), 'cstB'], pk)
                                CP('act', R_['sc'][:, 0, :], ps[H64, 0:2], pk, [rk('sc')])
                            qc, pc, zc = 'Qa', 'Pa', 'Za'
                            flip = {'Qa': 'Qb', 'Qb': 'Qa', 'Pa': 'Pb', 'Pb': 'Pa', 'Za': 'Zb', 'Zb': 'Za'}
                            for lev in range(1, 7):
                                qn, pn, zn = flip[qc], flip[pc], flip[zc]
                                for pp in range(2):
                                    R_ = RS[pp]
                                    rk = lambda nm, pp=pp: ('rs', pp, nm)

                                    def mm2(lk, rk2):
                                        ps, pk = psq(1)
                                        for hh in range(2):
                                            MM(ps[H64, hh * 64:(hh + 1) * 64], R_[lk][:, hh, :], R_[rk2][:, hh, :], [rk(lk), rk(rk2)], pk)
                                        return ps, pk

                                    if lev >= 2:
                                        ps, pk = mm2(pc, zc)
                                        TT('dve', f2(R_[zn]), ps[H64, :], f2(R_[zc]), ALU.add, pk + [rk(zc)], [rk(zn)])
                                    if lev <= 5:
                                        ps, pk = mm2(qc, pc)
                                        CP('act', f2(R_[pn]), ps[H64, :], pk, [rk(pn)])
                                    if lev <= 4:
                                        ps, pk = mm2(pc, qc)
                                        CP('act', f2(R_[qn]), ps[H64, :], pk, [rk(qn)])
                                if lev >= 2:
                                    zc = zn
                                if lev <= 5:
                                    pc = pn
                                if lev <= 4:
                                    qc = qn
                            Zf = zc
                            for pp in range(2):
                                R_ = RS[pp]
                                rk = lambda nm, pp=pp: ('rs', pp, nm)
                                gp = hg * 2 + pp
                                ps, pk = psq(1)
                                for hh in range(2):
                                    hi = pp * 2 + hh
                                    o_ = ps[H64, hh * 64:(hh + 1) * 64]
                                    MM(o_, TH(1, hi)[H64, cs], Hb[gp][:, hh, :], [KH(1, hi), ('Hb', gp)], pk, start=True, stop=False)
                                    MM(o_, R_['Abk'][:, hh, :], R_['Vst'][:, hh, :], [rk('Abk'), rk('Vst')], pk, start=False, stop=True)
                                CP('act', f2(R_['Xb']), ps[H64, :], pk, [rk('Xb')])
                            for pp in range(2):
                                R_ = RS[pp]
                                rk = lambda nm, pp=pp: ('rs', pp, nm)
                                ps, pk = psq(1)
                                for hh in range(2):
                                    MM(ps[H64, hh * 64:(hh + 1) * 64], R_[Zf][:, hh, :], R_['Xb'][:, hh, :], [rk(Zf), rk('Xb')], pk)
                                CP('act', f2(R_['Ub']), ps[H64, :], pk, [rk('Ub')])
                            for pp in range(2):
                                R_ = RS[pp]
                                rk = lambda nm, pp=pp: ('rs', pp, nm)
                                gp = hg * 2 + pp
                                keg = ('eg', pp)
                                psy, pky = psq(1)
                                psh, pkh = psq(1)
                                for hh in range(2):
                                    hi = pp * 2 + hh
                                    o_ = psy[H64, hh * 64:(hh + 1) * 64]
                                    MM(o_, TH(3, hi)[H64, cs], Hb[gp][:, hh, :], [KH(3, hi), ('Hb', gp)], pky, start=True, stop=False)
                                    MM(o_, R_['Ara'][:, hh, :], R_['Ub'][:, hh, :], [rk('Ara'), rk('Ub')], pky, start=False, stop=False)
                                    MM(o_, R_['Ark'][:, hh, :], R_['Vst'][:, hh, :], [rk('Ark'), rk('Vst')], pky, start=False, stop=True)
                                for hh in range(2):
                                    o_ = psh[H64, hh * 64:(hh + 1) * 64]
                                    MM(o_, R_['ast'][:, hh, :], R_['Ub'][:, hh, :], [rk('ast'), rk('Ub')], pkh, start=True, stop=False)
                                    MM(o_, R_['kst'][:, hh, :], R_['Vst'][:, hh, :], [rk('kst'), rk('Vst')], pkh, start=False, stop=True)
                                TT('dve', f2(R_['htmp']), psh[H64, :], f2(Hf[gp]), ALU.add, pkh + [('Hf', gp)], [rk('htmp')])
                                gam = egP[pp][:, :, c * 64 + 63:c * 64 + 64].broadcast_to([64, 2, 64])
                                TT('pool', Hf[gp][:, :, :], R_['htmp'][:, :, :], gam, ALU.mult, [rk('htmp'), keg], [('Hf', gp)])
                                TT('dve', Hb[gp][:, :, :], R_['htmp'][:, :, :], gam, ALU.mult, [rk('htmp'), keg], [('Hb', gp)])
                                e1, e2, e3, scr = R_['e1'], R_['e2'], R_['e3'], R_['sc']
                                y3 = psy[H64, :].rearrange("p (a b) -> p a b", a=2)
                                bc2 = lambda i: scr[:, i, :].unsqueeze(2).broadcast_to([64, 2, 64])
                                S.op('dve', lambda e, o=scr[:, 1, :], i=y3: e.reduce_sum(out=o, in_=i, axis=mybir.AxisListType.X), pky, [rk('sc')])
                                STT(e2[:, :, :], bc2(1), -1.0 / 64, y3, ALU.mult, ALU.add, pky + [rk('sc')], [rk('e2')])
                                ACT(f2(e1), f2(e2), AF.Square, [rk('e2')], [rk('e1')])
                                S.op('dve', lambda e, o=scr[:, 2, :], i=e1[:, :, :]: e.reduce_sum(out=o, in_=i, axis=mybir.AxisListType.X), [rk('e1')], [rk('sc')])
                                TS('dve', scr[:, 3, :], scr[:, 2, :], 1.0 / 64, ALU.mult, [rk('sc')], [rk('sc')], s2=64e-5, op1=ALU.add)
                                ACT(scr[:, 3, :], scr[:, 3, :], AF.Sqrt, [rk('sc')], [rk('sc')])
                                RECIP(scr[:, 4, :], scr[:, 3, :], [rk('sc')], [rk('sc')])
                                TT('dve', e3[:, :, :], e2[:, :, :], bc2(4), ALU.mult, [rk('e2'), rk('sc')], [rk('e3')])
                                TT('dve', e3[:, :, :], e3[:, :, :], gnx[:, 2 * gp:2 * gp + 2, :], ALU.mult, [rk('e3'), 'gnx'], [rk('e3')])
                                TT('dve', e3[:, :, :], e3[:, :, :], gnx[:, 8 + 2 * gp:8 + 2 * gp + 2, :], ALU.add, [rk('e3'), 'gnx'], [rk('e3')])
                                TT('pool', e1[:, :, :], R_['Vst'][:, :, :], bc2(0), ALU.mult, [rk('Vst'), rk('sc')], [rk('e1')])
                                TT('dve', e3[:, :, :], e3[:, :, :], e1[:, :, :], ALU.add, [rk('e3'), rk('e1')], [rk('e3')])
                                TT('dve', R_['ysa'][:, :, :], e3[:, :, :], R_['azst'][:, :, :], ALU.mult, [rk('e3'), rk('azst')], [rk('ysa')])
                                pt, ptk = pst()
                                for hh in range(2):
                                    TR(pt[hh * 64:(hh + 1) * 64, 0:64], R_['ysa'][:, hh, :], idB[H64, 0:64], [rk('ysa'), 'cstB'], ptk)
                                CP('act', ysT[:, gp, c * 64:c * 64 + 64], pt[:, 0:64], ptk, [('ysT', gp)])

                    S.enabled = 'B' in cfg.get('ph', 'ABCG')
                    sBQ, kBQ = win_slab(l, C_BQ, 512)
                    sBK, kBK = win_slab(l, C_BK, 512)
                    qT = [tmpH[i] for i in range(4)]
                    kTm = [tmpH[4 + i] for i in range(4)]
                    oz = [tmpH[8 + i] for i in range(4)]
                    ktg = [tmpH[12 + i] for i in range(8)]
                    ysb = [tmpH[20 + i] for i in range(4)]
                    for fc in range(8):
                        slab, skey = (sBQ, kBQ) if fc < 4 else (sBK, kBK)
                        rb = rawB[fc % 2]
                        rk_ = ('rawB', fc % 2)
                        ps, pk = psq(4)
                        proj_fm(slab, skey, (fc % 4) * 128, 128, ps[:, :TB], pk)
                        ACT(rb[:, 3:3 + TB], ps[:, :TB], AF.Identity, pk + ['PV'], [rk_], bias=pv(PV_BQK + fc))
                        CP('pool', rb[:, 0:3], carB[:, fc, :], ['carB'], [rk_])
                        tq = tmpF[fc % 2]
                        tk_ = ('tF', fc % 2)
                        ACT(tq[:, :TB], rb[:, 3:3 + TB], AF.Identity, [rk_, 'PV'], [tk_], scale=pv(PV_CW + 24 + fc), bias=pv(PV_CB + fc))
                        for jj in range(3):
                            STT(tq[:, :TB], rb[:, jj:jj + TB], pv(PV_CW + 8 * jj + fc), tq[:, :TB], ALU.mult, ALU.add, [rk_, 'PV', tk_], [tk_])
                        CP('pool', carB[:, fc, :], rb[:, TB:TB + 3], [rk_], ['carB'])
                        dst = qT[fc] if fc < 4 else kTm[fc - 4]
                        ACT(dst[:, :TB], tq[:, :TB], AF.Silu, [tk_], [('tH', fc)])
                    iv, sp_, fn_, bn_, x_, g_, t1_, t3_ = [tmpF[i] for i in range(8)]
                    K = [('tF', i) for i in range(8)]
                    ps, pk = psq(4)
                    for kc in range(KC):
                        MM(ps[:, :TB], wrep[:, 0, kc, :], xb[:, kc, :TB], ['wrep', KXB], pk, start=(kc == 0), stop=(kc == KC - 1))
                    ACT(iv[:, :TB], ps[:, :TB], AF.Identity, pk + ['PV'], [K[0]], bias=pv(PV_BI))
                    ps, pk = psq(4)
                    for kc in range(KC):
                        MM(ps[:, :TB], wrep[:, 1, kc, :], xb[:, kc, :TB], ['wrep', KXB], pk, start=(kc == 0), stop=(kc == KC - 1))
                    ACT(sp_[:, :TB], ps[:, :TB], AF.Exp, pk + ['PV'], [K[1]], bias=pv(PV_NBF), scale=-1.0)
                    ACT(sp_[:, :TB], sp_[:, :TB], AF.Ln, [K[1]], [K[1]], bias=1.0)
                    SCAN(fn_[:, :TB], sp_[:, :TB], sp_[:, :TB], carM[:, 0:1], ALU.add, ALU.bypass, [K[1], 'carM'], [K[2]])
                    SCAN(bn_[:, :TB], m01[:, :TB], sp_[:, :TB], 0.0, ALU.mult, ALU.add, ['cstF', K[1]], [K[3]])
                    TT('dve', x_[:, :TB], iv[:, :TB], fn_[:, :TB], ALU.add, [K[0], K[2]], [K[4]])
                    SCAN(g_[:, :TB], x_[:, :TB], x_[:, :TB], carM[:, 1:2], ALU.max, ALU.bypass, [K[4], 'carM'], [K[5]])
                    TT('dve', mbuf[:, 64:64 + TB], g_[:, :TB], fn_[:, :TB], ALU.subtract, [K[5], K[2]], ['mbuf'])
                    CP('pool', carM[:, 0:1], fn_[:, TB - 1:TB], [K[2]], ['carM'])
                    CP('pool', carM[:, 1:2], g_[:, TB - 1:TB], [K[5]], ['carM'])
                    mcur = mbuf[:, 64:64 + TB]
                    v3 = lambda ap: ap.rearrange("p (c t) -> p c t", t=64)
                    bc = lambda ap: v3(ap)[:, :, 63:64].broadcast_to([128, NCH, 64])
                    Ra, Rsc, Rcl, Rg, RgL = x_, g_, fn_, iv, t3_
                    STT(t1_[:, :TB], bn_[:, :TB], -1.0, mcur, ALU.mult, ALU.subtract, [K[3], 'mbuf'], [K[6]])
                    ACT(Ra[:, :TB], t1_[:, :TB], AF.Exp, [K[6]], [K[4]])
                    TT('dve', v3(t1_[:, :TB]), v3(t1_[:, :TB]), bc(mbuf[:, 0:TB]), ALU.add, [K[6], 'mbuf'], [K[6]])
                    ACT(Rsc[:, :TB], t1_[:, :TB], AF.Exp, [K[6]], [K[5]])
                    ACT(Rcl[:, :TB], mcur, AF.Exp, ['mbuf'], [K[2]], scale=-1.0)
                    TT('dve', t3_[:, :TB], iv[:, :TB], bn_[:, :TB], ALU.add, [K[0], K[3]], [K[7]])
                    ACT(Rg[:, :TB], t3_[:, :TB], AF.Exp, [K[7]], [K[0]], bias=LN_C)
                    TT('dve', v3(t3_[:, :TB]), v3(t3_[:, :TB]), bc(bn_[:, :TB]), ALU.subtract, [K[7], K[3]], [K[7]])
                    TT('dve', v3(t3_[:, :TB]), v3(t3_[:, :TB]), bc(mcur), ALU.subtract, [K[7], 'mbuf'], [K[7]])
                    ACT(RgL[:, :TB], t3_[:, :TB], AF.Exp, [K[7]], [K[7]], bias=LN_C)
                    RQ = [(Ra, K[4]), (Rsc, K[5]), (Rcl, K[2]), (Rg, K[0]), (RgL, K[7])]
                    for tt in range(NTL):
                        ps, pk = psq(4)
                        for qi in range(4):
                            MM(ps[0:TP, qi * 128:(qi + 1) * 128], RQ[qi][0][:, tt * TP:(tt + 1) * TP], idF, [RQ[qi][1], 'cstF'], pk)
                        CP('dve', tokS[0:TP, tt, 0:4, :], ps[0:TP, :].rearrange("p (q h r) -> p q h r", q=4, h=4)[:, :, :, 0], pk, [('tokS', tt)])
                        ps, pk = psq(1)
                        MM(ps[0:TP, :], RQ[4][0][:, tt * TP:(tt + 1) * TP], idF, [RQ[4][1], 'cstF'], pk)
                        CP('dve', tokS[0:TP, tt, 4, :], ps[0:TP, :].rearrange("p (h r) -> p h r", h=4)[:, :, 0], pk, [('tokS', tt)])
                        for jj in range(CPT):
                            TS('dve', tokS2[0:TP, tt, jj, :], tokS[0:TP, tt, 4, :], rmask[0:TP, jj:jj + 1], ALU.mult, [('tokS', tt), 'rmask'], [('tokS2', tt)])
                    ps, pk = psq(1)
                    for h in range(4):
                        MM(ps[:, h * NCH:(h + 1) * NCH], cstF[:, 768 + 128 * h:768 + 128 * (h + 1)],
                           v3(Rsc[:, :TB])[:, :, 63], ['cstF', K[5]], pk)
                    CP('dve', csB[:, :, 0:NCH], ps[:, 0:4 * NCH].rearrange("p (h c) -> p h c", h=4), pk, ['csB'])
                    CP('pool', mbuf[:, 63:64], mbuf[:, 63 + TB:64 + TB], ['mbuf'], ['mbuf'])
                    sV2, kV2 = win_slab(l, C_BV, 512)
                    sO, kO = win_slab(l, C_BO, 512)
                    for tt in range(NTL):
                        ps, pk = psq(4)
                        proj_tm(sV2, kV2, tt, ps[0:TP, :], pk, 0)
                        CP('act', vaug[0:TP, tt, :, 0:128], ps[0:TP, :].rearrange("p (h d) -> p h d", h=4), pk, [('vaug', tt)])
                    for tt in range(NTL):
                        ps, pk = psq(4)
                        proj_tm(sO, kO, tt, ps[0:TP, :], pk, 512)
                        ACT(oz[tt][0:TP, :], ps[0:TP, :], AF.Sigmoid, pk, [('tH', 8 + tt)])
                    sZ2, kZ2 = win_slab(l, C_BZ, 512)
                    for tt in range(NTL):
                        ps, pk = psq(4)
                        proj_tm(sZ2, kZ2, tt, ps[0:TP, :], pk, 1024)
                        ACT(tmpF[8][0:TP, :], ps[0:TP, :], AF.Silu, pk, [('tF', 8)])
                        TT('dve', oz[tt][0:TP, :], oz[tt][0:TP, :], tmpF[8][0:TP, :], ALU.mult, [('tH', 8 + tt), ('tF', 8)], [('tH', 8 + tt)])
                    for tt in range(NTL):
                        for h in range(4):
                            pt, ptk = pst()
                            TR(pt[0:TP, :], kTm[h][:, tt * TP:(tt + 1) * TP], idB, [('tH', 4 + h), 'cstB'], ptk)
                            for jj in range(CPT):
                                TS('dve', ktg[2 * tt + jj][0:TP, h * 128:(h + 1) * 128], pt[0:TP, :], tokS2[0:TP, tt, jj, h:h + 1], ALU.mult,
                                   ptk + [('tokS2', tt)], [('tH', 12 + 2 * tt + jj)])
                    for tt in range(NTL):
                        tsl = slice(tt * TP, (tt + 1) * TP)
                        for h in range(4):
                            hsl = slice(h * 128, (h + 1) * 128)
                            wt_ = mlh[h % 2]
                            kwt = ('mlh', h % 2)
                            ps, pk = psq(1)
                            MM(ps[0:TP, 0:TP], kTm[h][:, tsl], qT[h][:, tsl], [('tH', 4 + h), ('tH', h)], pk)
                            STT(wt_[0:TP, 0:TP], ps[0:TP, 0:TP], tokS[0:TP, tt, 3, h:h + 1], mIU[0:TP, 0:TP], ALU.mult, ALU.mult,
                                pk + [('tokS', tt), 'cstB'], [kwt])
                            psn, pkn = psq(2)
                            MM(psn[0:TP, 0:129], wt_[0:TP, 0:TP], vaug[0:TP, tt, h, :], [kwt, ('vaug', tt), 'vaug'], pkn)
                            psi, pki = psq(2)
                            for jj in range(CPT):
                                c = tt * CPT + jj
                                js = slice(jj * 64, jj * 64 + 64)
                                MM(psi[js, 0:129], qT[h][:, tt * TP + jj * 64:tt * TP + jj * 64 + 64], CTb[h][:], [('tH', h), ('CTb', h)], pki)
                                pss, pks = psq(2)
                                MM(pss[:, 0:129], ktg[2 * tt + jj][0:TP, hsl], vaug[0:TP, tt, h, :], [('tH', 12 + 2 * tt + jj), ('vaug', tt), 'vaug'], pks)
                                STT(CTf[h][:], CTf[h][:], csB[:, h, c:c + 1], pss[:, 0:129], ALU.mult, ALU.add, [('CTf', h), 'csB'] + pks, [('CTf', h)])
                                CP('pool', CTb[h][:], CTf[h][:], [('CTf', h)], [('CTb', h)])
                            m1, m2 = mlt[(2 * h) % 4], mlt[(2 * h + 1) % 4]
                            km1, km2 = ('mlt', (2 * h) % 4), ('mlt', (2 * h + 1) % 4)
                            sc_ = mls[:, h % 8, :] if False else mls[:, (tt * 4 + h) % 8, :]
                            ksc = ('mls', (tt * 4 + h) % 8)
                            TS('dve', m1[0:TP, 0:129], psi[0:TP, 0:129], tokS[0:TP, tt, 1, h:h + 1], ALU.mult, pki + [('tokS', tt)], [km1])
                            STT(m2[0:TP, 0:129], psn[0:TP, 0:129], tokS[0:TP, tt, 0, h:h + 1], m1[0:TP, 0:129], ALU.mult, ALU.add,
                                pkn + [('tokS', tt), km1], [km2])
                            ACT(sc_[0:TP, 0:1], m2[0:TP, 128:129], AF.Abs, [km2], [ksc])
                            TS('dve', sc_[0:TP, 0:1], sc_[0:TP, 0:1], tokS[0:TP, tt, 2, h:h + 1], ALU.max, [ksc, ('tokS', tt)], [ksc])
                            RECIP(sc_[0:TP, 1:2], sc_[0:TP, 0:1], [ksc], [ksc])
                            TS('dve', m1[0:TP, 0:128], m2[0:TP, 0:128], sc_[0:TP, 1:2], ALU.mult, [km2, ksc], [km1, ksc],
                               s2=0.0, op1=ALU.add, accum=sc_[0:TP, 2:3])
                            TS('dve', sc_[0:TP, 2:3], sc_[0:TP, 2:3], -1.0 / 128, ALU.mult, [ksc], [ksc])
                            TS('dve', m1[0:TP, 0:128], m1[0:TP, 0:128], sc_[0:TP, 2:3], ALU.add, [km1, ksc], [km1])
                            ACT(m2[0:TP, 0:128], m1[0:TP, 0:128], AF.Square, [km1], [km2, ksc], accum=sc_[0:TP, 3:4])
                            TS('dve', sc_[0:TP, 4:5], sc_[0:TP, 3:4], 1.0 / 128, ALU.mult, [ksc], [ksc], s2=1e-6, op1=ALU.add)
                            ACT(sc_[0:TP, 4:5], sc_[0:TP, 4:5], AF.Sqrt, [ksc], [ksc])
                            RECIP(sc_[0:TP, 5:6], sc_[0:TP, 4:5], [ksc], [ksc])
                            STT(m2[0:TP, 0:128], m1[0:TP, 0:128], sc_[0:TP, 5:6], hnG[0:TP, hsl], ALU.mult, ALU.mult, [km1, ksc, 'hnG'], [km2])
                            TT('dve', ysb[tt][0:TP, hsl], m2[0:TP, 0:128], oz[tt][0:TP, hsl], ALU.mult, [km2, ('tH', 8 + tt)], [('tH', 20 + tt)])
                        for h in range(4):
                            pt, ptk = pst()
                            TR(pt[:, 0:TP], ysb[tt][0:TP, h * 128:(h + 1) * 128], idB[0:TP, 0:TP], [('tH', 20 + tt), 'cstB'], ptk)
                            CP('act', ysT[:, 4 + h, tsl], pt[:, 0:TP], ptk, [('ysT', 4 + h)])

                    S.enabled = ('C' in cfg.get('ph', 'ABCG')) or ('c' in cfg.get('ph', 'ABCG'))
                    sQ, kQ = win_slab(l, C_CQ, 512)
                    sKc, kKc = win_slab(l, C_CK, 512)
                    qs16 = [tmpH[i] for i in range(8)]
                    nq16 = [tmpH[8 + i] for i in range(8)]
                    kf16 = [tmpH[16 + i] for i in range(4)]
                    for h in range(8):
                        oth = slice(64 - (h % 2) * 64, 128 - (h % 2) * 64)
                        MSET('pool', qs16[h][oth, :TB], 0.0, [('tH', h)])
                        MSET('pool', nq16[h][oth, :TB], 0.0, [('tH', 8 + h)])
                    for p in range(4):
                        ps, pk = psq(4)
                        proj_fm(sQ, kQ, p * 128, 128, ps[:, :TB], pk)
                        for hh in range(2):
                            h = 2 * p + hh
                            hs = slice(hh * 64, hh * 64 + 64)
                            ACT(qs16[h][hs, :TB], ps[hs, :TB], AF.Identity, pk + ['PV'], [('tH', h)], bias=PV[hs, l, PV_BCQ + p:PV_BCQ + p + 1], scale=0.125)
                            ACT(nq16[h][hs, :TB], ps[hs, :TB], AF.Identity, pk + ['PV'], [('tH', 8 + h)], bias=PV[hs, l, PV_NBCQ + p:PV_NBCQ + p + 1], scale=-0.125)
                    for p in range(4):
                        ps, pk = psq(4)
                        proj_fm(sKc, kKc, p * 128, 128, ps[:, :TB], pk)
                        ACT(kf16[p][:, :TB], ps[:, :TB], AF.Identity, pk + ['PV'], [('tH', 16 + p)], bias=pv(PV_BCK + p))
                        DMA('pool', X['kTd'][p, :, P + t0:P + t0 + TB], kf16[p][:, :TB], [('tH', 16 + p)], [('kTd', si)])
                    for tt in range(NTL):
                        ps, pk = psq(4)
                        proj_tm(sKc, kKc, tt, ps[0:TP, :], pk, 1536)
                        CP('act', tmpF[tt % 2][0:TP, :], ps[0:TP, :], pk, [('tF', tt % 2)])
                        DMA('pool', O['sbk'][l, t0 + tt * TP:t0 + (tt + 1) * TP, :], tmpF[tt % 2][0:TP, :], [('tF', tt % 2)], [], is_out=True)
                    sVc, kVc = win_slab(l, C_CV, 512)
                    for tt in range(NTL):
                        ps, pk = psq(4)
                        proj_tm(sVc, kVc, tt, ps[0:TP, :], pk, 2048)
                        CP('act', tmpF[2 + tt % 2][0:TP, :], ps[0:TP, :], pk, [('tF', 2 + tt % 2)])
                        CP('act', tmpH[20 + tt % 2][0:TP, :], ps[0:TP, :], pk, [('tH', 20 + tt % 2)])
                        DMA('pool', O['sbv'][l, t0 + tt * TP:t0 + (tt + 1) * TP, :], tmpF[2 + tt % 2][0:TP, :], [('tF', 2 + tt % 2)], [], is_out=True)
                        DMA('pool', X['vd'][P + t0 + tt * TP:P + t0 + (tt + 1) * TP, :], tmpH[20 + tt % 2][0:TP, :], [('tH', 20 + tt % 2)], [('vd', si)])
                    sZc, kZc = win_slab(l, C_CZ, 512)
                    S.enabled = 'C' in cfg.get('ph', 'ABCG')
                    q0 = P + t0
                    kend = q0 + TB
                    nkb = (kend + 127) // 128
                    for half in range(2):
                        accs = [psbank(4 + i) for i in range(2)]
                        SP = [tmpH[22 + i] for i in range(4)]
                        KSP = [('tH', 22 + i) for i in range(4)]
                        EF = [tmpF[4 + i] for i in range(4)]
                        KEF = [('tF', 4 + i) for i in range(4)]
                        S16 = [tmpH[16 + i] for i in range(4)]
                        KS16 = [('tH', 16 + i) for i in range(4)]
                        A16s = [mlh2[0], mlh2[1], tmpH[26], tmpH[27]]
                        KA16 = [('mlh2', 0), ('mlh2', 1), ('tH', 26), ('tH', 27)]
                        for ki, kb in enumerate(reversed(range(nkb))):
                            k0 = kb * 128
                            ks = min(128, kend - k0)
                            kt_, vt_ = kTblk[ki % 2], vblk[ki % 2]
                            kkt, kvt = ('kTblk', ki % 2), ('vblk', ki % 2)
                            DMA('sp', kt_[:, :, 0:ks], X['kTd'][:, :, k0:k0 + ks].rearrange("c p s -> p c s"), [('kTd', si)], [kkt])
                            DMA('sp', vt_[0:ks, :], X['vd'][k0:k0 + ks, :], [('vd', si)], [kvt])
                            masked = (k0 + ks > q0)
                            mk = attB[0:ks, (k0 - q0) // 128, 0:TB] if masked else None
                            HS = [half * 4 + hi for hi in range(4)]
                            Z = []
                            for hi, h in enumerate(HS):
                                psz, pkz = psq(4)
                                MM(psz[0:ks, :TB], kt_[:, h // 2, 0:ks], qs16[h][:, :TB], [kkt, ('tH', h)], pkz)
                                Z.append((psz, pkz))
                            for hi in range(4):
                                ACT(EF[hi][0:ks, :TB], Z[hi][0][0:ks, :TB], AF.Exp, Z[hi][1], [KEF[hi]])
                            for hi in range(4):
                                ACT(S16[hi][0:ks, :TB], EF[hi][0:ks, :TB], AF.Ln, [KEF[hi]], [KS16[hi]], bias=1.0)
                            if masked:
                                for hi in range(4):
                                    TT('pool', S16[hi][0:ks, :TB], S16[hi][0:ks, :TB], mk, ALU.mult, [KS16[hi], 'attB'], [KS16[hi]])
                            ARG = []
                            for hi, h in enumerate(HS):
                                psa, pka = psq(4)
                                MM(psa[0:ks, :TB], triB[0:ks, 0:ks], S16[hi][0:ks, :TB], ['cstB', KS16[hi]], pka, start=True, stop=False)
                                if ki > 0:
                                    MM(psa[0:ks, :TB], onesB[:, 0:ks], SP[hi][:, :TB], ['cstB', KSP[hi]], pka, start=False, stop=False)
                                MM(psa[0:ks, :TB], kt_[:, h // 2, 0:ks], nq16[h][:, :TB], [kkt, ('tH', 8 + h)], pka, start=False, stop=True)
                                ARG.append((psa, pka))
                            for hi in range(4):
                                ACT(A16s[hi][0:ks, :TB], ARG[hi][0][0:ks, :TB], AF.Exp, ARG[hi][1], [KA16[hi]], scale=-1.0)
                            if masked:
                                for hi in range(4):
                                    TT('dve', A16s[hi][0:ks, :TB], A16s[hi][0:ks, :TB], mk, ALU.mult, [KA16[hi], 'attB'], [KA16[hi]])
                            for hi, h in enumerate(HS):
                                hs = slice((h % 2) * 64, (h % 2) * 64 + 64)
                                acc, kacc = accs[hi // 2]
                                MM(acc[hs, :TB], vt_[0:ks, h * 64:(h + 1) * 64], A16s[hi][0:ks, :TB], [kvt, KA16[hi]], kacc,
                                   start=(ki == 0), stop=(ki == nkb - 1))
                            for hi in range(4):
                                if ki == 0:
                                    if ks < 128:
                                        MSET('pool', SP[hi][:, :TB], 0.0, [KSP[hi]])
                                    CP('pool', SP[hi][0:ks, :TB], S16[hi][0:ks, :TB], [KS16[hi]], [KSP[hi]])
                                elif ki < nkb - 1:
                                    TT('pool', SP[hi][0:ks, :TB], SP[hi][0:ks, :TB], S16[hi][0:ks, :TB], ALU.add, [KS16[hi], KSP[hi]], [KSP[hi]])
                        for pi in range(2):
                            p = half * 2 + pi
                            ps, pk = psq(4)
                            proj_fm(sZc, kZc, p * 128, 128, ps[:, :TB], pk)
                            ACT(tmpH[21][:, :TB], ps[:, :TB], AF.Silu, pk + ['PV'], [('tH', 21)], bias=pv(PV_BCZ + p))
                            acc, kacc = accs[pi]
                            TT('dve', ysT[:, 8 + p, :TB], acc[:, :TB], tmpH[21][:, :TB], ALU.mult, kacc + [('tH', 21)], [('ysT', 8 + p)])

                    S.enabled = 'G' in cfg.get('ph', 'ABCG')
                    for n in range(3):
                        sB, kB = load_slab(wb_br[l, n].rearrange("(k p) c -> p k c", p=128), 4, 1024, ('wb_br', l))
                        for half in range(2):
                            sG, kG = win_slab(l, C_G0 + n * 1024 + half * 512, 512)
                            for f4 in range(4):
                                fc = half * 4 + f4
                                psg, pkg = psq(4)
                                proj_fm(sG, kG, f4 * 128, 128, psg[:, :TB], pkg)
                                psr, pkr = psq(4)
                                for kc in range(4):
                                    MM(psr[:, :TB], sB[:, kc, fc * 128:(fc + 1) * 128], ysT[:, 4 * n + kc, :TB], [kB, ('ysT', 4 * n + kc)], pkr,
                                       start=(kc == 0), stop=(kc == 3))
                                ACT(tmpF[8][:, :TB], psg[:, :TB], AF.Sigmoid, pkg + ['PV'], [('tF', 8)], bias=pv(PV_BG + n * 8 + fc))
                                if n == 0:
                                    TT('dve', tmpF[fc][:, :TB], tmpF[8][:, :TB], psr[:, :TB], ALU.mult, [('tF', 8)] + pkr, [('tF', fc)])
                                else:
                                    TT('dve', tmpF[8][:, :TB], tmpF[8][:, :TB], psr[:, :TB], ALU.mult, [('tF', 8)] + pkr, [('tF', 8)])
                                    if n == 1:
                                        TT('dve', tmpF[fc][:, :TB], tmpF[fc][:, :TB], tmpF[8][:, :TB], ALU.add, [('tF', fc), ('tF', 8)], [('tF', fc)])
                                    else:
                                        TT('dve', tmpH[fc][:, :TB], tmpF[fc][:, :TB], tmpF[8][:, :TB], ALU.add, [('tF', fc), ('tF', 8)], [('tH', fc)])
                    s5, k5 = psbank(4)
                    s6, k6 = psbank(5)
                    alpha = (2.0 * cfg.get('DEPTH', 4)) ** 0.25
                    for half in range(2):
                        sWo, kWo = load_slab(wb_out[l].rearrange("(k p) c -> p k c", p=128)[:, :, half * 512:(half + 1) * 512], KC, 512, ('wb_out', l))
                        for f4 in range(4):
                            fc = half * 4 + f4
                            psm, pkm = psq(4)
                            for kc in range(KC):
                                MM(psm[:, :TB], sWo[:, kc, f4 * 128:(f4 + 1) * 128], tmpH[kc][:, :TB], [kWo, ('tH', kc)], pkm,
                                   start=(kc == 0), stop=(kc == KC - 1))
                            DMA('sp', tmpF[8][:, :TB], X['xres'][fc, :, t0:t0 + TB], [KX], [('tF', 8)])
                            STT(tmpF[fc][:, :TB], tmpF[8][:, :TB], alpha, psm[:, :TB], ALU.mult, ALU.add, [('tF', 8)] + pkm, [('tF', fc)])
                    layernorm(TB, lambda fc: pv(PV_LNG + fc), lambda fc: pv(PV_LNB + fc), 1e-5)
                    for fc in range(KC):
                        CP('pool', ysT[:, fc, :TB], tmpF[fc][:, :TB], [('tF', fc)], [('ysT', fc)])
                    sPl, kPl = load_slab(wb_ple[l].rearrange("(k p) c -> p k c", p=128), 2, 1024, ('wb_ple', l))
                    last = (l == L - 1)
                    for half in range(2):
                        sPg, kPg = load_slab(wb_pg[l].rearrange("(k p) c -> p k c", p=128)[:, :, half * 512:(half + 1) * 512], KC, 512, ('wb_pg', l))
                        for f4 in range(4):
                            fc = half * 4 + f4
                            psg, pkg = psq(4)
                            for kc in range(KC):
                                MM(psg[:, :TB], sPg[:, kc, f4 * 128:(f4 + 1) * 128], ysT[:, kc, :TB], [kPg, ('ysT', kc)], pkg,
                                   start=(kc == 0), stop=(kc == KC - 1))
                            psp, pkp = psq(4)
                            for kc in range(2):
                                MM(psp[:, :TB], sPl[:, kc, fc * 128:(fc + 1) * 128], p16[:, kc, :TB], [kPl, 'p16'], pkp, start=(kc == 0), stop=(kc == 1))
                            ACT(tmpF[8][:, :TB], psg[:, :TB], AF.Sigmoid, pkg, [('tF', 8)])
                            TT('dve', tmpF[8][:, :TB], tmpF[8][:, :TB], psp[:, :TB], ALU.mult, [('tF', 8)] + pkp, [('tF', 8)])
                            TT('dve', tmpF[fc][:, :TB], tmpF[fc][:, :TB], tmpF[8][:, :TB], ALU.add, [('tF', fc), ('tF', 8)], [('tF', fc)])
                            if last:
                                DMA('pool', O['yT'][fc * 128:(fc + 1) * 128, t0:t0 + TB], tmpF[fc][:, :TB], [('tF', fc)], [], is_out=True)
                            else:
                                DMA('pool', X['xres'][fc, :, t0:t0 + TB], tmpF[fc][:, :TB], [('tF', fc)], [KX])
                                CP('pool', tmpH[16 + fc][:, :TB], tmpF[fc][:, :TB], [('tF', fc)], [('tH', 16 + fc)])
                                DMA('pool', X['xTd'][fc, :, t0:t0 + TB], tmpH[16 + fc][:, :TB], [('tH', 16 + fc)], [KXD])

                S.enabled = True
                S.budget = None
                DMA('sp', O['shift'][l], carX[:], ['carX'], [], is_out=True)
                DMA('sp', O['conv'][l], carB[:], ['carB'], [], is_out=True)
                DMA('sp', O['m'][l], mbuf[:, 63:64], ['mbuf'], [], is_out=True)
                for p in range(4):
                    DMA('sp', O['wkv'][l, p], Hf[p][:, :, :], [('Hf', p)], [], is_out=True)
                    DMA('sp', O['ct'][l, p], CTf[p][:], [('CTf', p)], [], is_out=True)

        S.emit(nc)
    return nc


N_CORES = 8
_CACHE = {}


def _seq_inputs(i, x, p, st):
    d = {f"xT{i}": np.ascontiguousarray(x.T), f"pT{i}": np.ascontiguousarray(p.transpose(0, 2, 1))}
    if st is not None:
        shift, wkv, conv, c, n, m, ck, cv = st
        L = shift.shape[0]
        d[f"shift{i}"] = np.ascontiguousarray(shift.reshape(L, 26, 64).transpose(0, 2, 1))
        d[f"wkv{i}"] = np.ascontiguousarray(wkv.reshape(L, 4, 2, 64, 64).transpose(0, 1, 4, 2, 3))
        d[f"conv{i}"] = np.ascontiguousarray(conv.reshape(L, 3, 8, 128).transpose(0, 3, 2, 1))
        d[f"ct{i}"] = np.ascontiguousarray(np.concatenate([c.transpose(0, 1, 3, 2), n[..., None]], axis=-1))
        d[f"m{i}"] = np.ascontiguousarray(np.repeat(m, 32, axis=1)[..., None])
        Pn = ck.shape[1]
        d[f"ck{i}"] = np.ascontiguousarray(ck.reshape(L, Pn, 4, 128).transpose(0, 2, 3, 1))
        d[f"cv{i}"] = np.ascontiguousarray(cv.reshape(L, Pn, 512))
    return d


def _seq_outputs(r, i, L, T):
    y = r[f"yT{i}"].T
    shift = r[f"shift_o{i}"].transpose(0, 2, 1).reshape(L, 1664)
    wkv = r[f"wkv_o{i}"].transpose(0, 1, 3, 4, 2).reshape(L, 8, 64, 64)
    conv = r[f"conv_o{i}"].transpose(0, 3, 2, 1).reshape(L, 3, 1024)
    ct = r[f"ct_o{i}"]
    c = ct[..., 0:128].transpose(0, 1, 3, 2)
    n = ct[..., 128]
    m = r[f"m_o{i}"][:, ::32, 0]
    sbk = r[f"sbk{i}"].reshape(L, T, 8, 64)
    sbv = r[f"sbv{i}"].reshape(L, T, 8, 64)
    return [y, shift, wkv, conv, c, n, m, sbk, sbv]


WNAMES = ['ln_in_g', 'ln_in_b', 'w_in', 'b_in', 'mu_a', 'w0_a', 'w_decay_up', 'a0_a', 'w_iclr_up', 'k_k', 'k_a', 'r_k',
          'gn_a_g', 'gn_a_b', 'conv_b_w', 'conv_b_b', 'hn_b_g', 'w_branch', 'w_out', 'ln_g', 'ln_b', 'w_ple', 'w_ple_gate']


def weight_inputs(w, L):
    f = lambda a: np.ascontiguousarray(np.asarray(a, dtype=np.float32))
    wd = {k: f(w[k]) for k in WNAMES}
    wd['r_k'] = wd['r_k'].reshape(L, 512)
    wd['cst'] = make_consts()
    wd['catt'] = make_att()
    col = lambda v: v.reshape(-1, 128).T
    pv = np.zeros((128, L, NPV), np.float32)
    for l in range(L):
        b = wd['b_in'][l]
        pv[:, l, PV_BA:PV_BA + 17] = col(b[0:2176])
        pv[:, l, PV_BQK:PV_BQK + 8] = col(b[C_BQ:C_BQ + 1024])
        pv[:, l, PV_BCQ:PV_BCQ + 4] = col(b[C_CQ:C_CQ + 512])
        pv[:, l, PV_BCK:PV_BCK + 4] = col(b[C_CK:C_CK + 512])
        pv[:, l, PV_BCZ:PV_BCZ + 4] = col(b[C_CZ:C_CZ + 512])
        pv[:, l, PV_BG:PV_BG + 24] = col(b[C_G0:C_G0 + 3072])
        pv[:, l, PV_BI] = np.repeat(b[C_BI:C_BI + 4], 32)
        pv[:, l, PV_NBF] = np.repeat(b[C_BF:C_BF + 4], 32)
        pv[:, l, PV_MU:PV_MU + 13] = col(wd['mu_a'][l])
        for nm, c in (('w0_a', PV_W0), ('a0_a', PV_A0), ('k_k', PV_KK), ('k_a', PV_KA), ('r_k', PV_RK)):
            pv[:, l, c:c + 4] = col(wd[nm][l])
        for j in range(4):
            pv[:, l, PV_CW + 8 * j:PV_CW + 8 * j + 8] = col(wd['conv_b_w'][l, j])
        pv[:, l, PV_CB:PV_CB + 8] = col(wd['conv_b_b'][l])
        pv[:, l, PV_LNG:PV_LNG + 8] = col(wd['ln_g'][l])
        pv[:, l, PV_LNB:PV_LNB + 8] = col(wd['ln_b'][l])
    wd['pvh'] = pv
    wd['pvg'] = np.ascontiguousarray(np.concatenate([col(wd['ln_in_g']), col(wd['ln_in_b'])], axis=1))
    gn = np.zeros((L, 64, 16, 64), np.float32)
    gn[:, :, 0:8, :] = wd['gn_a_g'].reshape(L, 1, 8, 64)
    gn[:, :, 8:16, :] = wd['gn_a_b'].reshape(L, 1, 8, 64)
    wd['gnh'] = gn
    c64 = lambda v: v.reshape(-1, 64).T
    pvx = np.zeros((64, L, NX), np.float32)
    for l in range(L):
        pvx[:, l, X_B:X_B + 26] = c64(wd['b_in'][l, 0:1664])
        pvx[:, l, X_MU:X_MU + 26] = c64(wd['mu_a'][l])
        pvx[:, l, X_BZ:X_BZ + 8] = c64(wd['b_in'][l, C_AZ:C_AZ + 512])
        for nm, c in (('w0_a', X_W0), ('a0_a', X_A0), ('k_k', X_KK), ('k_a', X_KA), ('r_k', X_RK)):
            pvx[:, l, c:c + 8] = c64(wd[nm][l])
    wd['pvx'] = pvx
    i64 = np.arange(64)
    r_, c_ = i64[:, None], i64[None, :]
    wd['cst3'] = np.ascontiguousarray(np.concatenate([np.tile(m, (1, 2)) for m in (r_ < c_, r_ > c_, r_ <= c_, r_ == c_)], axis=1).astype(np.float32))
    wd['hnh'] = np.ascontiguousarray(np.broadcast_to(wd['hn_b_g'][:, None, :], (L, 128, 512)))
    return wd


def kernel(x_prompt, x_sample, state_shift_a, state_wkv, state_conv_b, state_mlstm_c, state_mlstm_n,
           state_mlstm_m, cache_sb_k, cache_sb_v, p_prompt, p_sample, **weights):
    f = lambda a: np.ascontiguousarray(np.asarray(a, dtype=np.float32))
    x_prompt, x_sample, p_prompt, p_sample = f(x_prompt), f(x_sample), f(p_prompt), f(p_sample)
    L = p_prompt.shape[0]
    Bp, Tp = x_prompt.shape[:2]
    Bs, Ts = x_sample.shape[:2]
    Pn = cache_sb_k.shape[2]
    npc = Bp // N_CORES
    cfg = {'L': L, 'seqs': [{'T': Tp, 'P': 0}] * npc + [{'T': Ts, 'P': Pn}]}
    key = (L, Tp, Ts, Pn, npc)
    if key not in _CACHE:
        _CACHE[key] = build(cfg)
    nc = _CACHE[key]
    wd = weight_inputs(weights, L)
    sts = [f(a) for a in (state_shift_a, state_wkv, state_conv_b, state_mlstm_c, state_mlstm_n, state_mlstm_m, cache_sb_k, cache_sb_v)]
    in_maps = []
    for c in range(N_CORES):
        d = dict(wd)
        for i in range(npc):
            b = c * npc + i
            d.update(_seq_inputs(i, x_prompt[b], p_prompt[:, b], None))
        d.update(_seq_inputs(npc, x_sample[c], p_sample[:, c], [a[:, c] for a in sts]))
        in_maps.append(d)
    res = run_bass_kernel_spmd(nc, in_maps, core_ids=list(range(N_CORES)))
    pr = [[] for _ in range(9)]
    sr = [[] for _ in range(9)]
    for c in range(N_CORES):
        r = res.results[c]
        for i in range(npc):
            for k, v in enumerate(_seq_outputs(r, i, L, Tp)):
                pr[k].append(v)
        for k, v in enumerate(_seq_outputs(r, npc, L, Ts)):
            sr[k].append(v)
    outs = []
    for k in range(9):
        a = np.stack(pr[k], 0)
        b = np.stack(sr[k], 0)
        if k >= 1:
            a = np.moveaxis(a, 1, 0)
            b = np.moveaxis(b, 1, 0)
        outs.append((np.ascontiguousarray(a, dtype=np.float32), np.ascontiguousarray(b, dtype=np.float32)))
    y, sh, wkv, conv, cc, nn, mm, sbk, sbv = outs
    return (y[0], y[1], sh[0], sh[1], wkv[0], wkv[1], conv[0], conv[1], cc[0], cc[1], nn[0], nn[1], mm[0], mm[1],
            sbk[0], sbk[1], sbv[0], sbv[1])
```
